# Optimizing a Trainium2 kernel written in Bass

```python
import math, functools
import jax, jax.numpy as jnp
from jax import lax
import numpy as np

D_MODEL = 2048
BATCH = 4
SEQ = 8192
DEPTH = 1
DEC_BATCH = 8
DEC_SEQ = 16
PAST_LEN = 1024

CHUNK = 64
Q_BLOCK = 128
HEAD_DIM = 128
V_DIM = 2 * HEAD_DIM
N_HEADS = D_MODEL // (2 * HEAD_DIM)
QK_WIDTH = N_HEADS * 2 * HEAD_DIM
ATTN_WIDTH = N_HEADS * V_DIM
GM_CHUNK = 128
GM_GROUPS = 8
GM_WIDTH = D_MODEL
GM_GROUP_DIM = GM_WIDTH // GM_GROUPS
D_FF = -(-8 * D_MODEL // (3 * 256)) * 256
IN_WIDTH = 2 * QK_WIDTH + ATTN_WIDTH + 2 * GM_WIDTH + 2 * D_MODEL
NORM_EPS = 1e-6
SUBLN_EPS = 1e-5

kernel_name = 'streaming_diffattn_gmlp_hybrid'


def _rmsnorm(x, w, eps=NORM_EPS):
    xf = x.astype(jnp.float32)
    y = xf * lax.rsqrt(jnp.mean(xf * xf, axis=-1, keepdims=True) + eps)
    return (y * w.astype(jnp.float32)).astype(x.dtype)


def _layernorm(x, w, b, eps=NORM_EPS):
    xf = x.astype(jnp.float32)
    xc = xf - jnp.mean(xf, axis=-1, keepdims=True)
    var = jnp.mean(xc * xc, axis=-1, keepdims=True)
    y = xc * lax.rsqrt(var + eps) * w.astype(jnp.float32) + b.astype(jnp.float32)
    return y.astype(x.dtype)


def _alibi_slopes():
    return 2.0 ** (-8.0 * jnp.arange(1, N_HEADS + 1, dtype=jnp.float32) / N_HEADS)


def _diff_attend(q, k, v, t_pos, s_pos, lam):
    s = jnp.einsum('bqhmd,bkhmd->bhmqk', q.astype(jnp.float32), k.astype(jnp.float32))
    dist = jnp.abs(t_pos[:, None] - s_pos[None, :]).astype(jnp.float32)
    bias = -_alibi_slopes()[:, None, None] * dist
    allowed = (s_pos[None, :] // CHUNK) <= (t_pos[:, None] // CHUNK)
    s = jnp.where(allowed, s + bias[None, :, None], -jnp.inf)
    p = jax.nn.softmax(s, axis=-1)
    a = p[:, :, 0] - lam * p[:, :, 1]
    return jnp.einsum('bhqk,bkhe->bqhe', a, v.astype(jnp.float32))


def _attend_prompt(q, k, v, lam):
    b, s = q.shape[0], q.shape[1]
    nb = s // Q_BLOCK
    q_blocks = q.reshape(b, nb, Q_BLOCK, N_HEADS, 2, HEAD_DIM).transpose(1, 0, 2, 3, 4, 5)
    s_pos = jnp.arange(s)

    def one_block(args):
        q_blk, i = args
        t_pos = i * Q_BLOCK + jnp.arange(Q_BLOCK)
        return _diff_attend(q_blk, k, v, t_pos, s_pos, lam)

    o = lax.map(one_block, (q_blocks, jnp.arange(nb)))
    return o.transpose(1, 0, 2, 3, 4).reshape(b, s, N_HEADS, V_DIM)


def _attend_sample(q, k, v, lam, cache_k, cache_v):
    past, n = cache_k.shape[1], q.shape[1]
    k_all = jnp.concatenate([cache_k.astype(k.dtype), k], axis=1)
    v_all = jnp.concatenate([cache_v.astype(v.dtype), v], axis=1)
    t_pos = past + jnp.arange(n)
    s_pos = jnp.arange(past + n)
    return _diff_attend(q, k_all, v_all, t_pos, s_pos, lam)


def _spatial_gate(vn, ws, bs):
    n = vn.shape[-3]
    ws_m = jnp.tril(ws[:, :n, :n])
    mixed = jnp.einsum('gts,...sgd->...tgd', ws_m, vn)
    return mixed + bs[:, :n].T[:, :, None]


def _layer(x, lp, lambda_init, attend, chunked):
    b, s, _ = x.shape
    h = _rmsnorm(x, lp['norm_mix_pre'])
    z = h @ lp['w_in']
    splits = np.cumsum([QK_WIDTH, QK_WIDTH, ATTN_WIDTH, GM_WIDTH, GM_WIDTH, D_MODEL]).tolist()
    zq, zk, zv, zu, zg, za, zb = jnp.split(z, splits, axis=-1)

    q = zq.reshape(b, s, N_HEADS, 2, HEAD_DIM) * (HEAD_DIM ** -0.5)
    k = zk.reshape(b, s, N_HEADS, 2, HEAD_DIM)
    v = zv.reshape(b, s, N_HEADS, V_DIM)
    lam = (jnp.exp(jnp.sum(lp['lambda_q1'].astype(jnp.float32) * lp['lambda_k1'].astype(jnp.float32)))
           - jnp.exp(jnp.sum(lp['lambda_q2'].astype(jnp.float32) * lp['lambda_k2'].astype(jnp.float32)))
           + lambda_init)
    o = attend(q, k, v, lam)
    o = _rmsnorm(o, lp['subln_w'], SUBLN_EPS) * (1.0 - lambda_init)
    attn_out = o.reshape(b, s, ATTN_WIDTH).astype(x.dtype)

    u = jax.nn.gelu(zu, approximate=False)
    vn = _layernorm(jax.nn.gelu(zg, approximate=False), lp['gm_ln_w'], lp['gm_ln_b'])
    vn = vn.reshape(b, s, GM_GROUPS, GM_GROUP_DIM)
    if chunked:
        vc = vn.reshape(b, s // GM_CHUNK, GM_CHUNK, GM_GROUPS, GM_GROUP_DIM)
        mixed = _spatial_gate(vc, lp['gm_ws'], lp['gm_bs']).reshape(b, s, GM_WIDTH)
    else:
        mixed = _spatial_gate(vn, lp['gm_ws'], lp['gm_bs']).reshape(b, s, GM_WIDTH)
    gm_out = u * mixed

    merged = (jax.nn.sigmoid(za) * (attn_out @ lp['w_branch_attn'])
              + jax.nn.sigmoid(zb) * (gm_out @ lp['w_branch_gmlp']))
    x = x + _rmsnorm(merged @ lp['w_out'], lp['norm_mix_post'])

    h2 = _rmsnorm(x, lp['norm_ffn_pre'])
    f = (jax.nn.silu(h2 @ lp['w_ffn_gate']) * (h2 @ lp['w_ffn_up'])) @ lp['w_ffn_down']
    x = x + _rmsnorm(f, lp['norm_ffn_post'])
    return x, k, v, vn


def setup_inputs(seed: int = 0) -> dict:
    key = jax.random.key(seed)
    ks = jax.random.split(key, 24)

    def nrm(k, shape, scale=1.0):
        return scale * jax.random.normal(k, shape, jnp.float32)

    return {
        'x_prompt': nrm(ks[0], (BATCH, SEQ, D_MODEL)),
        'x_sample': nrm(ks[1], (DEC_BATCH, DEC_SEQ, D_MODEL)),
        'cache_k': nrm(ks[2], (DEPTH, DEC_BATCH, PAST_LEN, N_HEADS, 2, HEAD_DIM)),
        'cache_v': nrm(ks[3], (DEPTH, DEC_BATCH, PAST_LEN, N_HEADS, V_DIM)),
        'norm_mix_pre': 1.0 + nrm(ks[4], (DEPTH, D_MODEL), 0.05),
        'norm_mix_post': 1.0 + nrm(ks[5], (DEPTH, D_MODEL), 0.05),
        'w_in': nrm(ks[6], (DEPTH, D_MODEL, IN_WIDTH), D_MODEL ** -0.5),
        'lambda_q1': nrm(ks[7], (DEPTH, HEAD_DIM), 0.1),
        'lambda_k1': nrm(ks[8], (DEPTH, HEAD_DIM), 0.1),
        'lambda_q2': nrm(ks[9], (DEPTH, HEAD_DIM), 0.1),
        'lambda_k2': nrm(ks[10], (DEPTH, HEAD_DIM), 0.1),
        'subln_w': 1.0 + nrm(ks[11], (DEPTH, V_DIM), 0.05),
        'gm_ln_w': 1.0 + nrm(ks[12], (DEPTH, GM_WIDTH), 0.05),
        'gm_ln_b': nrm(ks[13], (DEPTH, GM_WIDTH), 0.02),
        'gm_ws': nrm(ks[14], (DEPTH, GM_GROUPS, GM_CHUNK, GM_CHUNK), GM_CHUNK ** -0.5),
        'gm_bs': 1.0 + nrm(ks[15], (DEPTH, GM_GROUPS, GM_CHUNK), 0.05),
        'w_branch_attn': nrm(ks[16], (DEPTH, ATTN_WIDTH, D_MODEL), ATTN_WIDTH ** -0.5),
        'w_branch_gmlp': nrm(ks[17], (DEPTH, GM_WIDTH, D_MODEL), GM_WIDTH ** -0.5),
        'w_out': nrm(ks[18], (DEPTH, D_MODEL, D_MODEL), D_MODEL ** -0.5),
        'norm_ffn_pre': 1.0 + nrm(ks[19], (DEPTH, D_MODEL), 0.05),
        'norm_ffn_post': 1.0 + nrm(ks[20], (DEPTH, D_MODEL), 0.05),
        'w_ffn_gate': nrm(ks[21], (DEPTH, D_MODEL, D_FF), D_MODEL ** -0.5),
        'w_ffn_up': nrm(ks[22], (DEPTH, D_MODEL, D_FF), D_MODEL ** -0.5),
        'w_ffn_down': nrm(ks[23], (DEPTH, D_FF, D_MODEL), D_FF ** -0.5),
    }


def reference(x_prompt, x_sample, cache_k, cache_v, norm_mix_pre, norm_mix_post, w_in,
              lambda_q1, lambda_k1, lambda_q2, lambda_k2, subln_w, gm_ln_w, gm_ln_b, gm_ws, gm_bs,
              w_branch_attn, w_branch_gmlp, w_out, norm_ffn_pre, norm_ffn_post,
              w_ffn_gate, w_ffn_up, w_ffn_down):
    y_prompt, y_sample = x_prompt, x_sample
    k_prompt_rows, v_prompt_rows, k_sample_rows, v_sample_rows, gm_sample_rows = [], [], [], [], []
    for l in range(DEPTH):
        lp = dict(norm_mix_pre=norm_mix_pre[l], norm_mix_post=norm_mix_post[l], w_in=w_in[l],
                  lambda_q1=lambda_q1[l], lambda_k1=lambda_k1[l], lambda_q2=lambda_q2[l],
                  lambda_k2=lambda_k2[l], subln_w=subln_w[l], gm_ln_w=gm_ln_w[l], gm_ln_b=gm_ln_b[l],
                  gm_ws=gm_ws[l], gm_bs=gm_bs[l], w_branch_attn=w_branch_attn[l],
                  w_branch_gmlp=w_branch_gmlp[l], w_out=w_out[l], norm_ffn_pre=norm_ffn_pre[l],
                  norm_ffn_post=norm_ffn_post[l], w_ffn_gate=w_ffn_gate[l], w_ffn_up=w_ffn_up[l],
                  w_ffn_down=w_ffn_down[l])
        lambda_init = 0.8 - 0.6 * math.exp(-0.3 * l)
        y_prompt, kp, vp, _ = _layer(y_prompt, lp, lambda_init, _attend_prompt, True)
        attend_s = functools.partial(_attend_sample, cache_k=cache_k[l], cache_v=cache_v[l])
        y_sample, ks_, vs_, gs_ = _layer(y_sample, lp, lambda_init, attend_s, False)
        k_prompt_rows.append(kp)
        v_prompt_rows.append(vp)
        k_sample_rows.append(ks_)
        v_sample_rows.append(vs_)
        gm_sample_rows.append(gs_)
    return (y_prompt, y_sample, jnp.stack(k_prompt_rows), jnp.stack(v_prompt_rows),
            jnp.stack(k_sample_rows), jnp.stack(v_sample_rows), jnp.stack(gm_sample_rows))
```

```python
import numpy as np
from contextlib import ExitStack
import concourse.bass as bass
import concourse.mybir as mybir
from concourse.bass_utils import run_bass_kernel_spmd

F32 = mybir.dt.float32
BF16 = mybir.dt.bfloat16
U8 = mybir.dt.uint8
AF = mybir.ActivationFunctionType
ALU = mybir.AluOpType

D = 2048
NH = 8
HD = 128
VD = 256
DFF = 5632
INW = 14336
TT = 256
PAST = 1024
NSMP = 16
CK = 8
NM = 68
NEG = -30000.0
LAMBDA_INIT = 0.8 - 0.6 * 1.0
PAGE = 4096


def _dsz(dt):
    if dt == F32:
        return 4
    if dt == BF16:
        return 2
    if dt == U8:
        return 1
    raise ValueError(str(dt))


class Sync:
    LIMIT = 30000
    NDS = 24

    def __init__(self, nc, es):
        self.nc = nc
        self.es = es
        self.engs = {'pe': nc.tensor, 'act': nc.scalar, 'dve': nc.vector, 'pool': nc.gpsimd, 'sp': nc.sync}
        self.sems = {}
        self.owner = {}
        self.cur = {}
        self.seen = {e: {} for e in self.engs}
        self.W = {}
        self.R = {}
        self.pend = {e: ([], []) for e in self.engs}
        self.nalloc = 0
        for e in ('pe', 'act', 'dve', 'pool'):
            self._epoch(e)
        self.dsem = []
        for k in range(self.NDS):
            key = self._alloc("dq%d" % k, 'dma')
            self.dsem.append([key, 0])
        self.drr = 0
        self.nwait = 0
        self.nops = 0

    def _alloc(self, name, owner):
        h = self.es.enter_context(self.nc.semaphore(name))
        self.sems[name] = h
        self.owner[name] = owner
        self.nalloc += 1
        return name

    def _epoch(self, e):
        key = self._alloc("%s_e%d" % (e, self.nalloc), e)
        self.cur[e] = [key, 0]

    def reg(self, x):
        if isinstance(x, tuple):
            return [x]
        ap = x
        esz = _dsz(ap.dtype)
        a = ap.ap
        pstep = a[0][0]
        off = ap.offset - ap.start_partition() * pstep if pstep else ap.offset
        ext = 1
        for s, c in a[1:]:
            ext += (c - 1) * abs(s)
        lo = off * esz
        hi = (off + ext) * esz
        name = ap.name
        if name.startswith('ps'):
            return [((name, 0), 0, 2048)]
        out = []
        pg = lo // PAGE
        while pg * PAGE < hi:
            l = max(lo, pg * PAGE)
            h = min(hi, (pg + 1) * PAGE)
            out.append(((name, pg), l, h))
            pg += 1
        return out

    def regs(self, xs):
        out = []
        for x in xs:
            if x is None:
                continue
            out.extend(self.reg(x))
        return out

    def _deps(self, eng, r, w):
        evs = {}
        own = self.owner

        def add(k, v):
            if evs.get(k, 0) < v:
                evs[k] = v
        for (sp, lo, hi) in r:
            for e in self.W.get(sp, ()):
                if e[0] < hi and lo < e[1]:
                    add(e[2][0], e[2][1])
            if sp[0].startswith('ps'):
                for e in self.R.get(sp, ()):
                    for k, v in e[2].items():
                        if own[k] != eng:
                            add(k, v)
        for (sp, lo, hi) in w:
            for e in self.W.get(sp, ()):
                if e[0] < hi and lo < e[1]:
                    add(e[2][0], e[2][1])
            for e in self.R.get(sp, ()):
                if e[0] < hi and lo < e[1]:
                    for k, v in e[2].items():
                        if own[k] == eng:
                            continue
                        add(k, v)
        if eng == 'pe':
            evs = {k: v for k, v in evs.items() if own[k] != 'pe'}
        return evs

    def _wait(self, eng, evs):
        seen = self.seen[eng]
        for k, v in evs.items():
            if seen.get(k, 0) >= v:
                continue
            self.engs[eng].wait_ge(self.sems[k], v)
            seen[k] = v
            self.nwait += 1

    def _commit(self, r, w, ev):
        k, v = ev
        for (sp, lo, hi) in w:
            wl = self.W.get(sp)
            if wl is None:
                wl = self.W[sp] = []
            else:
                wl[:] = [e for e in wl if not (lo <= e[0] and e[1] <= hi)]
            wl.append((lo, hi, ev))
            rl = self.R.get(sp)
            if rl:
                rl[:] = [e for e in rl if not (lo <= e[0] and e[1] <= hi)]
        for (sp, lo, hi) in r:
            rl = self.R.get(sp)
            if rl is None:
                rl = self.R[sp] = []
            for e in rl:
                if e[0] == lo and e[1] == hi:
                    if e[2].get(k, 0) < v:
                        e[2][k] = v
                    break
            else:
                rl.append((lo, hi, {k: v}))

    def op(self, eng, fn, reads=(), writes=(), inc=True):
        r = self.regs(reads)
        w = self.regs(writes)
        self._wait(eng, self._deps(eng, r, w))
        ins = fn()
        self.nops += 1
        pr, pw = self.pend[eng]
        if inc:
            c = self.cur[eng]
            c[1] += 1
            ins.then_inc(self.sems[c[0]], 1)
            ev = (c[0], c[1])
            if pr or pw:
                r = pr + r
                w = pw + w
                self.pend[eng] = ([], [])
            self._commit(r, w, ev)
            if c[1] >= self.LIMIT:
                self._epoch(eng)
        else:
            pr.extend(r)
            pw.extend(w)
        return ins

    def dma(self, out, in_, reads=(), writes=(), q='sp', **kw):
        r = self.regs(reads)
        w = self.regs(writes)
        evs = self._deps(q, r, w)
        slot = self.dsem[self.drr]
        self.drr = (self.drr + 1) % self.NDS
        if slot[1] > 0 and evs.get(slot[0], 0) < slot[1]:
            evs[slot[0]] = slot[1]
        self._wait(q, evs)
        ins = self.engs[q].dma_start(out=out, in_=in_, **kw)
        slot[1] += 16
        ins.then_inc(self.sems[slot[0]], 16)
        self._commit(r, w, (slot[0], slot[1]))
        self.nops += 1
        return ins

    def finish(self, q='sp'):
        evs = {s[0]: s[1] for s in self.dsem if s[1] > 0}
        self._wait(q, evs)


class Prog:
    def __init__(self, NT, with_sample=True):
        self.NT = NT
        self.NKP = NT * 4 * 128
        self.with_sample = with_sample
        self.es = ExitStack()
        self.nc = nc = bass.Bass("TRN2", target_bir_lowering=False)
        self.S = None

    def dram_in(self, name, shape, dt=F32):
        return self.nc.dram_tensor(name, list(shape), dt, kind="ExternalInput").ap()

    def dram_out(self, name, shape, dt=F32):
        return self.nc.dram_tensor(name, list(shape), dt, kind="ExternalOutput").ap()

    def dram_tmp(self, name, shape, dt):
        return self.nc.dram_tensor(name, list(shape), dt, kind="Internal").ap()

    def alloc(self, nbytes, align=64):
        self.top = (self.top + align - 1) // align * align
        o = self.top
        self.top += nbytes
        return o

    def view(self, off, dt, shape):
        n = 1
        for s in shape[1:]:
            n *= s
        v = self.arena[:, off:off + n * _dsz(dt)].bitcast(dt)
        if len(shape) == 3:
            v = v.rearrange("p (a b) -> p a b", a=shape[1])
        elif len(shape) == 4:
            v = v.rearrange("p (a b c) -> p a b c", a=shape[1], b=shape[2])
        return v

    def build(self):
        nc = self.nc
        es = self.es
        NT = self.NT
        NKT = self.NKP + PAST + 128
        self.NKT = NKT
        d = {}
        d['x_own'] = self.dram_in('x_own', [NT, TT, D])
        d['x_oth'] = self.dram_in('x_oth', [NT, TT, D])
        d['x_smp'] = self.dram_in('x_smp', [NSMP, D])
        d['cache_k'] = self.dram_in('cache_k', [PAST, D])
        d['cache_v'] = self.dram_in('cache_v', [PAST, D])
        for nm in ('norm_mix_pre', 'norm_mix_post', 'norm_ffn_pre', 'norm_ffn_post', 'gm_ln_w', 'gm_ln_b'):
            d[nm] = self.dram_in(nm, [1, D])
        for nm in ('lambda_q1', 'lambda_k1', 'lambda_q2', 'lambda_k2'):
            d[nm] = self.dram_in(nm, [1, HD])
        d['subln_w'] = self.dram_in('subln_w', [1, VD])
        d['gm_ws'] = self.dram_in('gm_ws', [8, 128, 128])
        d['gm_bs'] = self.dram_in('gm_bs', [8, 128])
        d['w_in'] = self.dram_in('w_in', [D, INW])
        d['w_branch_attn'] = self.dram_in('w_branch_attn', [D, D])
        d['w_branch_gmlp'] = self.dram_in('w_branch_gmlp', [D, D])
        d['w_out'] = self.dram_in('w_out', [D, D])
        d['w_ffn_gate'] = self.dram_in('w_ffn_gate', [D, DFF])
        d['w_ffn_up'] = self.dram_in('w_ffn_up', [D, DFF])
        d['w_ffn_down'] = self.dram_in('w_ffn_down', [DFF, D])
        d['c_ident'] = self.dram_in('c_ident', [128, 128])
        d['c_tril'] = self.dram_in('c_tril', [128, 128])
        d['c_btile'] = self.dram_in('c_btile', [NH, 128, 5 * 256])
        d['c_cc'] = self.dram_in('c_cc', [128, NH * NM])
        d['c_sbias'] = self.dram_in('c_sbias', [NH, 128, 9 * NSMP])
        d['y_own'] = self.dram_out('y_own', [NT, TT, D])
        d['k_own'] = self.dram_out('k_own', [NT, TT, D])
        d['v_own'] = self.dram_out('v_own', [NT, TT, D])
        d['y_smp'] = self.dram_out('y_smp', [NSMP, D])
        d['k_smp'] = self.dram_out('k_smp', [NSMP, D])
        d['v_smp'] = self.dram_out('v_smp', [NSMP, D])
        d['g_smp'] = self.dram_out('g_smp', [NSMP, D])
        self.blocks = self.block_list()
        NB = len(self.blocks)
        d['wsc'] = self.dram_tmp('wsc', [NB, 128, 16 * 512], BF16)
        d['KT'] = self.dram_tmp('KTs', [2 * NH, 128, NKT], BF16)
        d['V'] = self.dram_tmp('Vs', [NH, NKT, VD], BF16)
        self.d = d
        ARENA = 200704
        self.arena = es.enter_context(nc.sbuf_tensor("arena", [128, ARENA], U8))
        self.banks = [es.enter_context(nc.psum_tensor("ps%d" % i, [128, 512], F32)) for i in range(8)]
        self.S = Sync(nc, es)
        self.top = 0
        self.rrA = 0
        v = self.view
        self.ring = [v(self.alloc(16384), BF16, [128, 16, 512]) for _ in range(3)]
        self.ident = v(self.alloc(256), BF16, [128, 128])
        self.wsT = v(self.alloc(2048), BF16, [128, 8, 128])
        self.bsT = v(self.alloc(32), F32, [128, 8])
        self.gpre = v(self.alloc(64), F32, [128, 16])
        self.gffn = v(self.alloc(64), F32, [128, 16])
        self.cst = v(self.alloc(32), F32, [128, 8])
        self.stat = v(self.alloc(1024), F32, [128, 256])
        self.nstat = 0
        self.sublnrow = v(self.alloc(1024), F32, [128, 256])
        self.cc = v(self.alloc(NH * NM * 4), F32, [128, NH * NM])
        self.rowA = v(self.alloc(8192), F32, [128, D])
        self.rowB = v(self.alloc(8192), F32, [128, D])
        self.xs = v(self.alloc(8192), F32, [128, D])
        self.xn = v(self.alloc(4096), BF16, [128, D])
        self.junk = v(self.alloc(4096), BF16, [128, D])
        self.junk2 = v(self.alloc(4096), BF16, [128, D])
        pst = self.alloc(8192)
        self.kout = v(pst, F32, [128, 512])
        self.kbf = v(pst + 2048, BF16, [128, 512])
        self.kTst = v(pst + 3072, BF16, [128, 4, 256])
        self.vout = v(pst + 5120, F32, [128, 512])
        self.vbf = v(pst + 7168, BF16, [128, 512])
        oq = self.alloc(8192)
        self.qT = v(oq, BF16, [128, 16, TT])
        self.mergedT = self.qT
        Y = self.alloc(57344)
        self.hT = v(Y, BF16, [128, 16, TT])
        self.gg = v(Y + 8192, F32, [128, 2, D])
        self.fo = self.gg
        self.u = v(Y + 24576, BF16, [128, 2, D])
        self.vn = v(Y + 32768, BF16, [128, 2, D])
        self.saT = v(Y + 40960, BF16, [128, 16, TT])
        self.sbT = v(Y + 49152, BF16, [128, 16, TT])
        self.f1T = v(Y + 24576, BF16, [128, 44, TT])
        self.sg = v(Y + 24576 + 22528, F32, [128, 4, TT])
        a0 = Y
        self.Kc = [v(a0 + i * 8256, BF16, [128, 2, CK * 128]) for i in range(2)]
        self.Vc = [v(a0 + i * 8256 + 4096, BF16, [128, CK, 260]) for i in range(2)]
        a1 = a0 + 2 * 8256
        self.btl = [v(a1 + i * 5120, F32, [128, 5, 256]) for i in range(2)]
        a2 = a1 + 10240
        self.sadd = [v(a2 + i * 2048, F32, [128, 2, 256]) for i in range(2)]
        a3 = a2 + 4096
        self.PT = [v(a3 + i * 1024, BF16, [128, 2, 256]) for i in range(3)]
        a4 = a3 + 3072
        self.of32 = [v(a4 + i * 1024, F32, [128, 256]) for i in range(2)]
        assert a4 + 2048 <= Y + 40960
        self.cst_f = [v(Y + i * 8192, F32, [128, 4, 512]) for i in range(4)]
        self.cst_b = [v(Y + 32768 + i * 4096, BF16, [128, 4, 512]) for i in range(4)]
        self.gmT = v(self.alloc(8192), BF16, [128, 16, TT])
        R8 = self.alloc(16384)
        self.ao_tm = v(R8, BF16, [128, 2, D])
        self.aoT = v(R8 + 8192, BF16, [128, 16, TT])
        self.mo = v(R8, F32, [128, 2, D])
        self.m1 = v(self.alloc(4096), F32, [128, 4, TT])
        self.tmp2 = v(self.alloc(1024), F32, [128, TT])
        assert self.top <= ARENA, self.top

        import os
        self.stop = os.environ.get("PROG_STOP", "")
        self.prologue()
        if self.stop.startswith("pro"):
            self.S.finish()
            es.close()
            return nc
        self.wseq = self.weight_sequence()
        self.wpos = 0
        self.wload = 0
        for _ in range(3):
            self.issue_wload()
        try:
            for i in range(NT):
                self.tile('oth', i)
                self.ck('oth')
                self.tile('own', i)
                self.ck('own')
            if self.with_sample:
                self.tile('smp', 0)
            assert self.wpos == len(self.wseq), (self.wpos, len(self.wseq))
        except StopIteration:
            pass
        self.S.finish()
        es.close()
        return nc

    def ck(self, name):
        if self.stop == name:
            raise StopIteration()

    def block_list(self):
        bl = []
        for cb in range(INW // 512):
            bl.append(('w_in', cb, 0, 16, 'pre'))
        for cb in range(4):
            bl.append(('w_branch_attn', cb, 0, 16, None))
        for cb in range(4):
            bl.append(('w_branch_gmlp', cb, 0, 16, None))
        for cb in range(4):
            bl.append(('w_out', cb, 0, 16, None))
        for cb in range(11):
            bl.append(('w_ffn_gate', cb, 0, 16, 'ffn'))
        for cb in range(11):
            bl.append(('w_ffn_up', cb, 0, 16, 'ffn'))
        for cb in range(4):
            for (k0, ks) in ((0, 16), (16, 16), (32, 12)):
                bl.append(('w_ffn_down', cb, k0, ks, None))
        self.bidx = {(b[0], b[1], b[2]): i for i, b in enumerate(bl)}
        return bl

    def weight_sequence(self):
        seq = []
        bi = self.bidx

        def full():
            s = [bi[('w_in', cb, 0)] for cb in range(28)]
            for cb in range(4):
                s.append(bi[('w_branch_attn', cb, 0)])
                s.append(bi[('w_branch_gmlp', cb, 0)])
            s += [bi[('w_out', cb, 0)] for cb in range(4)]
            for cb in range(11):
                s.append(bi[('w_ffn_gate', cb, 0)])
                s.append(bi[('w_ffn_up', cb, 0)])
            for cb in range(4):
                for k0 in (0, 16, 32):
                    s.append(bi[('w_ffn_down', cb, k0)])
            return s
        for i in range(self.NT):
            seq += [bi[('w_in', cb, 0)] for cb in range(4, 12)]
            seq += full()
        if self.with_sample:
            seq += full()
        return seq

    def issue_wload(self):
        if self.wload >= len(self.wseq):
            return
        b = self.wseq[self.wload]
        buf = self.ring[self.wload % 3]
        kcs = self.blocks[b][3]
        src = self.d['wsc'][b].rearrange("p (k n) -> p k n", k=16)
        for k0 in range(0, kcs, 4):
            k1 = min(kcs, k0 + 4)
            self.S.dma(buf[:, k0:k1, :], src[:, k0:k1, :], reads=[('wsc', b, b + 1)], writes=[buf[:, k0:k1, :]])
        self.wload += 1

    def wnext(self, name, cb, k0=0):
        b = self.wseq[self.wpos]
        assert b == self.bidx[(name, cb, k0)], (self.wpos, b, name, cb, k0)
        buf = self.ring[self.wpos % 3]
        self.wpos += 1
        return buf, self.blocks[b][3]

    def wdone(self):
        self.issue_wload()

    def stc(self, n=1):
        c = self.nstat
        self.nstat = (self.nstat + n) % 240
        if self.nstat + 16 > 240:
            self.nstat = 0
        return self.stat[:, c:c + n]

    def bank(self):
        b = self.banks[self.rrA % 4]
        self.rrA += 1
        return b

    def prologue(self):
        S = self.S
        nc = self.nc
        d = self.d
        V, G, A = nc.vector, nc.gpsimd, nc.scalar
        S.op('dve', lambda: V.memset(self.cst[:, 0:1], 1.0), writes=[self.cst[:, 0:1]])
        S.op('dve', lambda: V.memset(self.cst[:, 1:2], -0.5), writes=[self.cst[:, 1:2]])
        S.op('dve', lambda: V.memset(self.cst[:, 3:4], 0.0), writes=[self.cst[:, 3:4]])
        for i in range(2):
            ones = self.Vc[i][:, :, 256:257]
            S.op('dve', lambda ones=ones: V.memset(ones, 1.0), writes=[ones])
        xsv = self.xs[:, 0:128]
        S.dma(xsv, d['c_ident'], writes=[xsv])
        S.op('dve', lambda: V.tensor_scalar(out=self.ident, in0=xsv, scalar1=1.0, scalar2=None, op0=ALU.mult), reads=[xsv], writes=[self.ident])
        S.dma(self.cc, d['c_cc'], writes=[self.cc])
        for (dstv, srcap, nr, coff) in ((self.bsT, d['gm_bs'], 8, 1280),
                                        (self.gpre, d['norm_mix_pre'].rearrange("o (k p) -> (o k) p", p=128), 16, 1408),
                                        (self.gffn, d['norm_ffn_pre'].rearrange("o (k p) -> (o k) p", p=128), 16, 1536)):
            stg = self.xs[0:nr, coff:coff + 128]
            S.dma(stg, srcap, writes=[stg])
            bk = self.bank()
            o_ = bk[:, 0:nr]
            S.op('pe', lambda o_=o_, stg=stg, nr=nr: nc.tensor.transpose(out=o_, in_=stg, identity=xsv[0:nr, 0:nr]),
                 reads=[stg, xsv], writes=[o_])
            S.op('dve', lambda o_=o_, dstv=dstv: V.tensor_scalar(out=dstv, in0=o_, scalar1=1.0, scalar2=None, op0=ALU.mult), reads=[o_], writes=[dstv])
        S.dma(self.sublnrow, d['subln_w'].partition_broadcast(128), writes=[self.sublnrow])
        S.op('dve', lambda: V.tensor_scalar(out=self.sublnrow, in0=self.sublnrow, scalar1=1.0 - LAMBDA_INIT,
                                            scalar2=None, op0=ALU.mult), reads=[self.sublnrow], writes=[self.sublnrow])
        lw = self.xs[:, 512:1024].rearrange("p (a b) -> p a b", a=4)
        for j, nm in enumerate(('lambda_q1', 'lambda_k1', 'lambda_q2', 'lambda_k2')):
            S.dma(lw[:, j, :], d[nm].partition_broadcast(128), writes=[lw[:, j, :]])
        s1 = self.stc()
        s2 = self.stc()
        jk = self.xs[:, 1024:1152]
        S.op('dve', lambda: V.scalar_tensor_tensor(out=jk, in0=lw[:, 0, :], scalar=1.0, in1=lw[:, 1, :], op0=ALU.mult, op1=ALU.mult, accum_out=s1),
             reads=[lw[:, 0, :], lw[:, 1, :]], writes=[jk, s1])
        S.op('dve', lambda: V.scalar_tensor_tensor(out=jk, in0=lw[:, 2, :], scalar=1.0, in1=lw[:, 3, :], op0=ALU.mult, op1=ALU.mult, accum_out=s2),
             reads=[lw[:, 2, :], lw[:, 3, :]], writes=[jk, s2])
        e1 = self.stc()
        e2 = self.stc()
        S.op('act', lambda: A.activation(out=e1, in_=s1, func=AF.Exp), reads=[s1], writes=[e1])
        S.op('act', lambda: A.activation(out=e2, in_=s2, func=AF.Exp), reads=[s2], writes=[e2])
        t = self.stc()
        S.op('dve', lambda: V.tensor_tensor(out=t, in0=e2, in1=e1, op=ALU.subtract), reads=[e1, e2], writes=[t])
        S.op('dve', lambda: V.tensor_scalar(out=self.cst[:, 2:3], in0=t, scalar1=-LAMBDA_INIT, scalar2=None, op0=ALU.add),
             reads=[t], writes=[self.cst[:, 2:3]])
        wsf = self.gg[:, 0, 0:1024].rearrange("p (g s) -> p g s", g=8)
        S.dma(wsf, d['gm_ws'].rearrange("g t s -> t g s"), writes=[wsf])
        trl = self.xs[:, 128:256]
        S.dma(trl, d['c_tril'], writes=[trl])
        wsb = self.gg[:, 1, 0:512].bitcast(BF16).rearrange("p (g s) -> p g s", g=8)
        S.op('dve', lambda: V.tensor_tensor(out=wsb, in0=wsf, in1=trl.unsqueeze(1).to_broadcast([128, 8, 128]), op=ALU.mult),
             reads=[wsf, trl], writes=[wsb])
        for g2 in range(2):
            bk = self.bank()
            bkb = bk[:, :].bitcast(BF16)
            for j in range(4):
                g = g2 * 4 + j
                S.op('pe', lambda g=g, j=j: nc.tensor.transpose(out=bkb[:, j * 128:(j + 1) * 128], in_=wsb[:, g, :], identity=self.ident),
                     reads=[wsb[:, g, :], self.ident], writes=[bkb[:, j * 128:(j + 1) * 128]], inc=(j == 3))
            S.op('dve', lambda g2=g2: V.tensor_scalar(out=self.wsT[:, g2 * 4:(g2 + 1) * 4, :],
                                                    in0=bkb[:, 0:512].rearrange("p (a b) -> p a b", a=4), scalar1=1.0, scalar2=None, op0=ALU.mult),
                 reads=[bkb[:, 0:512]], writes=[self.wsT[:, g2 * 4:(g2 + 1) * 4, :]])
        if self.stop == "pro1":
            return
        engs = ['act', 'dve', 'pool']
        n = 0
        for b, (nm, cb, k0, kcs, gain) in enumerate(self.blocks):
            Wm = d[nm].rearrange("(k p) n -> p k n", p=128)
            dst = d['wsc'][b].rearrange("p (k n) -> p k n", k=16)
            for h0 in range(0, kcs, 4):
                hs = min(4, kcs - h0)
                sf = self.cst_f[n % 4]
                sb = self.cst_b[n % 4]
                S.dma(sf[:, 0:hs, :], Wm[:, k0 + h0:k0 + h0 + hs, cb * 512:(cb + 1) * 512], writes=[sf[:, 0:hs, :]])
                for kk in range(hs):
                    kc = k0 + h0 + kk
                    if gain == 'pre':
                        sc = self.gpre[:, kc:kc + 1]
                    elif gain == 'ffn':
                        sc = self.gffn[:, kc:kc + 1]
                    else:
                        sc = self.cst[:, 0:1]
                    e = engs[(n * 4 + kk) % 3]
                    o_, i_ = sb[:, kk, :], sf[:, kk, :]
                    if e == 'act':
                        S.op('act', lambda o_=o_, i_=i_, sc=sc: A.activation(out=o_, in_=i_, func=AF.Copy, scale=sc),
                             reads=[i_, sc], writes=[o_])
                    elif e == 'dve':
                        S.op('dve', lambda o_=o_, i_=i_, sc=sc: V.tensor_scalar(out=o_, in0=i_, scalar1=sc, scalar2=None, op0=ALU.mult),
                             reads=[i_, sc], writes=[o_])
                    else:
                        S.op('pool', lambda o_=o_, i_=i_, sc=sc: G.tensor_scalar(out=o_, in0=i_, scalar1=sc, scalar2=None, op0=ALU.mult),
                             reads=[i_, sc], writes=[o_])
                S.dma(dst[:, h0:h0 + hs, :], sb[:, 0:hs, :], reads=[sb[:, 0:hs, :]], writes=[('wsc', b, b + 1)])
                n += 1

    def rstd_from_ss(self, ss, n, eps, rows):
        S = self.S
        V, G = self.nc.vector, self.nc.gpsimd
        vv = self.stc()
        rs = self.stc()
        S.op('dve', lambda: V.tensor_scalar(out=vv[0:rows], in0=ss[0:rows], scalar1=1.0 / n, scalar2=eps, op0=ALU.mult, op1=ALU.add),
             reads=[ss], writes=[vv])
        S.op('pool', lambda: G.tensor_tensor(out=rs[0:rows], in0=vv[0:rows], in1=self.cst[0:rows, 1:2], op=ALU.pow),
             reads=[vv, self.cst[:, 1:2]], writes=[rs])
        return rs

    def transpose_tm(self, src, dstT, st, rows, evac_eng=('dve', 'act')):
        S = self.S
        nc = self.nc
        for cg in range(4):
            bk = self.bank()
            bkb = bk[:, :].bitcast(BF16)
            for j in range(4):
                c = cg * 4 + j
                i_ = src[0:rows, c * 128:(c + 1) * 128]
                o_ = bkb[:, j * 128:j * 128 + rows]
                S.op('pe', lambda i_=i_, o_=o_: nc.tensor.transpose(out=o_, in_=i_, identity=self.ident[0:rows, 0:rows]),
                     reads=[i_, self.ident], writes=[o_], inc=(j == 3))
            srcv = bkb[:, 0:512].rearrange("p (a b) -> p a b", a=4)[:, :, 0:rows]
            dstv = dstT[:, cg * 4:(cg + 1) * 4, st * 128:st * 128 + rows]
            e = evac_eng[cg % len(evac_eng)]
            if e == 'dve':
                S.op('dve', lambda srcv=srcv, dstv=dstv: nc.vector.tensor_scalar(out=dstv, in0=srcv, scalar1=1.0, scalar2=None, op0=ALU.mult), reads=[srcv], writes=[dstv])
            else:
                S.op('act', lambda srcv=srcv, dstv=dstv: nc.scalar.copy(out=dstv, in_=srcv), reads=[srcv], writes=[dstv])

    def norm_to_T(self, src_f32, dstT, st, rows):
        S = self.S
        nc = self.nc
        ss = self.stc()
        S.op('act', lambda: nc.scalar.activation(out=self.junk[0:rows, :], in_=src_f32[0:rows, :], func=AF.Square, accum_out=ss[0:rows]),
             reads=[src_f32[0:rows, :]], writes=[ss, self.junk[0:rows, :]])
        rs = self.rstd_from_ss(ss, D, 1e-6, rows)
        S.op('act', lambda: nc.scalar.activation(out=self.xn[0:rows, :], in_=src_f32[0:rows, :], func=AF.Copy, scale=rs[0:rows]),
             reads=[src_f32[0:rows, :], rs], writes=[self.xn[0:rows, :]])
        self.transpose_tm(self.xn, dstT, st, rows)

    def a_type(self, blk, kcs, srcT, kc0, T, evac):
        S = self.S
        nc = self.nc
        for oc in range(4):
            bk = self.bank()
            o_ = bk[:, 0:T]
            for kc in range(kcs):
                l_ = blk[:, kc, oc * 128:(oc + 1) * 128]
                r_ = srcT[:, kc0 + kc, 0:T]
                S.op('pe', lambda l_=l_, r_=r_, kc=kc: nc.tensor.matmul(o_, lhsT=l_, rhs=r_, start=(kc == 0), stop=(kc == kcs - 1)),
                     reads=[l_, r_], writes=[o_], inc=(kc == kcs - 1))
            evac(oc, o_)

    def b_type(self, blk, kcs, srcT, kc0, nst, rows, evac, banks=None, first=True, last=True):
        S = self.S
        nc = self.nc
        for st in range(nst):
            bk = banks[st] if banks else self.bank()
            o_ = bk[0:rows, :]
            for kc in range(kcs):
                l_ = srcT[:, kc0 + kc, st * 128:st * 128 + rows]
                r_ = blk[:, kc, :]
                S.op('pe', lambda l_=l_, r_=r_, kc=kc: nc.tensor.matmul(o_, lhsT=l_, rhs=r_, start=(first and kc == 0),
                                                                        stop=(last and kc == kcs - 1)),
                     reads=[l_, r_], writes=[o_], inc=(kc == kcs - 1))
            if last:
                evac(st, o_)

    def tile(self, kind, i):
        S = self.S
        nc = self.nc
        d = self.d
        V, G, A, PE = nc.vector, nc.gpsimd, nc.scalar, nc.tensor
        smp = (kind == 'smp')
        nst = 1 if smp else 2
        rows = NSMP if smp else 128
        T = NSMP if smp else TT
        full = kind in ('own', 'smp')
        if smp:
            xin = [d['x_smp']]
            gblk = [self.NKP // 128 + PAST // 128]
        else:
            xin = [d['x_' + kind][i, st * 128:(st + 1) * 128, :] for st in range(2)]
            gblk = [4 * i + 2 * st + (0 if kind == 'own' else 1) for st in range(2)]
        if smp:
            self.ingest_cache()
        for st in range(nst):
            S.dma(self.xs[0:rows, :], xin[st], writes=[self.xs[0:rows, :]])
            self.norm_to_T(self.xs, self.hT, st, rows)
        hT = self.hT
        if full:
            for b in range(4):
                blk, kcs = self.wnext('w_in', b)

                def ev_q(oc, ps, b=b):
                    o_ = self.qT[:, 4 * b + oc, 0:T]
                    S.op('act', lambda: A.activation(out=o_, in_=ps, func=AF.Copy, scale=float(HD) ** -0.5), reads=[ps], writes=[o_])
                self.a_type(blk, kcs, hT, 0, T, ev_q)
                self.wdone()
        self.ck('q')
        kdst = d['k_smp'] if smp else (d['k_own'][i] if kind == 'own' else None)
        vdst = d['v_smp'] if smp else (d['v_own'][i] if kind == 'own' else None)
        import os
        if os.environ.get("NO_KVOUT") == "1":
            kdst = vdst = None
        if os.environ.get("NO_KVOUT") == "k":
            vdst = None
        for b in range(4):
            blk, kcs = self.wnext('w_in', 4 + b)

            def ev_k(st, ps, b=b):
                if kdst is not None:
                    S.op('dve', lambda: V.tensor_scalar(out=self.kout[0:rows, :], in0=ps, scalar1=1.0, scalar2=None, op0=ALU.mult), reads=[ps], writes=[self.kout[0:rows, :]])
                    S.dma(kdst[st * 128:st * 128 + rows, b * 512:(b + 1) * 512], self.kout[0:rows, :], reads=[self.kout[0:rows, :]])
                S.op('act', lambda: A.copy(out=self.kbf[0:rows, :], in_=ps), reads=[ps], writes=[self.kbf[0:rows, :]])
                bk = self.bank()
                bkb = bk[:, :].bitcast(BF16)
                for j in range(4):
                    i_ = self.kbf[0:rows, j * 128:(j + 1) * 128]
                    o_ = bkb[:, j * 128:j * 128 + rows]
                    S.op('pe', lambda i_=i_, o_=o_: PE.transpose(out=o_, in_=i_, identity=self.ident[0:rows, 0:rows]),
                         reads=[i_, self.ident], writes=[o_], inc=(j == 3))
                sv = bkb[:, 0:512].rearrange("p (a b) -> p a b", a=4)[:, :, 0:rows]
                dv = self.kTst[:, :, st * 128:st * 128 + rows]
                S.op('dve', lambda: V.tensor_scalar(out=dv, in0=sv, scalar1=1.0, scalar2=None, op0=ALU.mult), reads=[sv], writes=[dv])
                g = gblk[st]
                dst = d['KT'][4 * b:4 * b + 4, :, g * 128:g * 128 + rows].rearrange("m d t -> d m t")
                S.dma(dst, dv, reads=[dv], writes=[('KT', g, g + 1)])
            self.b_type(blk, kcs, hT, 0, nst, rows, ev_k)
            self.wdone()
        for b in range(4):
            blk, kcs = self.wnext('w_in', 8 + b)

            def ev_v(st, ps, b=b):
                if vdst is not None:
                    S.op('dve', lambda: V.tensor_scalar(out=self.vout[0:rows, :], in0=ps, scalar1=1.0, scalar2=None, op0=ALU.mult), reads=[ps], writes=[self.vout[0:rows, :]])
                    S.dma(vdst[st * 128:st * 128 + rows, b * 512:(b + 1) * 512], self.vout[0:rows, :], reads=[self.vout[0:rows, :]])
                S.op('act', lambda: A.copy(out=self.vbf[0:rows, :], in_=ps), reads=[ps], writes=[self.vbf[0:rows, :]])
                g = gblk[st]
                dst = d['V'][2 * b:2 * b + 2, g * 128:g * 128 + rows, :].rearrange("h t e -> t h e")
                S.dma(dst, self.vbf[0:rows, :].rearrange("p (h e) -> p h e", h=2), reads=[self.vbf[0:rows, :]],
                      writes=[('V', g, g + 1)])
            self.b_type(blk, kcs, hT, 0, nst, rows, ev_v)
            self.wdone()
        if not full:
            return
        self.ck('kv')
        for b in range(4):
            blk, kcs = self.wnext('w_in', 12 + b)

            def ev_u(st, ps, b=b):
                o_ = self.u[0:rows, st, b * 512:(b + 1) * 512]
                S.op('act', lambda: A.activation(out=o_, in_=ps, func=AF.Gelu), reads=[ps], writes=[o_])
            self.b_type(blk, kcs, hT, 0, nst, rows, ev_u)
            self.wdone()
        self.ck('u')
        s1 = [self.stc(4) for _ in range(nst)]
        s2 = [self.stc(4) for _ in range(nst)]
        for b in range(4):
            blk, kcs = self.wnext('w_in', 16 + b)

            def ev_g(st, ps, b=b):
                o_ = self.gg[0:rows, st, b * 512:(b + 1) * 512]
                S.op('act', lambda: A.activation(out=o_, in_=ps, func=AF.Gelu, accum_out=s1[st][0:rows, b:b + 1]),
                     reads=[ps], writes=[o_, s1[st][:, b:b + 1]])
                jk = self.junk2[0:rows, b * 512:(b + 1) * 512]
                S.op('dve', lambda: V.scalar_tensor_tensor(out=jk, in0=o_, scalar=1.0, in1=o_, op0=ALU.mult, op1=ALU.mult, accum_out=s2[st][0:rows, b:b + 1]),
                     reads=[o_], writes=[s2[st][:, b:b + 1], jk])
            self.b_type(blk, kcs, hT, 0, nst, rows, ev_g)
            self.wdone()
        self.ck('sp')
        for gi_, dstT in ((0, self.saT), (1, self.sbT)):
            for b in range(4):
                blk, kcs = self.wnext('w_in', 20 + 4 * gi_ + b)

                def ev_s(oc, ps, b=b, dstT=dstT):
                    o_ = dstT[:, 4 * b + oc, 0:T]
                    S.op('act', lambda: A.activation(out=o_, in_=ps, func=AF.Sigmoid), reads=[ps], writes=[o_])
                self.a_type(blk, kcs, hT, 0, T, ev_s)
                self.wdone()
        self.ck('g')
        S.dma(self.rowA, d['gm_ln_w'].partition_broadcast(128), writes=[self.rowA])
        S.dma(self.rowB, d['gm_ln_b'].partition_broadcast(128), writes=[self.rowB])
        for st in range(nst):
            t1, t2, mean, msq, ve = self.stc(), self.stc(), self.stc(), self.stc(), self.stc()
            rs = self.stc()
            S.op('dve', lambda: V.reduce_sum(out=t1[0:rows], in_=s1[st][0:rows, :], axis=mybir.AxisListType.X), reads=[s1[st]], writes=[t1])
            S.op('dve', lambda: V.reduce_sum(out=t2[0:rows], in_=s2[st][0:rows, :], axis=mybir.AxisListType.X), reads=[s2[st]], writes=[t2])
            S.op('dve', lambda: V.tensor_scalar(out=mean[0:rows], in0=t1[0:rows], scalar1=1.0 / D, scalar2=None, op0=ALU.mult),
                 reads=[t1], writes=[mean])
            S.op('dve', lambda: V.tensor_tensor(out=msq[0:rows], in0=mean[0:rows], in1=mean[0:rows], op=ALU.mult), reads=[mean], writes=[msq])
            S.op('dve', lambda: V.tensor_scalar(out=t2[0:rows], in0=t2[0:rows], scalar1=1.0 / D, scalar2=1e-6, op0=ALU.mult, op1=ALU.add),
                 reads=[t2], writes=[t2])
            S.op('dve', lambda: V.tensor_tensor(out=ve[0:rows], in0=t2[0:rows], in1=msq[0:rows], op=ALU.subtract), reads=[t2, msq], writes=[ve])
            S.op('pool', lambda: G.tensor_tensor(out=rs[0:rows], in0=ve[0:rows], in1=self.cst[0:rows, 1:2], op=ALU.pow),
                 reads=[ve, self.cst[:, 1:2]], writes=[rs])
            gs = self.gg[0:rows, st, :]
            S.op('dve', lambda: V.tensor_scalar(out=gs, in0=gs, scalar1=mean[0:rows], scalar2=rs[0:rows], op0=ALU.subtract, op1=ALU.mult),
                 reads=[gs, mean, rs], writes=[gs])
            S.op('pool', lambda: G.tensor_tensor(out=gs, in0=gs, in1=self.rowA[0:rows, :], op=ALU.mult), reads=[gs, self.rowA], writes=[gs])
            vs = self.vn[0:rows, st, :]
            if smp:
                S.op('pool', lambda: G.tensor_tensor(out=gs, in0=gs, in1=self.rowB[0:rows, :], op=ALU.add), reads=[gs, self.rowB], writes=[gs])
                S.dma(d['g_smp'], gs, reads=[gs])
                S.op('dve', lambda: V.tensor_scalar(out=vs, in0=gs, scalar1=1.0, scalar2=None, op0=ALU.mult), reads=[gs], writes=[vs])
            else:
                S.op('pool', lambda: G.tensor_tensor(out=vs, in0=gs, in1=self.rowB[0:rows, :], op=ALU.add), reads=[gs, self.rowB], writes=[vs])
            self.ck('ln')
            for g2 in range(4):
                bk = self.bank()
                for j in range(2):
                    gi = g2 * 2 + j
                    o_ = bk[0:rows, j * 256:(j + 1) * 256]
                    l_ = self.wsT[0:rows, gi, 0:rows]
                    r_ = self.vn[0:rows, st, gi * 256:(gi + 1) * 256]
                    S.op('pe', lambda o_=o_, l_=l_, r_=r_: PE.matmul(o_, lhsT=l_, rhs=r_, start=True, stop=True),
                         reads=[l_, r_], writes=[o_], inc=(j == 1))
                for j in range(2):
                    gi = g2 * 2 + j
                    o_ = bk[0:rows, j * 256:(j + 1) * 256]
                    uu = self.u[0:rows, st, gi * 256:(gi + 1) * 256]
                    S.op('dve', lambda o_=o_, uu=uu, gi=gi: V.scalar_tensor_tensor(out=uu, in0=o_, scalar=self.bsT[0:rows, gi:gi + 1], in1=uu,
                                                                                  op0=ALU.add, op1=ALU.mult),
                         reads=[o_, uu, self.bsT], writes=[uu])
            self.transpose_tm(self.u[:, st, :], self.gmT, st, rows)
        self.ck('B')
        self.attention(kind, i, rows, T)
        self.ck('C')
        for b in range(4):
            blk, kcs = self.wnext('w_branch_attn', b)

            def ev_ba(oc, ps, b=b):
                o_ = self.m1[:, oc, 0:T]
                s_ = self.saT[:, 4 * b + oc, 0:T]
                S.op('dve', lambda: V.tensor_tensor(out=o_, in0=ps, in1=s_, op=ALU.mult), reads=[ps, s_], writes=[o_])
            self.a_type(blk, kcs, self.aoT, 0, T, ev_ba)
            self.wdone()
            blk, kcs = self.wnext('w_branch_gmlp', b)

            def ev_bg(oc, ps, b=b):
                t_ = self.tmp2[:, 0:T]
                s_ = self.sbT[:, 4 * b + oc, 0:T]
                S.op('dve', lambda: V.tensor_tensor(out=t_, in0=ps, in1=s_, op=ALU.mult), reads=[ps, s_], writes=[t_])
                o_ = self.mergedT[:, 4 * b + oc, 0:T]
                m_ = self.m1[:, oc, 0:T]
                S.op('pool', lambda: G.tensor_tensor(out=o_, in0=t_, in1=m_, op=ALU.add), reads=[t_, m_], writes=[o_])
            self.a_type(blk, kcs, self.gmT, 0, T, ev_bg)
            self.wdone()
        sq = [self.stc(4) for _ in range(nst)]
        for b in range(4):
            blk, kcs = self.wnext('w_out', b)

            def ev_o(st, ps, b=b):
                jk = self.junk[0:rows, 0:512]
                S.op('act', lambda: A.activation(out=jk, in_=ps, func=AF.Square, accum_out=sq[st][0:rows, b:b + 1]),
                     reads=[ps], writes=[sq[st][:, b:b + 1], jk])
                o_ = self.mo[0:rows, st, b * 512:(b + 1) * 512]
                S.op('dve', lambda: V.tensor_scalar(out=o_, in0=ps, scalar1=1.0, scalar2=None, op0=ALU.mult), reads=[ps], writes=[o_])
            self.b_type(blk, kcs, self.mergedT, 0, nst, rows, ev_o)
            self.wdone()
        S.dma(self.rowA, d['norm_mix_post'].partition_broadcast(128), writes=[self.rowA])
        for st in range(nst):
            tot = self.stc()
            S.op('dve', lambda: V.reduce_sum(out=tot[0:rows], in_=sq[st][0:rows, :], axis=mybir.AxisListType.X), reads=[sq[st]], writes=[tot])
            rs = self.rstd_from_ss(tot, D, 1e-6, rows)
            S.dma(self.xs[0:rows, :], xin[st], writes=[self.xs[0:rows, :]])
            ms = self.mo[0:rows, st, :]
            S.op('dve', lambda: V.scalar_tensor_tensor(out=ms, in0=ms, scalar=rs[0:rows], in1=self.rowA[0:rows, :], op0=ALU.mult, op1=ALU.mult),
                 reads=[ms, rs, self.rowA], writes=[ms])
            S.op('pool', lambda: G.tensor_tensor(out=ms, in0=ms, in1=self.xs[0:rows, :], op=ALU.add), reads=[ms, self.xs[0:rows, :]], writes=[ms])
        self.ck('D')
        for st in range(nst):
            self.norm_to_T(self.mo[:, st, :], self.hT, st, rows)
        for j in range(11):
            blk, kcs = self.wnext('w_ffn_gate', j)

            def ev_gate(oc, ps):
                o_ = self.sg[:, oc, 0:T]
                S.op('act', lambda: A.activation(out=o_, in_=ps, func=AF.Silu), reads=[ps], writes=[o_])
            self.a_type(blk, kcs, self.hT, 0, T, ev_gate)
            self.wdone()
            blk, kcs = self.wnext('w_ffn_up', j)

            def ev_up(oc, ps, j=j):
                o_ = self.f1T[:, 4 * j + oc, 0:T]
                s_ = self.sg[:, oc, 0:T]
                S.op('dve', lambda: V.tensor_tensor(out=o_, in0=ps, in1=s_, op=ALU.mult), reads=[ps, s_], writes=[o_])
            self.a_type(blk, kcs, self.hT, 0, T, ev_up)
            self.wdone()
        sq2 = [self.stc(4) for _ in range(nst)]
        for cb in range(4):
            accb = [self.banks[4 + (cb % 2) * 2 + st] for st in range(2)]
            for ki, k0 in enumerate((0, 16, 32)):
                blk, kcs = self.wnext('w_ffn_down', cb, k0)

                def ev_d(st, ps, cb=cb):
                    jk = self.junk[0:rows, 0:512]
                    S.op('act', lambda: A.activation(out=jk, in_=ps, func=AF.Square, accum_out=sq2[st][0:rows, cb:cb + 1]),
                         reads=[ps], writes=[sq2[st][:, cb:cb + 1], jk])
                    o_ = self.fo[0:rows, st, cb * 512:(cb + 1) * 512]
                    S.op('dve', lambda: V.tensor_scalar(out=o_, in0=ps, scalar1=1.0, scalar2=None, op0=ALU.mult), reads=[ps], writes=[o_])
                self.b_type(blk, kcs, self.f1T, k0, nst, rows, ev_d, banks=accb, first=(ki == 0), last=(ki == 2))
                self.wdone()
        S.dma(self.rowB, d['norm_ffn_post'].partition_broadcast(128), writes=[self.rowB])
        ydst = d['y_smp'] if smp else d['y_own'][i]
        for st in range(nst):
            tot = self.stc()
            S.op('dve', lambda: V.reduce_sum(out=tot[0:rows], in_=sq2[st][0:rows, :], axis=mybir.AxisListType.X), reads=[sq2[st]], writes=[tot])
            rs = self.rstd_from_ss(tot, D, 1e-6, rows)
            fs = self.fo[0:rows, st, :]
            S.op('dve', lambda: V.scalar_tensor_tensor(out=fs, in0=fs, scalar=rs[0:rows], in1=self.rowB[0:rows, :], op0=ALU.mult, op1=ALU.mult),
                 reads=[fs, rs, self.rowB], writes=[fs])
            S.op('pool', lambda: G.tensor_tensor(out=fs, in0=fs, in1=self.mo[0:rows, st, :], op=ALU.add),
                 reads=[fs, self.mo[0:rows, st, :]], writes=[fs])
            S.dma(ydst[st * 128:st * 128 + rows, :], fs, reads=[fs])

    def ingest_cache(self):
        S = self.S
        nc = self.nc
        d = self.d
        V, A, PE = nc.vector, nc.scalar, nc.tensor
        kb0 = self.NKP // 128
        for kb in range(PAST // 128):
            g = kb0 + kb
            S.dma(self.xs, d['cache_k'][kb * 128:(kb + 1) * 128, :], writes=[self.xs])
            S.op('act', lambda: A.copy(out=self.xn, in_=self.xs), reads=[self.xs], writes=[self.xn])
            self.transpose_tm(self.xn, self.hT, 0, 128)
            dst = d['KT'][:, :, g * 128:(g + 1) * 128].rearrange("m d t -> d m t")
            sv = self.hT[:, :, 0:128]
            S.dma(dst, sv, reads=[sv], writes=[('KT', g, g + 1)])
            S.dma(self.xs, d['cache_v'][kb * 128:(kb + 1) * 128, :], writes=[self.xs])
            S.op('dve', lambda: V.tensor_scalar(out=self.xn, in0=self.xs, scalar1=1.0, scalar2=None, op0=ALU.mult), reads=[self.xs], writes=[self.xn])
            dstv = d['V'][:, g * 128:(g + 1) * 128, :].rearrange("h t e -> t h e")
            S.dma(dstv, self.xn.rearrange("p (h e) -> p h e", h=NH), reads=[self.xn], writes=[('V', g, g + 1)])

    def attention(self, kind, i, rows, T):
        S = self.S
        nc = self.nc
        d = self.d
        V, G, A, PE = nc.vector, nc.gpsimd, nc.scalar, nc.tensor
        smp = (kind == 'smp')
        if smp:
            kb_list = [(self.NKP // 128 + kb, 128) for kb in range(PAST // 128)] + [(self.NKP // 128 + PAST // 128, NSMP)]
            nslot = 1
        else:
            kb_list = [(kb, 128) for kb in range(4 * i + 4)]
            nslot = 2
        nkb = len(kb_list)
        nq = T
        qn = min(128, nq)
        LA = 2
        acc = [[self.banks[4 + s * 2 + m] for m in range(2)] for s in range(2)]
        for i2 in range(2):
            ones = self.Vc[i2][:, :, 256:257]
            S.op('dve', lambda ones=ones: V.memset(ones, 1.0), writes=[ones])
        chunks = []
        items = []
        for h in range(NH):
            for c0 in range(0, nkb, CK):
                cn = min(CK, nkb - c0)
                ci = len(chunks)
                chunks.append((h, c0, cn))
                for j in range(cn):
                    items.append((h, ci, j, c0 + j))
        btvs = {}

        def load_bias(h):
            bt = self.btl[h % 2]
            if smp:
                btv = bt[:, :, :].rearrange("p a b -> p (a b)")[:, 0:9 * NSMP]
                S.dma(btv, d['c_sbias'][h], writes=[btv])
                btvs[h] = btv.rearrange("p (a b) -> p a b", a=9)
            else:
                S.dma(bt[:, :, :].rearrange("p a b -> p (a b)"), d['c_btile'][h], writes=[bt[:, :, :]])
                btvs[h] = bt

        def load_chunk(ci):
            h, c0, cn = chunks[ci]
            Kc = self.Kc[ci % 2]
            Vc = self.Vc[ci % 2]
            g0 = kb_list[c0][0]
            nkeys = sum(n for _, n in kb_list[c0:c0 + cn])
            ksrc = d['KT'][2 * h:2 * h + 2, :, g0 * 128:g0 * 128 + nkeys].rearrange("m d t -> d m t")
            S.dma(Kc[:, :, 0:nkeys], ksrc, reads=[('KT', g0, g0 + cn)], writes=[Kc[:, :, 0:nkeys]])
            nfull = nkeys // 128
            if nfull:
                vsrc = d['V'][h, g0 * 128:(g0 + nfull) * 128, :].rearrange("(b k) e -> k b e", k=128)
                S.dma(Vc[:, 0:nfull, 0:256], vsrc, reads=[('V', g0, g0 + nfull)], writes=[Vc[:, 0:nfull, 0:256]])
            if nkeys % 128:
                r_ = nkeys % 128
                vsrc = d['V'][h, (g0 + nfull) * 128:(g0 + nfull) * 128 + r_, :]
                S.dma(Vc[0:r_, nfull, 0:256], vsrc, reads=[('V', g0 + nfull, g0 + nfull + 1)], writes=[Vc[0:r_, nfull, 0:256]])

        sbank = {}

        def emit_qk(t):
            h, ci, j, kbi = items[t]
            Kc = self.Kc[ci % 2]
            nk = kb_list[kbi][1]
            bS = self.bank()
            sbank[t] = bS
            for m in range(2):
                o_ = bS[0:nk, m * 256:m * 256 + nq]
                l_ = Kc[:, m, j * 128:j * 128 + nk]
                r_ = self.qT[:, 2 * h + m, 0:nq]
                S.op('pe', lambda o_=o_, l_=l_, r_=r_: PE.matmul(o_, lhsT=l_, rhs=r_, start=True, stop=True),
                     reads=[l_, r_], writes=[o_], inc=(m == 1))

        def emit_rest(t):
            h, ci, j, kbi = items[t]
            Vc = self.Vc[ci % 2]
            nk = kb_list[kbi][1]
            bS = sbank.pop(t)
            sa = self.sadd[t % 2]
            pt = self.PT[t % 3]
            sin = bS[0:nk, :].rearrange("p (m q) -> p m q", m=2)[:, :, 0:nq]
            if smp:
                bsrc = btvs[h][0:nk, kbi, :]
                ccol = self.cst[0:nk, 3:4]
            else:
                mrel = 4 * i - kb_list[kbi][0]
                ty = 0 if mrel >= 1 else 1 - mrel
                bsrc = btvs[h][0:nk, ty, :]
                ccol = self.cc[0:nk, h * NM + mrel + 3:h * NM + mrel + 4]
            so = sa[0:nk, :, 0:nq]
            S.op('dve', lambda: V.tensor_tensor(out=so, in0=sin, in1=bsrc.unsqueeze(1).to_broadcast([nk, 2, nq]), op=ALU.add),
                 reads=[sin, bsrc], writes=[so])
            po = pt[0:nk, :, 0:nq]
            S.op('act', lambda: A.activation(out=po, in_=so, func=AF.Exp, bias=ccol, scale=1.0),
                 reads=[so, ccol], writes=[po])
            for m in range(2):
                for s in range(nslot):
                    o_ = acc[s][m][0:qn, 0:257]
                    l_ = pt[0:nk, m, s * 128:s * 128 + qn]
                    r_ = Vc[0:nk, j, 0:257]
                    S.op('pe', lambda o_=o_, l_=l_, r_=r_: PE.matmul(o_, lhsT=l_, rhs=r_, start=(kbi == 0), stop=(kbi == nkb - 1)),
                         reads=[l_, r_], writes=[o_], inc=(m == 1 and s == nslot - 1))

        def finalize(h):
            for s in range(nslot):
                r1, r2, nr2 = self.stc(), self.stc(), self.stc()
                a0_, a1_ = acc[s][0], acc[s][1]
                S.op('dve', lambda: V.reciprocal(out=r1[0:qn], in_=a0_[0:qn, 256:257]), reads=[a0_[0:qn, 256:257]], writes=[r1])
                S.op('dve', lambda: V.reciprocal(out=r2[0:qn], in_=a1_[0:qn, 256:257]), reads=[a1_[0:qn, 256:257]], writes=[r2])
                S.op('dve', lambda: V.tensor_tensor(out=nr2[0:qn], in0=r2[0:qn], in1=self.cst[0:qn, 2:3], op=ALU.mult),
                     reads=[r2, self.cst[:, 2:3]], writes=[nr2])
                o1 = self.of32[0][0:qn, :]
                o2 = self.of32[1][0:qn, :]
                S.op('dve', lambda: V.tensor_scalar(out=o1, in0=a0_[0:qn, 0:256], scalar1=r1[0:qn], scalar2=None, op0=ALU.mult),
                     reads=[a0_[0:qn, 0:256], r1], writes=[o1])
                S.op('dve', lambda: V.scalar_tensor_tensor(out=o2, in0=a1_[0:qn, 0:256], scalar=nr2[0:qn], in1=o1, op0=ALU.mult, op1=ALU.add),
                     reads=[a1_[0:qn, 0:256], nr2, o1], writes=[o2])
                ss = self.stc()
                S.op('dve', lambda: V.scalar_tensor_tensor(out=o1, in0=o2, scalar=1.0, in1=o2, op0=ALU.mult, op1=ALU.mult, accum_out=ss[0:qn]),
                     reads=[o2], writes=[o1, ss])
                rs = self.rstd_from_ss(ss, VD, 1e-5, qn)
                ao = self.ao_tm[0:qn, s, h * 256:(h + 1) * 256]
                S.op('dve', lambda: V.scalar_tensor_tensor(out=ao, in0=o2, scalar=rs[0:qn], in1=self.sublnrow[0:qn, :], op0=ALU.mult, op1=ALU.mult),
                     reads=[o2, rs, self.sublnrow], writes=[ao])

        load_bias(0)
        load_bias(1)
        n_it = len(items)
        last_item = {}
        for t_, it_ in enumerate(items):
            last_item[it_[1]] = t_
        loaded = set()
        st_ = {'nr': 0}

        def do_rest(tr):
            h, ci, j, kbi = items[tr]
            emit_rest(tr)
            if kbi == nkb - 1:
                finalize(h)
                if h + 2 < NH:
                    load_bias(h + 2)
            st_['nr'] = tr + 1

        def buffer_free(ci):
            return ci < 2 or st_['nr'] > last_item[ci - 2]

        def ensure_loaded(ci):
            if ci in loaded:
                return
            if ci >= 2:
                while st_['nr'] <= last_item[ci - 2]:
                    do_rest(st_['nr'])
            load_chunk(ci)
            loaded.add(ci)

        for t in range(n_it):
            ci = items[t][1]
            ensure_loaded(ci)
            emit_qk(t)
            while st_['nr'] <= t - LA:
                do_rest(st_['nr'])
            nxt = ci + 1
            if nxt < len(chunks) and nxt not in loaded and buffer_free(nxt):
                load_chunk(nxt)
                loaded.add(nxt)
        while st_['nr'] < n_it:
            do_rest(st_['nr'])
        for s in range(nslot):
            self.transpose_tm(self.ao_tm[:, s, :], self.aoT, s, qn)


_CACHE = {}


def _consts(p):
    slopes = 2.0 ** (-8.0 * np.arange(1, NH + 1, dtype=np.float64) / NH)
    k = np.arange(128)[:, None]
    q = np.arange(256)[None, :]
    qp = q + 128 * (q >= 128)
    bt = np.zeros((NH, 128, 5, 256), np.float64)
    for h in range(NH):
        bt[h, :, 0, :] = -slopes[h] * (qp - k)
        for ty in range(1, 5):
            mrel = 1 - ty
            c = -mrel
            kg = 2 * (c // 2) + (p if c % 2 == 0 else 1 - p)
            s_pos = 128 * kg + k
            t_pos = 128 * p + qp
            allowed = (s_pos // 64) <= (t_pos // 64)
            bias = -slopes[h] * np.abs(t_pos - s_pos)
            bt[h, :, ty, :] = np.where(allowed, bias, NEG)
    cc = np.zeros((NH, NM), np.float64)
    return slopes, bt, cc


def _consts_full(p):
    slopes, bt, cc = _consts(p)
    for h in range(NH):
        for m in range(1, NM - 3):
            dd = m if m % 2 == 0 else m + 2 * p
            cc[h, m + 3] = -slopes[h] * 128.0 * dd
    return slopes, bt.astype(np.float32), cc.astype(np.float32)


def _sbias():
    slopes = 2.0 ** (-8.0 * np.arange(1, NH + 1, dtype=np.float64) / NH)
    sb = np.full((NH, 128, 9, NSMP), NEG, np.float64)
    k = np.arange(128)[:, None]
    q = np.arange(NSMP)[None, :]
    for h in range(NH):
        for kb in range(8):
            sb[h, :, kb, :] = -slopes[h] * np.abs(PAST + q - (128 * kb + k))
        kk = np.arange(NSMP)[:, None]
        sb[h, 0:NSMP, 8, :] = -slopes[h] * np.abs(q - kk)
    return sb.astype(np.float32)


def kernel(**inputs):
    x_prompt = np.asarray(inputs['x_prompt'], np.float32)
    B, SEQ, _ = x_prompt.shape
    NT = SEQ // 512
    x_sample = np.asarray(inputs['x_sample'], np.float32)
    key = NT
    if key not in _CACHE:
        _CACHE[key] = Prog(NT).build()
    nc = _CACHE[key]
    ident = np.eye(128, dtype=np.float32)
    tril = np.tril(np.ones((128, 128), np.float32))
    sbias = _sbias().reshape(NH, 128, 9 * NSMP)
    shared = {}
    for nm in ('norm_mix_pre', 'norm_mix_post', 'norm_ffn_pre', 'norm_ffn_post', 'gm_ln_w', 'gm_ln_b',
               'lambda_q1', 'lambda_k1', 'lambda_q2', 'lambda_k2', 'subln_w'):
        shared[nm] = np.ascontiguousarray(np.asarray(inputs[nm], np.float32).reshape(1, -1))
    shared['gm_ws'] = np.ascontiguousarray(np.asarray(inputs['gm_ws'], np.float32)[0])
    shared['gm_bs'] = np.ascontiguousarray(np.asarray(inputs['gm_bs'], np.float32)[0])
    for nm in ('w_in', 'w_branch_attn', 'w_branch_gmlp', 'w_out', 'w_ffn_gate', 'w_ffn_up', 'w_ffn_down'):
        shared[nm] = np.ascontiguousarray(np.asarray(inputs[nm], np.float32)[0])
    shared['c_ident'] = ident
    shared['c_tril'] = tril
    shared['c_sbias'] = sbias
    cache_k = np.asarray(inputs['cache_k'], np.float32)[0]
    cache_v = np.asarray(inputs['cache_v'], np.float32)[0]
    in_maps = []
    for c in range(8):
        b, p = c // 2, c % 2
        xb = x_prompt[b].reshape(SEQ // 128, 128, D)
        own = np.ascontiguousarray(xb[p::2].reshape(NT, TT, D))
        oth = np.ascontiguousarray(xb[(1 - p)::2].reshape(NT, TT, D))
        _, bt, cc = _consts_full(p)
        m = dict(shared)
        m['x_own'] = own
        m['x_oth'] = oth
        m['x_smp'] = np.ascontiguousarray(x_sample[c])
        m['cache_k'] = np.ascontiguousarray(cache_k[c].reshape(PAST, D))
        m['cache_v'] = np.ascontiguousarray(cache_v[c].reshape(PAST, D))
        m['c_btile'] = np.ascontiguousarray(bt.reshape(NH, 128, 5 * 256))
        m['c_cc'] = np.ascontiguousarray(np.broadcast_to(cc.reshape(1, NH * NM), (128, NH * NM)))
        in_maps.append(m)
    res = run_bass_kernel_spmd(nc, in_maps, core_ids=list(range(8)))
    R = res.results
    y_prompt = np.empty((B, SEQ // 128, 128, D), np.float32)
    k_prompt = np.empty((B, SEQ // 128, 128, D), np.float32)
    v_prompt = np.empty((B, SEQ // 128, 128, D), np.float32)
    for c in range(8):
        b, p = c // 2, c % 2
        y_prompt[b, p::2] = R[c]['y_own'].reshape(-1, 128, D)
        k_prompt[b, p::2] = R[c]['k_own'].reshape(-1, 128, D)
        v_prompt[b, p::2] = R[c]['v_own'].reshape(-1, 128, D)
    y_sample = np.stack([R[c]['y_smp'] for c in range(8)])
    k_sample = np.stack([R[c]['k_smp'] for c in range(8)])
    v_sample = np.stack([R[c]['v_smp'] for c in range(8)])
    g_sample = np.stack([R[c]['g_smp'] for c in range(8)])
    return (y_prompt.reshape(B, SEQ, D), y_sample,
            k_prompt.reshape(1, B, SEQ, NH, 2, HD), v_prompt.reshape(1, B, SEQ, NH, VD),
            k_sample.reshape(1, 8, NSMP, NH, 2, HD), v_sample.reshape(1, 8, NSMP, NH, VD),
            g_sample.reshape(1, 8, NSMP, 8, 256))
```

```python
import numpy as np
from contextlib import ExitStack
import concourse.bass as bass
import concourse.mybir as mybir
from concourse.bass_utils import run_bass_kernel_spmd

F32 = mybir.dt.float32
BF16 = mybir.dt.bfloat16
U8 = mybir.dt.uint8
AF = mybir.ActivationFunctionType
ALU = mybir.AluOpType

D = 2048
NH = 8
HD = 128
VD = 256
DFF = 5632
INW = 14336
TT = 256
PAST = 1024
NSMP = 16
CK = 8
NM = 68
NEG = -30000.0
LAMBDA_INIT = 0.8 - 0.6 * 1.0
PAGE = 4096


def _dsz(dt):
    if dt == F32:
        return 4
    if dt == BF16:
        return 2
    if dt == U8:
        return 1
    raise ValueError(str(dt))


class Sync:
    LIMIT = 30000
    NDS = 14

    def __init__(self, nc, es):
        self.nc = nc
        self.es = es
        self.engs = {'pe': nc.tensor, 'act': nc.scalar, 'dve': nc.vector, 'pool': nc.gpsimd, 'sp': nc.sync}
        self.sems = {}
        self.owner = {}
        self.cur = {}
        self.seen = {e: {} for e in self.engs}
        self.W = {}
        self.R = {}
        self.pend = {e: ([], []) for e in self.engs}
        self.nalloc = 0
        for e in ('pe', 'act', 'dve', 'pool'):
            self._epoch(e)
        self.dsem = []
        for k in range(self.NDS):
            key = self._alloc("dq%d" % k, 'dma')
            self.dsem.append([key, 0])
        self.drr = 0
        self.nwait = 0
        self.nops = 0

    def _alloc(self, name, owner):
        h = self.es.enter_context(self.nc.semaphore(name))
        self.sems[name] = h
        self.owner[name] = owner
        self.nalloc += 1
        return name

    def _epoch(self, e):
        key = self._alloc("%s_e%d" % (e, self.nalloc), e)
        self.cur[e] = [key, 0]

    def reg(self, x):
        if isinstance(x, tuple):
            return [x]
        ap = x
        esz = _dsz(ap.dtype)
        a = ap.ap
        pstep = a[0][0]
        off = ap.offset - ap.start_partition() * pstep if pstep else ap.offset
        ext = 1
        for s, c in a[1:]:
            ext += (c - 1) * abs(s)
        lo = off * esz
        hi = (off + ext) * esz
        name = ap.name
        if name.startswith('ps'):
            return [((name, 0), 0, 2048)]
        out = []
        pg = lo // PAGE
        while pg * PAGE < hi:
            l = max(lo, pg * PAGE)
            h = min(hi, (pg + 1) * PAGE)
            out.append(((name, pg), l, h))
            pg += 1
        return out

    def regs(self, xs):
        out = []
        for x in xs:
            if x is None:
                continue
            out.extend(self.reg(x))
        return out

    def _deps(self, eng, r, w):
        evs = {}
        own = self.owner

        def add(k, v):
            if evs.get(k, 0) < v:
                evs[k] = v
        for (sp, lo, hi) in r:
            for e in self.W.get(sp, ()):
                if e[0] < hi and lo < e[1]:
                    add(e[2][0], e[2][1])
            if sp[0].startswith('ps'):
                for e in self.R.get(sp, ()):
                    for k, v in e[2].items():
                        if own[k] != eng:
                            add(k, v)
        for (sp, lo, hi) in w:
            for e in self.W.get(sp, ()):
                if e[0] < hi and lo < e[1]:
                    add(e[2][0], e[2][1])
            for e in self.R.get(sp, ()):
                if e[0] < hi and lo < e[1]:
                    for k, v in e[2].items():
                        add(k, v)
        if eng == 'pe':
            evs = {k: v for k, v in evs.items() if own[k] != 'pe'}
        return evs

    def _wait(self, eng, evs):
        seen = self.seen[eng]
        for k, v in evs.items():
            if seen.get(k, 0) >= v:
                continue
            self.engs[eng].wait_ge(self.sems[k], v)
            seen[k] = v
            self.nwait += 1

    def _commit(self, r, w, ev):
        k, v = ev
        for (sp, lo, hi) in w:
            wl = self.W.get(sp)
            if wl is None:
                wl = self.W[sp] = []
            else:
                wl[:] = [e for e in wl if not (lo <= e[0] and e[1] <= hi)]
            wl.append((lo, hi, ev))
            rl = self.R.get(sp)
            if rl:
                rl[:] = [e for e in rl if not (lo <= e[0] and e[1] <= hi)]
        for (sp, lo, hi) in r:
            rl = self.R.get(sp)
            if rl is None:
                rl = self.R[sp] = []
            for e in rl:
                if e[0] == lo and e[1] == hi:
                    if e[2].get(k, 0) < v:
                        e[2][k] = v
                    break
            else:
                rl.append((lo, hi, {k: v}))

    def op(self, eng, fn, reads=(), writes=(), inc=True):
        r = self.regs(reads)
        w = self.regs(writes)
        self._wait(eng, self._deps(eng, r, w))
        ins = fn()
        self.nops += 1
        pr, pw = self.pend[eng]
        if inc:
            c = self.cur[eng]
            c[1] += 1
            ins.then_inc(self.sems[c[0]], 1)
            ev = (c[0], c[1])
            if pr or pw:
                r = pr + r
                w = pw + w
                self.pend[eng] = ([], [])
            self._commit(r, w, ev)
            if c[1] >= self.LIMIT:
                self._epoch(eng)
        else:
            pr.extend(r)
            pw.extend(w)
        return ins

    def dma(self, out, in_, reads=(), writes=(), q='sp', **kw):
        r = self.regs(reads)
        w = self.regs(writes)
        evs = self._deps(q, r, w)
        slot = self.dsem[self.drr]
        self.drr = (self.drr + 1) % self.NDS
        if slot[1] > 0 and evs.get(slot[0], 0) < slot[1]:
            evs[slot[0]] = slot[1]
        self._wait(q, evs)
        ins = self.engs[q].dma_start(out=out, in_=in_, **kw)
        slot[1] += 16
        ins.then_inc(self.sems[slot[0]], 16)
        self._commit(r, w, (slot[0], slot[1]))
        self.nops += 1
        return ins

    def finish(self, q='sp'):
        evs = {s[0]: s[1] for s in self.dsem if s[1] > 0}
        self._wait(q, evs)


class Prog:
    def __init__(self, NT, with_sample=True):
        self.NT = NT
        self.NKP = NT * 4 * 128
        self.with_sample = with_sample
        self.es = ExitStack()
        self.nc = nc = bass.Bass("TRN2", target_bir_lowering=False)
        self.S = None

    def dram_in(self, name, shape, dt=F32):
        return self.nc.dram_tensor(name, list(shape), dt, kind="ExternalInput").ap()

    def dram_out(self, name, shape, dt=F32):
        return self.nc.dram_tensor(name, list(shape), dt, kind="ExternalOutput").ap()

    def dram_tmp(self, name, shape, dt):
        return self.nc.dram_tensor(name, list(shape), dt, kind="Internal").ap()

    def alloc(self, nbytes, align=64):
        self.top = (self.top + align - 1) // align * align
        o = self.top
        self.top += nbytes
        return o

    def view(self, off, dt, shape):
        n = 1
        for s in shape[1:]:
            n *= s
        v = self.arena[:, off:off + n * _dsz(dt)].bitcast(dt)
        if len(shape) == 3:
            v = v.rearrange("p (a b) -> p a b", a=shape[1])
        elif len(shape) == 4:
            v = v.rearrange("p (a b c) -> p a b c", a=shape[1], b=shape[2])
        return v

    def build(self):
        nc = self.nc
        es = self.es
        NT = self.NT
        NKT = self.NKP + PAST + 128
        self.NKT = NKT
        d = {}
        d['x_own'] = self.dram_in('x_own', [NT, TT, D])
        d['x_oth'] = self.dram_in('x_oth', [NT, TT, D])
        d['x_smp'] = self.dram_in('x_smp', [NSMP, D])
        d['cache_k'] = self.dram_in('cache_k', [PAST, D])
        d['cache_v'] = self.dram_in('cache_v', [PAST, D])
        for nm in ('norm_mix_pre', 'norm_mix_post', 'norm_ffn_pre', 'norm_ffn_post', 'gm_ln_w', 'gm_ln_b'):
            d[nm] = self.dram_in(nm, [1, D])
        for nm in ('lambda_q1', 'lambda_k1', 'lambda_q2', 'lambda_k2'):
            d[nm] = self.dram_in(nm, [1, HD])
        d['subln_w'] = self.dram_in('subln_w', [1, VD])
        d['gm_ws'] = self.dram_in('gm_ws', [8, 128, 128])
        d['gm_bs'] = self.dram_in('gm_bs', [8, 128])
        d['w_in'] = self.dram_in('w_in', [D, INW])
        d['w_branch_attn'] = self.dram_in('w_branch_attn', [D, D])
        d['w_branch_gmlp'] = self.dram_in('w_branch_gmlp', [D, D])
        d['w_out'] = self.dram_in('w_out', [D, D])
        d['w_ffn_gate'] = self.dram_in('w_ffn_gate', [D, DFF])
        d['w_ffn_up'] = self.dram_in('w_ffn_up', [D, DFF])
        d['w_ffn_down'] = self.dram_in('w_ffn_down', [DFF, D])
        d['c_ident'] = self.dram_in('c_ident', [128, 128])
        d['c_tril'] = self.dram_in('c_tril', [128, 128])
        d['c_btile'] = self.dram_in('c_btile', [NH, 128, 5 * 256])
        d['c_cc'] = self.dram_in('c_cc', [128, NH * NM])
        d['c_sbias'] = self.dram_in('c_sbias', [NH, 128, 9 * NSMP])
        d['y_own'] = self.dram_out('y_own', [NT, TT, D])
        d['k_own'] = self.dram_out('k_own', [NT, TT, D])
        d['v_own'] = self.dram_out('v_own', [NT, TT, D])
        d['y_smp'] = self.dram_out('y_smp', [NSMP, D])
        d['k_smp'] = self.dram_out('k_smp', [NSMP, D])
        d['v_smp'] = self.dram_out('v_smp', [NSMP, D])
        d['g_smp'] = self.dram_out('g_smp', [NSMP, D])
        self.blocks = self.block_list()
        NB = len(self.blocks)
        d['wsc'] = self.dram_tmp('wsc', [NB, 128, 16 * 512], BF16)
        d['KT'] = self.dram_tmp('KTs', [2 * NH, 128, NKT], BF16)
        d['V'] = self.dram_tmp('Vs', [NH, NKT, VD], BF16)
        self.d = d
        ARENA = 200704
        self.arena = es.enter_context(nc.sbuf_tensor("arena", [128, ARENA], U8))
        self.banks = [es.enter_context(nc.psum_tensor("ps%d" % i, [128, 512], F32)) for i in range(8)]
        self.S = Sync(nc, es)
        self.top = 0
        self.rrA = 0
        v = self.view
        self.ring = [v(self.alloc(16384), BF16, [128, 16, 512]) for _ in range(3)]
        self.ident = v(self.alloc(256), BF16, [128, 128])
        self.wsT = v(self.alloc(2048), BF16, [128, 8, 128])
        self.bsT = v(self.alloc(32), F32, [128, 8])
        self.gpre = v(self.alloc(64), F32, [128, 16])
        self.gffn = v(self.alloc(64), F32, [128, 16])
        self.cst = v(self.alloc(32), F32, [128, 8])
        self.stat = v(self.alloc(1024), F32, [128, 256])
        self.nstat = 0
        self.sublnrow = v(self.alloc(1024), F32, [128, 256])
        self.cc = v(self.alloc(NH * NM * 4), F32, [128, NH * NM])
        self.rowA = v(self.alloc(8192), F32, [128, D])
        self.rowB = v(self.alloc(8192), F32, [128, D])
        self.xs = v(self.alloc(8192), F32, [128, D])
        self.xn = v(self.alloc(4096), BF16, [128, D])
        self.junk = v(self.alloc(4096), BF16, [128, D])
        self.junk2 = v(self.alloc(4096), BF16, [128, D])
        pst = self.alloc(8192)
        self.kout = v(pst, F32, [128, 512])
        self.kbf = v(pst + 2048, BF16, [128, 512])
        self.kTst = v(pst + 3072, BF16, [128, 4, 256])
        self.vout = v(pst + 5120, F32, [128, 512])
        self.vbf = v(pst + 7168, BF16, [128, 512])
        oq = self.alloc(8192)
        self.qT = v(oq, BF16, [128, 16, TT])
        self.mergedT = self.qT
        Y = self.alloc(57344)
        self.hT = v(Y, BF16, [128, 16, TT])
        self.gg = v(Y + 8192, F32, [128, 2, D])
        self.fo = self.gg
        self.u = v(Y + 24576, BF16, [128, 2, D])
        self.vn = v(Y + 32768, BF16, [128, 2, D])
        self.saT = v(Y + 40960, BF16, [128, 16, TT])
        self.sbT = v(Y + 49152, BF16, [128, 16, TT])
        self.f1T = v(Y + 24576, BF16, [128, 44, TT])
        self.sg = v(Y + 24576 + 22528, F32, [128, 4, TT])
        a0 = Y
        self.Kc = [v(a0 + i * 8256, BF16, [128, 2, CK * 128]) for i in range(2)]
        self.Vc = [v(a0 + i * 8256 + 4096, BF16, [128, CK, 260]) for i in range(2)]
        a1 = a0 + 2 * 8256
        self.btl = [v(a1 + i * 5120, F32, [128, 5, 256]) for i in range(2)]
        a2 = a1 + 10240
        self.sadd = [v(a2 + i * 2048, F32, [128, 2, 256]) for i in range(2)]
        a3 = a2 + 4096
        self.PT = [v(a3 + i * 1024, BF16, [128, 2, 256]) for i in range(3)]
        a4 = a3 + 3072
        self.of32 = [v(a4 + i * 1024, F32, [128, 256]) for i in range(2)]
        assert a4 + 2048 <= Y + 40960
        self.cst_f = [v(Y + i * 16384, F32, [128, 8, 512]) for i in range(2)]
        self.cst_b = [v(Y + 32768 + i * 8192, BF16, [128, 8, 512]) for i in range(2)]
        self.gmT = v(self.alloc(8192), BF16, [128, 16, TT])
        R8 = self.alloc(16384)
        self.ao_tm = v(R8, BF16, [128, 2, D])
        self.aoT = v(R8 + 8192, BF16, [128, 16, TT])
        self.mo = v(R8, F32, [128, 2, D])
        self.m1 = v(self.alloc(4096), F32, [128, 4, TT])
        self.tmp2 = v(self.alloc(1024), F32, [128, TT])
        assert self.top <= ARENA, self.top

        import os
        self.stop = os.environ.get("PROG_STOP", "")
        self.prologue()
        if self.stop.startswith("pro"):
            self.S.finish()
            es.close()
            return nc
        self.wseq = self.weight_sequence()
        self.wpos = 0
        self.wload = 0
        for _ in range(3):
            self.issue_wload()
        try:
            for i in range(NT):
                self.tile('oth', i)
                self.ck('oth')
                self.tile('own', i)
                self.ck('own')
            if self.with_sample:
                self.tile('smp', 0)
            assert self.wpos == len(self.wseq), (self.wpos, len(self.wseq))
        except StopIteration:
            pass
        self.S.finish()
        es.close()
        return nc

    def ck(self, name):
        if self.stop == name:
            raise StopIteration()

    def block_list(self):
        bl = []
        for cb in range(INW // 512):
            bl.append(('w_in', cb, 0, 16, 'pre'))
        for cb in range(4):
            bl.append(('w_branch_attn', cb, 0, 16, None))
        for cb in range(4):
            bl.append(('w_branch_gmlp', cb, 0, 16, None))
        for cb in range(4):
            bl.append(('w_out', cb, 0, 16, None))
        for cb in range(11):
            bl.append(('w_ffn_gate', cb, 0, 16, 'ffn'))
        for cb in range(11):
            bl.append(('w_ffn_up', cb, 0, 16, 'ffn'))
        for cb in range(4):
            for (k0, ks) in ((0, 16), (16, 16), (32, 12)):
                bl.append(('w_ffn_down', cb, k0, ks, None))
        self.bidx = {(b[0], b[1], b[2]): i for i, b in enumerate(bl)}
        return bl

    def weight_sequence(self):
        seq = []
        bi = self.bidx

        def full():
            s = [bi[('w_in', cb, 0)] for cb in range(28)]
            for cb in range(4):
                s.append(bi[('w_branch_attn', cb, 0)])
                s.append(bi[('w_branch_gmlp', cb, 0)])
            s += [bi[('w_out', cb, 0)] for cb in range(4)]
            for cb in range(11):
                s.append(bi[('w_ffn_gate', cb, 0)])
                s.append(bi[('w_ffn_up', cb, 0)])
            for cb in range(4):
                for k0 in (0, 16, 32):
                    s.append(bi[('w_ffn_down', cb, k0)])
            return s
        for i in range(self.NT):
            seq += [bi[('w_in', cb, 0)] for cb in range(4, 12)]
            seq += full()
        if self.with_sample:
            seq += full()
        return seq

    def issue_wload(self):
        if self.wload >= len(self.wseq):
            return
        b = self.wseq[self.wload]
        buf = self.ring[self.wload % 3]
        kcs = self.blocks[b][3]
        src = self.d['wsc'][b].rearrange("p (k n) -> p k n", k=16)
        self.S.dma(buf[:, 0:kcs, :], src[:, 0:kcs, :], reads=[('wsc', b, b + 1)], writes=[buf[:, 0:kcs, :]])
        self.wload += 1

    def wnext(self, name, cb, k0=0):
        b = self.wseq[self.wpos]
        assert b == self.bidx[(name, cb, k0)], (self.wpos, b, name, cb, k0)
        buf = self.ring[self.wpos % 3]
        self.wpos += 1
        return buf, self.blocks[b][3]

    def wdone(self):
        self.issue_wload()

    def stc(self, n=1):
        c = self.nstat
        self.nstat = (self.nstat + n) % 240
        if self.nstat + 16 > 240:
            self.nstat = 0
        return self.stat[:, c:c + n]

    def bank(self):
        b = self.banks[self.rrA % 4]
        self.rrA += 1
        return b

    def prologue(self):
        S = self.S
        nc = self.nc
        d = self.d
        V, G, A = nc.vector, nc.gpsimd, nc.scalar
        S.op('dve', lambda: V.memset(self.cst[:, 0:1], 1.0), writes=[self.cst[:, 0:1]])
        S.op('dve', lambda: V.memset(self.cst[:, 1:2], -0.5), writes=[self.cst[:, 1:2]])
        S.op('dve', lambda: V.memset(self.cst[:, 3:4], 0.0), writes=[self.cst[:, 3:4]])
        for i in range(2):
            ones = self.Vc[i][:, :, 256:257]
            S.op('dve', lambda ones=ones: V.memset(ones, 1.0), writes=[ones])
        xsv = self.xs[:, 0:128]
        S.dma(xsv, d['c_ident'], writes=[xsv])
        S.op('dve', lambda: V.tensor_scalar(out=self.ident, in0=xsv, scalar1=1.0, scalar2=None, op0=ALU.mult), reads=[xsv], writes=[self.ident])
        S.dma(self.cc, d['c_cc'], writes=[self.cc])
        for (dstv, srcap, nr, coff) in ((self.bsT, d['gm_bs'], 8, 1280),
                                        (self.gpre, d['norm_mix_pre'].rearrange("o (k p) -> (o k) p", p=128), 16, 1408),
                                        (self.gffn, d['norm_ffn_pre'].rearrange("o (k p) -> (o k) p", p=128), 16, 1536)):
            stg = self.xs[0:nr, coff:coff + 128]
            S.dma(stg, srcap, writes=[stg])
            bk = self.bank()
            o_ = bk[:, 0:nr]
            S.op('pe', lambda o_=o_, stg=stg, nr=nr: nc.tensor.transpose(out=o_, in_=stg, identity=xsv[0:nr, 0:nr]),
                 reads=[stg, xsv], writes=[o_])
            S.op('dve', lambda o_=o_, dstv=dstv: V.tensor_scalar(out=dstv, in0=o_, scalar1=1.0, scalar2=None, op0=ALU.mult), reads=[o_], writes=[dstv])
        S.dma(self.sublnrow, d['subln_w'].partition_broadcast(128), writes=[self.sublnrow])
        S.op('dve', lambda: V.tensor_scalar(out=self.sublnrow, in0=self.sublnrow, scalar1=1.0 - LAMBDA_INIT,
                                            scalar2=None, op0=ALU.mult), reads=[self.sublnrow], writes=[self.sublnrow])
        lw = self.xs[:, 512:1024].rearrange("p (a b) -> p a b", a=4)
        for j, nm in enumerate(('lambda_q1', 'lambda_k1', 'lambda_q2', 'lambda_k2')):
            S.dma(lw[:, j, :], d[nm].partition_broadcast(128), writes=[lw[:, j, :]])
        s1 = self.stc()
        s2 = self.stc()
        jk = self.xs[:, 1024:1152]
        S.op('dve', lambda: V.scalar_tensor_tensor(out=jk, in0=lw[:, 0, :], scalar=1.0, in1=lw[:, 1, :], op0=ALU.mult, op1=ALU.mult, accum_out=s1),
             reads=[lw[:, 0, :], lw[:, 1, :]], writes=[jk, s1])
        S.op('dve', lambda: V.scalar_tensor_tensor(out=jk, in0=lw[:, 2, :], scalar=1.0, in1=lw[:, 3, :], op0=ALU.mult, op1=ALU.mult, accum_out=s2),
             reads=[lw[:, 2, :], lw[:, 3, :]], writes=[jk, s2])
        e1 = self.stc()
        e2 = self.stc()
        S.op('act', lambda: A.activation(out=e1, in_=s1, func=AF.Exp), reads=[s1], writes=[e1])
        S.op('act', lambda: A.activation(out=e2, in_=s2, func=AF.Exp), reads=[s2], writes=[e2])
        t = self.stc()
        S.op('dve', lambda: V.tensor_tensor(out=t, in0=e2, in1=e1, op=ALU.subtract), reads=[e1, e2], writes=[t])
        S.op('dve', lambda: V.tensor_scalar(out=self.cst[:, 2:3], in0=t, scalar1=-LAMBDA_INIT, scalar2=None, op0=ALU.add),
             reads=[t], writes=[self.cst[:, 2:3]])
        wsf = self.gg[:, 0, 0:1024].rearrange("p (g s) -> p g s", g=8)
        S.dma(wsf, d['gm_ws'].rearrange("g t s -> t g s"), writes=[wsf])
        trl = self.xs[:, 128:256]
        S.dma(trl, d['c_tril'], writes=[trl])
        wsb = self.gg[:, 1, 0:512].bitcast(BF16).rearrange("p (g s) -> p g s", g=8)
        S.op('dve', lambda: V.tensor_tensor(out=wsb, in0=wsf, in1=trl.unsqueeze(1).to_broadcast([128, 8, 128]), op=ALU.mult),
             reads=[wsf, trl], writes=[wsb])
        for g2 in range(2):
            bk = self.bank()
            bkb = bk[:, :].bitcast(BF16)
            for j in range(4):
                g = g2 * 4 + j
                S.op('pe', lambda g=g, j=j: nc.tensor.transpose(out=bkb[:, j * 128:(j + 1) * 128], in_=wsb[:, g, :], identity=self.ident),
                     reads=[wsb[:, g, :], self.ident], writes=[bkb[:, j * 128:(j + 1) * 128]], inc=(j == 3))
            S.op('dve', lambda g2=g2: V.tensor_scalar(out=self.wsT[:, g2 * 4:(g2 + 1) * 4, :],
                                                    in0=bkb[:, 0:512].rearrange("p (a b) -> p a b", a=4), scalar1=1.0, scalar2=None, op0=ALU.mult),
                 reads=[bkb[:, 0:512]], writes=[self.wsT[:, g2 * 4:(g2 + 1) * 4, :]])
        if self.stop == "pro1":
            return
        engs = ['act', 'dve']
        n = 0
        for b, (nm, cb, k0, kcs, gain) in enumerate(self.blocks):
            Wm = d[nm].rearrange("(k p) n -> p k n", p=128)
            dst = d['wsc'][b].rearrange("p (k n) -> p k n", k=16)
            for h0 in range(0, kcs, 8):
                hs = min(8, kcs - h0)
                sf = self.cst_f[n % 2]
                sb = self.cst_b[n % 2]
                S.dma(sf[:, 0:hs, :], Wm[:, k0 + h0:k0 + h0 + hs, cb * 512:(cb + 1) * 512], writes=[sf[:, 0:hs, :]])
                for kk in range(hs):
                    kc = k0 + h0 + kk
                    if gain == 'pre':
                        sc = self.gpre[:, kc:kc + 1]
                    elif gain == 'ffn':
                        sc = self.gffn[:, kc:kc + 1]
                    else:
                        sc = self.cst[:, 0:1]
                    e = engs[(n * 8 + kk) % 2]
                    o_, i_ = sb[:, kk, :], sf[:, kk, :]
                    if e == 'act':
                        S.op('act', lambda o_=o_, i_=i_, sc=sc: A.activation(out=o_, in_=i_, func=AF.Copy, scale=sc),
                             reads=[i_, sc], writes=[o_])
                    elif e == 'dve':
                        S.op('dve', lambda o_=o_, i_=i_, sc=sc: V.tensor_scalar(out=o_, in0=i_, scalar1=sc, scalar2=None, op0=ALU.mult),
                             reads=[i_, sc], writes=[o_])
                    else:
                        S.op('pool', lambda o_=o_, i_=i_, sc=sc: G.tensor_scalar(out=o_, in0=i_, scalar1=sc, scalar2=None, op0=ALU.mult),
                             reads=[i_, sc], writes=[o_])
                S.dma(dst[:, h0:h0 + hs, :], sb[:, 0:hs, :], reads=[sb[:, 0:hs, :]], writes=[('wsc', b, b + 1)])
                n += 1

    def rstd_from_ss(self, ss, n, eps, rows):
        S = self.S
        V, G = self.nc.vector, self.nc.gpsimd
        vv = self.stc()
        rs = self.stc()
        S.op('dve', lambda: V.tensor_scalar(out=vv[0:rows], in0=ss[0:rows], scalar1=1.0 / n, scalar2=eps, op0=ALU.mult, op1=ALU.add),
             reads=[ss], writes=[vv])
        S.op('pool', lambda: G.tensor_tensor(out=rs[0:rows], in0=vv[0:rows], in1=self.cst[0:rows, 1:2], op=ALU.pow),
             reads=[vv, self.cst[:, 1:2]], writes=[rs])
        return rs

    def transpose_tm(self, src, dstT, st, rows, evac_eng=('dve', 'act')):
        S = self.S
        nc = self.nc
        for cg in range(4):
            bk = self.bank()
            bkb = bk[:, :].bitcast(BF16)
            for j in range(4):
                c = cg * 4 + j
                i_ = src[0:rows, c * 128:(c + 1) * 128]
                o_ = bkb[:, j * 128:j * 128 + rows]
                S.op('pe', lambda i_=i_, o_=o_: nc.tensor.transpose(out=o_, in_=i_, identity=self.ident[0:rows, 0:rows]),
                     reads=[i_, self.ident], writes=[o_], inc=(j == 3))
            srcv = bkb[:, 0:512].rearrange("p (a b) -> p a b", a=4)[:, :, 0:rows]
            dstv = dstT[:, cg * 4:(cg + 1) * 4, st * 128:st * 128 + rows]
            e = evac_eng[cg % len(evac_eng)]
            if e == 'dve':
                S.op('dve', lambda srcv=srcv, dstv=dstv: nc.vector.tensor_scalar(out=dstv, in0=srcv, scalar1=1.0, scalar2=None, op0=ALU.mult), reads=[srcv], writes=[dstv])
            else:
                S.op('act', lambda srcv=srcv, dstv=dstv: nc.scalar.copy(out=dstv, in_=srcv), reads=[srcv], writes=[dstv])

    def norm_to_T(self, src_f32, dstT, st, rows):
        S = self.S
        nc = self.nc
        ss = self.stc()
        S.op('act', lambda: nc.scalar.activation(out=self.junk[0:rows, :], in_=src_f32[0:rows, :], func=AF.Square, accum_out=ss[0:rows]),
             reads=[src_f32[0:rows, :]], writes=[ss, self.junk[0:rows, :]])
        rs = self.rstd_from_ss(ss, D, 1e-6, rows)
        S.op('act', lambda: nc.scalar.activation(out=self.xn[0:rows, :], in_=src_f32[0:rows, :], func=AF.Copy, scale=rs[0:rows]),
             reads=[src_f32[0:rows, :], rs], writes=[self.xn[0:rows, :]])
        self.transpose_tm(self.xn, dstT, st, rows)

    def a_type(self, blk, kcs, srcT, kc0, T, evac):
        S = self.S
        nc = self.nc
        for oc in range(4):
            bk = self.bank()
            o_ = bk[:, 0:T]
            for kc in range(kcs):
                l_ = blk[:, kc, oc * 128:(oc + 1) * 128]
                r_ = srcT[:, kc0 + kc, 0:T]
                S.op('pe', lambda l_=l_, r_=r_, kc=kc: nc.tensor.matmul(o_, lhsT=l_, rhs=r_, start=(kc == 0), stop=(kc == kcs - 1)),
                     reads=[l_, r_], writes=[o_], inc=(kc == kcs - 1))
            evac(oc, o_)

    def b_type(self, blk, kcs, srcT, kc0, nst, rows, evac, banks=None, first=True, last=True):
        S = self.S
        nc = self.nc
        for st in range(nst):
            bk = banks[st] if banks else self.bank()
            o_ = bk[0:rows, :]
            for kc in range(kcs):
                l_ = srcT[:, kc0 + kc, st * 128:st * 128 + rows]
                r_ = blk[:, kc, :]
                S.op('pe', lambda l_=l_, r_=r_, kc=kc: nc.tensor.matmul(o_, lhsT=l_, rhs=r_, start=(first and kc == 0),
                                                                        stop=(last and kc == kcs - 1)),
                     reads=[l_, r_], writes=[o_], inc=(kc == kcs - 1))
            if last:
                evac(st, o_)

    def tile(self, kind, i):
        S = self.S
        nc = self.nc
        d = self.d
        V, G, A, PE = nc.vector, nc.gpsimd, nc.scalar, nc.tensor
        smp = (kind == 'smp')
        nst = 1 if smp else 2
        rows = NSMP if smp else 128
        T = NSMP if smp else TT
        full = kind in ('own', 'smp')
        if smp:
            xin = [d['x_smp']]
            gblk = [self.NKP // 128 + PAST // 128]
        else:
            xin = [d['x_' + kind][i, st * 128:(st + 1) * 128, :] for st in range(2)]
            gblk = [4 * i + 2 * st + (0 if kind == 'own' else 1) for st in range(2)]
        if smp:
            self.ingest_cache()
        for st in range(nst):
            S.dma(self.xs[0:rows, :], xin[st], writes=[self.xs[0:rows, :]])
            self.norm_to_T(self.xs, self.hT, st, rows)
        hT = self.hT
        if full:
            for b in range(4):
                blk, kcs = self.wnext('w_in', b)

                def ev_q(oc, ps, b=b):
                    o_ = self.qT[:, 4 * b + oc, 0:T]
                    S.op('act', lambda: A.activation(out=o_, in_=ps, func=AF.Copy, scale=float(HD) ** -0.5), reads=[ps], writes=[o_])
                self.a_type(blk, kcs, hT, 0, T, ev_q)
                self.wdone()
        self.ck('q')
        kdst = d['k_smp'] if smp else (d['k_own'][i] if kind == 'own' else None)
        vdst = d['v_smp'] if smp else (d['v_own'][i] if kind == 'own' else None)
        import os
        if os.environ.get("NO_KVOUT") == "1":
            kdst = vdst = None
        if os.environ.get("NO_KVOUT") == "k":
            vdst = None
        for b in range(4):
            blk, kcs = self.wnext('w_in', 4 + b)

            def ev_k(st, ps, b=b):
                if kdst is not None:
                    S.op('dve', lambda: V.tensor_scalar(out=self.kout[0:rows, :], in0=ps, scalar1=1.0, scalar2=None, op0=ALU.mult), reads=[ps], writes=[self.kout[0:rows, :]])
                    S.dma(kdst[st * 128:st * 128 + rows, b * 512:(b + 1) * 512], self.kout[0:rows, :], reads=[self.kout[0:rows, :]])
                S.op('act', lambda: A.copy(out=self.kbf[0:rows, :], in_=ps), reads=[ps], writes=[self.kbf[0:rows, :]])
                bk = self.bank()
                bkb = bk[:, :].bitcast(BF16)
                for j in range(4):
                    i_ = self.kbf[0:rows, j * 128:(j + 1) * 128]
                    o_ = bkb[:, j * 128:j * 128 + rows]
                    S.op('pe', lambda i_=i_, o_=o_: PE.transpose(out=o_, in_=i_, identity=self.ident[0:rows, 0:rows]),
                         reads=[i_, self.ident], writes=[o_], inc=(j == 3))
                sv = bkb[:, 0:512].rearrange("p (a b) -> p a b", a=4)[:, :, 0:rows]
                dv = self.kTst[:, :, st * 128:st * 128 + rows]
                S.op('dve', lambda: V.tensor_scalar(out=dv, in0=sv, scalar1=1.0, scalar2=None, op0=ALU.mult), reads=[sv], writes=[dv])
                g = gblk[st]
                dst = d['KT'][4 * b:4 * b + 4, :, g * 128:g * 128 + rows].rearrange("m d t -> d m t")
                S.dma(dst, dv, reads=[dv], writes=[('KT', g, g + 1)])
            self.b_type(blk, kcs, hT, 0, nst, rows, ev_k)
            self.wdone()
        for b in range(4):
            blk, kcs = self.wnext('w_in', 8 + b)

            def ev_v(st, ps, b=b):
                if vdst is not None:
                    S.op('dve', lambda: V.tensor_scalar(out=self.vout[0:rows, :], in0=ps, scalar1=1.0, scalar2=None, op0=ALU.mult), reads=[ps], writes=[self.vout[0:rows, :]])
                    S.dma(vdst[st * 128:st * 128 + rows, b * 512:(b + 1) * 512], self.vout[0:rows, :], reads=[self.vout[0:rows, :]])
                S.op('act', lambda: A.copy(out=self.vbf[0:rows, :], in_=ps), reads=[ps], writes=[self.vbf[0:rows, :]])
                g = gblk[st]
                dst = d['V'][2 * b:2 * b + 2, g * 128:g * 128 + rows, :].rearrange("h t e -> t h e")
                S.dma(dst, self.vbf[0:rows, :].rearrange("p (h e) -> p h e", h=2), reads=[self.vbf[0:rows, :]],
                      writes=[('V', g, g + 1)])
            self.b_type(blk, kcs, hT, 0, nst, rows, ev_v)
            self.wdone()
        if not full:
            return
        self.ck('kv')
        for b in range(4):
            blk, kcs = self.wnext('w_in', 12 + b)

            def ev_u(st, ps, b=b):
                o_ = self.u[0:rows, st, b * 512:(b + 1) * 512]
                S.op('act', lambda: A.activation(out=o_, in_=ps, func=AF.Gelu), reads=[ps], writes=[o_])
            self.b_type(blk, kcs, hT, 0, nst, rows, ev_u)
            self.wdone()
        self.ck('u')
        s1 = [self.stc(4) for _ in range(nst)]
        s2 = [self.stc(4) for _ in range(nst)]
        for b in range(4):
            blk, kcs = self.wnext('w_in', 16 + b)

            def ev_g(st, ps, b=b):
                o_ = self.gg[0:rows, st, b * 512:(b + 1) * 512]
                S.op('act', lambda: A.activation(out=o_, in_=ps, func=AF.Gelu, accum_out=s1[st][0:rows, b:b + 1]),
                     reads=[ps], writes=[o_, s1[st][:, b:b + 1]])
                jk = self.junk2[0:rows, b * 512:(b + 1) * 512]
                S.op('dve', lambda: V.scalar_tensor_tensor(out=jk, in0=o_, scalar=1.0, in1=o_, op0=ALU.mult, op1=ALU.mult, accum_out=s2[st][0:rows, b:b + 1]),
                     reads=[o_], writes=[s2[st][:, b:b + 1], jk])
            self.b_type(blk, kcs, hT, 0, nst, rows, ev_g)
            self.wdone()
        self.ck('sp')
        for gi_, dstT in ((0, self.saT), (1, self.sbT)):
            for b in range(4):
                blk, kcs = self.wnext('w_in', 20 + 4 * gi_ + b)

                def ev_s(oc, ps, b=b, dstT=dstT):
                    o_ = dstT[:, 4 * b + oc, 0:T]
                    S.op('act', lambda: A.activation(out=o_, in_=ps, func=AF.Sigmoid), reads=[ps], writes=[o_])
                self.a_type(blk, kcs, hT, 0, T, ev_s)
                self.wdone()
        self.ck('g')
        S.dma(self.rowA, d['gm_ln_w'].partition_broadcast(128), writes=[self.rowA])
        S.dma(self.rowB, d['gm_ln_b'].partition_broadcast(128), writes=[self.rowB])
        for st in range(nst):
            t1, t2, mean, msq, ve = self.stc(), self.stc(), self.stc(), self.stc(), self.stc()
            rs = self.stc()
            S.op('dve', lambda: V.reduce_sum(out=t1[0:rows], in_=s1[st][0:rows, :], axis=mybir.AxisListType.X), reads=[s1[st]], writes=[t1])
            S.op('dve', lambda: V.reduce_sum(out=t2[0:rows], in_=s2[st][0:rows, :], axis=mybir.AxisListType.X), reads=[s2[st]], writes=[t2])
            S.op('dve', lambda: V.tensor_scalar(out=mean[0:rows], in0=t1[0:rows], scalar1=1.0 / D, scalar2=None, op0=ALU.mult),
                 reads=[t1], writes=[mean])
            S.op('dve', lambda: V.tensor_tensor(out=msq[0:rows], in0=mean[0:rows], in1=mean[0:rows], op=ALU.mult), reads=[mean], writes=[msq])
            S.op('dve', lambda: V.tensor_scalar(out=t2[0:rows], in0=t2[0:rows], scalar1=1.0 / D, scalar2=1e-6, op0=ALU.mult, op1=ALU.add),
                 reads=[t2], writes=[t2])
            S.op('dve', lambda: V.tensor_tensor(out=ve[0:rows], in0=t2[0:rows], in1=msq[0:rows], op=ALU.subtract), reads=[t2, msq], writes=[ve])
            S.op('pool', lambda: G.tensor_tensor(out=rs[0:rows], in0=ve[0:rows], in1=self.cst[0:rows, 1:2], op=ALU.pow),
                 reads=[ve, self.cst[:, 1:2]], writes=[rs])
            gs = self.gg[0:rows, st, :]
            S.op('dve', lambda: V.tensor_scalar(out=gs, in0=gs, scalar1=mean[0:rows], scalar2=rs[0:rows], op0=ALU.subtract, op1=ALU.mult),
                 reads=[gs, mean, rs], writes=[gs])
            S.op('dve', lambda: V.tensor_tensor(out=gs, in0=gs, in1=self.rowA[0:rows, :], op=ALU.mult), reads=[gs, self.rowA], writes=[gs])
            vs = self.vn[0:rows, st, :]
            if smp:
                S.op('dve', lambda: V.tensor_tensor(out=gs, in0=gs, in1=self.rowB[0:rows, :], op=ALU.add), reads=[gs, self.rowB], writes=[gs])
                S.dma(d['g_smp'], gs, reads=[gs])
                S.op('dve', lambda: V.tensor_scalar(out=vs, in0=gs, scalar1=1.0, scalar2=None, op0=ALU.mult), reads=[gs], writes=[vs])
            else:
                S.op('dve', lambda: V.tensor_tensor(out=vs, in0=gs, in1=self.rowB[0:rows, :], op=ALU.add), reads=[gs, self.rowB], writes=[vs])
            self.ck('ln')
            for g2 in range(4):
                bk = self.bank()
                for j in range(2):
                    gi = g2 * 2 + j
                    o_ = bk[0:rows, j * 256:(j + 1) * 256]
                    l_ = self.wsT[0:rows, gi, 0:rows]
                    r_ = self.vn[0:rows, st, gi * 256:(gi + 1) * 256]
                    S.op('pe', lambda o_=o_, l_=l_, r_=r_: PE.matmul(o_, lhsT=l_, rhs=r_, start=True, stop=True),
                         reads=[l_, r_], writes=[o_], inc=(j == 1))
                for j in range(2):
                    gi = g2 * 2 + j
                    o_ = bk[0:rows, j * 256:(j + 1) * 256]
                    uu = self.u[0:rows, st, gi * 256:(gi + 1) * 256]
                    S.op('dve', lambda o_=o_, uu=uu, gi=gi: V.scalar_tensor_tensor(out=uu, in0=o_, scalar=self.bsT[0:rows, gi:gi + 1], in1=uu,
                                                                                  op0=ALU.add, op1=ALU.mult),
                         reads=[o_, uu, self.bsT], writes=[uu])
            self.transpose_tm(self.u[:, st, :], self.gmT, st, rows)
        self.ck('B')
        self.attention(kind, i, rows, T)
        self.ck('C')
        for b in range(4):
            blk, kcs = self.wnext('w_branch_attn', b)

            def ev_ba(oc, ps, b=b):
                o_ = self.m1[:, oc, 0:T]
                s_ = self.saT[:, 4 * b + oc, 0:T]
                S.op('dve', lambda: V.tensor_tensor(out=o_, in0=ps, in1=s_, op=ALU.mult), reads=[ps, s_], writes=[o_])
            self.a_type(blk, kcs, self.aoT, 0, T, ev_ba)
            self.wdone()
            blk, kcs = self.wnext('w_branch_gmlp', b)

            def ev_bg(oc, ps, b=b):
                t_ = self.tmp2[:, 0:T]
                s_ = self.sbT[:, 4 * b + oc, 0:T]
                S.op('dve', lambda: V.tensor_tensor(out=t_, in0=ps, in1=s_, op=ALU.mult), reads=[ps, s_], writes=[t_])
                o_ = self.mergedT[:, 4 * b + oc, 0:T]
                m_ = self.m1[:, oc, 0:T]
                S.op('dve', lambda: V.tensor_tensor(out=o_, in0=t_, in1=m_, op=ALU.add), reads=[t_, m_], writes=[o_])
            self.a_type(blk, kcs, self.gmT, 0, T, ev_bg)
            self.wdone()
        sq = [self.stc(4) for _ in range(nst)]
        for b in range(4):
            blk, kcs = self.wnext('w_out', b)

            def ev_o(st, ps, b=b):
                jk = self.junk[0:rows, 0:512]
                S.op('act', lambda: A.activation(out=jk, in_=ps, func=AF.Square, accum_out=sq[st][0:rows, b:b + 1]),
                     reads=[ps], writes=[sq[st][:, b:b + 1], jk])
                o_ = self.mo[0:rows, st, b * 512:(b + 1) * 512]
                S.op('dve', lambda: V.tensor_scalar(out=o_, in0=ps, scalar1=1.0, scalar2=None, op0=ALU.mult), reads=[ps], writes=[o_])
            self.b_type(blk, kcs, self.mergedT, 0, nst, rows, ev_o)
            self.wdone()
        S.dma(self.rowA, d['norm_mix_post'].partition_broadcast(128), writes=[self.rowA])
        for st in range(nst):
            tot = self.stc()
            S.op('dve', lambda: V.reduce_sum(out=tot[0:rows], in_=sq[st][0:rows, :], axis=mybir.AxisListType.X), reads=[sq[st]], writes=[tot])
            rs = self.rstd_from_ss(tot, D, 1e-6, rows)
            S.dma(self.xs[0:rows, :], xin[st], writes=[self.xs[0:rows, :]])
            ms = self.mo[0:rows, st, :]
            S.op('dve', lambda: V.scalar_tensor_tensor(out=ms, in0=ms, scalar=rs[0:rows], in1=self.rowA[0:rows, :], op0=ALU.mult, op1=ALU.mult),
                 reads=[ms, rs, self.rowA], writes=[ms])
            S.op('dve', lambda: V.tensor_tensor(out=ms, in0=ms, in1=self.xs[0:rows, :], op=ALU.add), reads=[ms, self.xs[0:rows, :]], writes=[ms])
        self.ck('D')
        for st in range(nst):
            self.norm_to_T(self.mo[:, st, :], self.hT, st, rows)
        for j in range(11):
            blk, kcs = self.wnext('w_ffn_gate', j)

            def ev_gate(oc, ps):
                o_ = self.sg[:, oc, 0:T]
                S.op('act', lambda: A.activation(out=o_, in_=ps, func=AF.Silu), reads=[ps], writes=[o_])
            self.a_type(blk, kcs, self.hT, 0, T, ev_gate)
            self.wdone()
            blk, kcs = self.wnext('w_ffn_up', j)

            def ev_up(oc, ps, j=j):
                o_ = self.f1T[:, 4 * j + oc, 0:T]
                s_ = self.sg[:, oc, 0:T]
                S.op('dve', lambda: V.tensor_tensor(out=o_, in0=ps, in1=s_, op=ALU.mult), reads=[ps, s_], writes=[o_])
            self.a_type(blk, kcs, self.hT, 0, T, ev_up)
            self.wdone()
        sq2 = [self.stc(4) for _ in range(nst)]
        for cb in range(4):
            accb = [self.banks[4 + (cb % 2) * 2 + st] for st in range(2)]
            for ki, k0 in enumerate((0, 16, 32)):
                blk, kcs = self.wnext('w_ffn_down', cb, k0)

                def ev_d(st, ps, cb=cb):
                    jk = self.junk[0:rows, 0:512]
                    S.op('act', lambda: A.activation(out=jk, in_=ps, func=AF.Square, accum_out=sq2[st][0:rows, cb:cb + 1]),
                         reads=[ps], writes=[sq2[st][:, cb:cb + 1], jk])
                    o_ = self.fo[0:rows, st, cb * 512:(cb + 1) * 512]
                    S.op('dve', lambda: V.tensor_scalar(out=o_, in0=ps, scalar1=1.0, scalar2=None, op0=ALU.mult), reads=[ps], writes=[o_])
                self.b_type(blk, kcs, self.f1T, k0, nst, rows, ev_d, banks=accb, first=(ki == 0), last=(ki == 2))
                self.wdone()
        S.dma(self.rowB, d['norm_ffn_post'].partition_broadcast(128), writes=[self.rowB])
        ydst = d['y_smp'] if smp else d['y_own'][i]
        for st in range(nst):
            tot = self.stc()
            S.op('dve', lambda: V.reduce_sum(out=tot[0:rows], in_=sq2[st][0:rows, :], axis=mybir.AxisListType.X), reads=[sq2[st]], writes=[tot])
            rs = self.rstd_from_ss(tot, D, 1e-6, rows)
            fs = self.fo[0:rows, st, :]
            S.op('dve', lambda: V.scalar_tensor_tensor(out=fs, in0=fs, scalar=rs[0:rows], in1=self.rowB[0:rows, :], op0=ALU.mult, op1=ALU.mult),
                 reads=[fs, rs, self.rowB], writes=[fs])
            S.op('dve', lambda: V.tensor_tensor(out=fs, in0=fs, in1=self.mo[0:rows, st, :], op=ALU.add),
                 reads=[fs, self.mo[0:rows, st, :]], writes=[fs])
            S.dma(ydst[st * 128:st * 128 + rows, :], fs, reads=[fs])

    def ingest_cache(self):
        S = self.S
        nc = self.nc
        d = self.d
        V, A, PE = nc.vector, nc.scalar, nc.tensor
        kb0 = self.NKP // 128
        for kb in range(PAST // 128):
            g = kb0 + kb
            S.dma(self.xs, d['cache_k'][kb * 128:(kb + 1) * 128, :], writes=[self.xs])
            S.op('act', lambda: A.copy(out=self.xn, in_=self.xs), reads=[self.xs], writes=[self.xn])
            self.transpose_tm(self.xn, self.hT, 0, 128)
            dst = d['KT'][:, :, g * 128:(g + 1) * 128].rearrange("m d t -> d m t")
            sv = self.hT[:, :, 0:128]
            S.dma(dst, sv, reads=[sv], writes=[('KT', g, g + 1)])
            S.dma(self.xs, d['cache_v'][kb * 128:(kb + 1) * 128, :], writes=[self.xs])
            S.op('dve', lambda: V.tensor_scalar(out=self.xn, in0=self.xs, scalar1=1.0, scalar2=None, op0=ALU.mult), reads=[self.xs], writes=[self.xn])
            dstv = d['V'][:, g * 128:(g + 1) * 128, :].rearrange("h t e -> t h e")
            S.dma(dstv, self.xn.rearrange("p (h e) -> p h e", h=NH), reads=[self.xn], writes=[('V', g, g + 1)])

    def attention(self, kind, i, rows, T):
        S = self.S
        nc = self.nc
        d = self.d
        V, G, A, PE = nc.vector, nc.gpsimd, nc.scalar, nc.tensor
        smp = (kind == 'smp')
        if smp:
            kb_list = [(self.NKP // 128 + kb, 128) for kb in range(PAST // 128)] + [(self.NKP // 128 + PAST // 128, NSMP)]
            nslot = 1
        else:
            kb_list = [(kb, 128) for kb in range(4 * i + 4)]
            nslot = 2
        nkb = len(kb_list)
        nq = T
        qn = min(128, nq)
        LA = 2
        acc = [[self.banks[4 + s * 2 + m] for m in range(2)] for s in range(2)]
        for i2 in range(2):
            ones = self.Vc[i2][:, :, 256:257]
            S.op('dve', lambda ones=ones: V.memset(ones, 1.0), writes=[ones])
        chunks = []
        items = []
        for h in range(NH):
            for c0 in range(0, nkb, CK):
                cn = min(CK, nkb - c0)
                ci = len(chunks)
                chunks.append((h, c0, cn))
                for j in range(cn):
                    items.append((h, ci, j, c0 + j))
        btvs = {}

        def load_bias(h):
            bt = self.btl[h % 2]
            if smp:
                btv = bt[:, :, :].rearrange("p a b -> p (a b)")[:, 0:9 * NSMP]
                S.dma(btv, d['c_sbias'][h], writes=[btv])
                btvs[h] = btv.rearrange("p (a b) -> p a b", a=9)
            else:
                S.dma(bt[:, :, :].rearrange("p a b -> p (a b)"), d['c_btile'][h], writes=[bt[:, :, :]])
                btvs[h] = bt

        def load_chunk(ci):
            h, c0, cn = chunks[ci]
            Kc = self.Kc[ci % 2]
            Vc = self.Vc[ci % 2]
            g0 = kb_list[c0][0]
            nkeys = sum(n for _, n in kb_list[c0:c0 + cn])
            ksrc = d['KT'][2 * h:2 * h + 2, :, g0 * 128:g0 * 128 + nkeys].rearrange("m d t -> d m t")
            S.dma(Kc[:, :, 0:nkeys], ksrc, reads=[('KT', g0, g0 + cn)], writes=[Kc[:, :, 0:nkeys]])
            nfull = nkeys // 128
            if nfull:
                vsrc = d['V'][h, g0 * 128:(g0 + nfull) * 128, :].rearrange("(b k) e -> k b e", k=128)
                S.dma(Vc[:, 0:nfull, 0:256], vsrc, reads=[('V', g0, g0 + nfull)], writes=[Vc[:, 0:nfull, 0:256]])
            if nkeys % 128:
                r_ = nkeys % 128
                vsrc = d['V'][h, (g0 + nfull) * 128:(g0 + nfull) * 128 + r_, :]
                S.dma(Vc[0:r_, nfull, 0:256], vsrc, reads=[('V', g0 + nfull, g0 + nfull + 1)], writes=[Vc[0:r_, nfull, 0:256]])

        sbank = {}

        def emit_qk(t):
            h, ci, j, kbi = items[t]
            Kc = self.Kc[ci % 2]
            nk = kb_list[kbi][1]
            bS = self.bank()
            sbank[t] = bS
            for m in range(2):
                o_ = bS[0:nk, m * 256:m * 256 + nq]
                l_ = Kc[:, m, j * 128:j * 128 + nk]
                r_ = self.qT[:, 2 * h + m, 0:nq]
                S.op('pe', lambda o_=o_, l_=l_, r_=r_: PE.matmul(o_, lhsT=l_, rhs=r_, start=True, stop=True),
                     reads=[l_, r_], writes=[o_], inc=(m == 1))

        def emit_rest(t):
            h, ci, j, kbi = items[t]
            Vc = self.Vc[ci % 2]
            nk = kb_list[kbi][1]
            bS = sbank.pop(t)
            sa = self.sadd[t % 2]
            pt = self.PT[t % 3]
            sin = bS[0:nk, :].rearrange("p (m q) -> p m q", m=2)[:, :, 0:nq]
            if smp:
                bsrc = btvs[h][0:nk, kbi, :]
                ccol = self.cst[0:nk, 3:4]
            else:
                mrel = 4 * i - kb_list[kbi][0]
                ty = 0 if mrel >= 1 else 1 - mrel
                bsrc = btvs[h][0:nk, ty, :]
                ccol = self.cc[0:nk, h * NM + mrel + 3:h * NM + mrel + 4]
            so = sa[0:nk, :, 0:nq]
            S.op('dve', lambda: V.tensor_tensor(out=so, in0=sin, in1=bsrc.unsqueeze(1).to_broadcast([nk, 2, nq]), op=ALU.add),
                 reads=[sin, bsrc], writes=[so])
            po = pt[0:nk, :, 0:nq]
            S.op('act', lambda: A.activation(out=po, in_=so, func=AF.Exp, bias=ccol, scale=1.0),
                 reads=[so, ccol], writes=[po])
            for m in range(2):
                for s in range(nslot):
                    o_ = acc[s][m][0:qn, 0:257]
                    l_ = pt[0:nk, m, s * 128:s * 128 + qn]
                    r_ = Vc[0:nk, j, 0:257]
                    S.op('pe', lambda o_=o_, l_=l_, r_=r_: PE.matmul(o_, lhsT=l_, rhs=r_, start=(kbi == 0), stop=(kbi == nkb - 1)),
                         reads=[l_, r_], writes=[o_], inc=(m == 1 and s == nslot - 1))

        def finalize(h):
            for s in range(nslot):
                r1, r2, nr2 = self.stc(), self.stc(), self.stc()
                a0_, a1_ = acc[s][0], acc[s][1]
                S.op('dve', lambda: V.reciprocal(out=r1[0:qn], in_=a0_[0:qn, 256:257]), reads=[a0_[0:qn, 256:257]], writes=[r1])
                S.op('dve', lambda: V.reciprocal(out=r2[0:qn], in_=a1_[0:qn, 256:257]), reads=[a1_[0:qn, 256:257]], writes=[r2])
                S.op('dve', lambda: V.tensor_tensor(out=nr2[0:qn], in0=r2[0:qn], in1=self.cst[0:qn, 2:3], op=ALU.mult),
                     reads=[r2, self.cst[:, 2:3]], writes=[nr2])
                o1 = self.of32[0][0:qn, :]
                o2 = self.of32[1][0:qn, :]
                S.op('dve', lambda: V.tensor_scalar(out=o1, in0=a0_[0:qn, 0:256], scalar1=r1[0:qn], scalar2=None, op0=ALU.mult),
                     reads=[a0_[0:qn, 0:256], r1], writes=[o1])
                S.op('dve', lambda: V.scalar_tensor_tensor(out=o2, in0=a1_[0:qn, 0:256], scalar=nr2[0:qn], in1=o1, op0=ALU.mult, op1=ALU.add),
                     reads=[a1_[0:qn, 0:256], nr2, o1], writes=[o2])
                ss = self.stc()
                S.op('dve', lambda: V.scalar_tensor_tensor(out=o1, in0=o2, scalar=1.0, in1=o2, op0=ALU.mult, op1=ALU.mult, accum_out=ss[0:qn]),
                     reads=[o2], writes=[o1, ss])
                rs = self.rstd_from_ss(ss, VD, 1e-5, qn)
                ao = self.ao_tm[0:qn, s, h * 256:(h + 1) * 256]
                S.op('dve', lambda: V.scalar_tensor_tensor(out=ao, in0=o2, scalar=rs[0:qn], in1=self.sublnrow[0:qn, :], op0=ALU.mult, op1=ALU.mult),
                     reads=[o2, rs, self.sublnrow], writes=[ao])

        load_bias(0)
        load_bias(1)
        n_it = len(items)
        last_item = {}
        for t_, it_ in enumerate(items):
            last_item[it_[1]] = t_
        loaded = set()
        st_ = {'nr': 0}

        def do_rest(tr):
            h, ci, j, kbi = items[tr]
            emit_rest(tr)
            if kbi == nkb - 1:
                finalize(h)
                if h + 2 < NH:
                    load_bias(h + 2)
            st_['nr'] = tr + 1

        def buffer_free(ci):
            return ci < 2 or st_['nr'] > last_item[ci - 2]

        def ensure_loaded(ci):
            if ci in loaded:
                return
            if ci >= 2:
                while st_['nr'] <= last_item[ci - 2]:
                    do_rest(st_['nr'])
            load_chunk(ci)
            loaded.add(ci)

        for t in range(n_it):
            ci = items[t][1]
            ensure_loaded(ci)
            emit_qk(t)
            while st_['nr'] <= t - LA:
                do_rest(st_['nr'])
            nxt = ci + 1
            if nxt < len(chunks) and nxt not in loaded and buffer_free(nxt):
                load_chunk(nxt)
                loaded.add(nxt)
        while st_['nr'] < n_it:
            do_rest(st_['nr'])
        for s in range(nslot):
            self.transpose_tm(self.ao_tm[:, s, :], self.aoT, s, qn)


_CACHE = {}


def _consts(p):
    slopes = 2.0 ** (-8.0 * np.arange(1, NH + 1, dtype=np.float64) / NH)
    k = np.arange(128)[:, None]
    q = np.arange(256)[None, :]
    qp = q + 128 * (q >= 128)
    bt = np.zeros((NH, 128, 5, 256), np.float64)
    for h in range(NH):
        bt[h, :, 0, :] = -slopes[h] * (qp - k)
        for ty in range(1, 5):
            mrel = 1 - ty
            c = -mrel
            kg = 2 * (c // 2) + (p if c % 2 == 0 else 1 - p)
            s_pos = 128 * kg + k
            t_pos = 128 * p + qp
            allowed = (s_pos // 64) <= (t_pos // 64)
            bias = -slopes[h] * np.abs(t_pos - s_pos)
            bt[h, :, ty, :] = np.where(allowed, bias, NEG)
    cc = np.zeros((NH, NM), np.float64)
    return slopes, bt, cc


def _consts_full(p):
    slopes, bt, cc = _consts(p)
    for h in range(NH):
        for m in range(1, NM - 3):
            dd = m if m % 2 == 0 else m + 2 * p
            cc[h, m + 3] = -slopes[h] * 128.0 * dd
    return slopes, bt.astype(np.float32), cc.astype(np.float32)


def _sbias():
    slopes = 2.0 ** (-8.0 * np.arange(1, NH + 1, dtype=np.float64) / NH)
    sb = np.full((NH, 128, 9, NSMP), NEG, np.float64)
    k = np.arange(128)[:, None]
    q = np.arange(NSMP)[None, :]
    for h in range(NH):
        for kb in range(8):
            sb[h, :, kb, :] = -slopes[h] * np.abs(PAST + q - (128 * kb + k))
        kk = np.arange(NSMP)[:, None]
        sb[h, 0:NSMP, 8, :] = -slopes[h] * np.abs(q - kk)
    return sb.astype(np.float32)


def kernel(**inputs):
    x_prompt = np.asarray(inputs['x_prompt'], np.float32)
    B, SEQ, _ = x_prompt.shape
    NT = SEQ // 512
    x_sample = np.asarray(inputs['x_sample'], np.float32)
    key = NT
    if key not in _CACHE:
        _CACHE[key] = Prog(NT).build()
    nc = _CACHE[key]
    ident = np.eye(128, dtype=np.float32)
    tril = np.tril(np.ones((128, 128), np.float32))
    sbias = _sbias().reshape(NH, 128, 9 * NSMP)
    shared = {}
    for nm in ('norm_mix_pre', 'norm_mix_post', 'norm_ffn_pre', 'norm_ffn_post', 'gm_ln_w', 'gm_ln_b',
               'lambda_q1', 'lambda_k1', 'lambda_q2', 'lambda_k2', 'subln_w'):
        shared[nm] = np.ascontiguousarray(np.asarray(inputs[nm], np.float32).reshape(1, -1))
    shared['gm_ws'] = np.ascontiguousarray(np.asarray(inputs['gm_ws'], np.float32)[0])
    shared['gm_bs'] = np.ascontiguousarray(np.asarray(inputs['gm_bs'], np.float32)[0])
    for nm in ('w_in', 'w_branch_attn', 'w_branch_gmlp', 'w_out', 'w_ffn_gate', 'w_ffn_up', 'w_ffn_down'):
        shared[nm] = np.ascontiguousarray(np.asarray(inputs[nm], np.float32)[0])
    shared['c_ident'] = ident
    shared['c_tril'] = tril
    shared['c_sbias'] = sbias
    cache_k = np.asarray(inputs['cache_k'], np.float32)[0]
    cache_v = np.asarray(inputs['cache_v'], np.float32)[0]
    in_maps = []
    for c in range(8):
        b, p = c // 2, c % 2
        xb = x_prompt[b].reshape(SEQ // 128, 128, D)
        own = np.ascontiguousarray(xb[p::2].reshape(NT, TT, D))
        oth = np.ascontiguousarray(xb[(1 - p)::2].reshape(NT, TT, D))
        _, bt, cc = _consts_full(p)
        m = dict(shared)
        m['x_own'] = own
        m['x_oth'] = oth
        m['x_smp'] = np.ascontiguousarray(x_sample[c])
        m['cache_k'] = np.ascontiguousarray(cache_k[c].reshape(PAST, D))
        m['cache_v'] = np.ascontiguousarray(cache_v[c].reshape(PAST, D))
        m['c_btile'] = np.ascontiguousarray(bt.reshape(NH, 128, 5 * 256))
        m['c_cc'] = np.ascontiguousarray(np.broadcast_to(cc.reshape(1, NH * NM), (128, NH * NM)))
        in_maps.append(m)
    res = run_bass_kernel_spmd(nc, in_maps, core_ids=list(range(8)))
    R = res.results
    y_prompt = np.empty((B, SEQ // 128, 128, D), np.float32)
    k_prompt = np.empty((B, SEQ // 128, 128, D), np.float32)
    v_prompt = np.empty((B, SEQ // 128, 128, D), np.float32)
    for c in range(8):
        b, p = c // 2, c % 2
        y_prompt[b, p::2] = R[c]['y_own'].reshape(-1, 128, D)
        k_prompt[b, p::2] = R[c]['k_own'].reshape(-1, 128, D)
        v_prompt[b, p::2] = R[c]['v_own'].reshape(-1, 128, D)
    y_sample = np.stack([R[c]['y_smp'] for c in range(8)])
    k_sample = np.stack([R[c]['k_smp'] for c in range(8)])
    v_sample = np.stack([R[c]['v_smp'] for c in range(8)])
    g_sample = np.stack([R[c]['g_smp'] for c in range(8)])
    return (y_prompt.reshape(B, SEQ, D), y_sample,
            k_prompt.reshape(1, B, SEQ, NH, 2, HD), v_prompt.reshape(1, B, SEQ, NH, VD),
            k_sample.reshape(1, 8, NSMP, NH, 2, HD), v_sample.reshape(1, 8, NSMP, NH, VD),
            g_sample.reshape(1, 8, NSMP, 8, 256))
```

```python
import numpy as np
from contextlib import ExitStack
import concourse.bass as bass
import concourse.mybir as mybir
from concourse.bass_utils import run_bass_kernel_spmd

F32 = mybir.dt.float32
BF16 = mybir.dt.bfloat16
U8 = mybir.dt.uint8
AF = mybir.ActivationFunctionType
ALU = mybir.AluOpType

D = 2048
NH = 8
HD = 128
VD = 256
DFF = 5632
INW = 14336
TT = 256
PAST = 1024
NSMP = 16
CK = 8
NM = 68
NEG = -30000.0
LAMBDA_INIT = 0.8 - 0.6 * 1.0
PAGE = 4096


def _dsz(dt):
    if dt == F32:
        return 4
    if dt == BF16:
        return 2
    if dt == U8:
        return 1
    raise ValueError(str(dt))


class Sync:
    LIMIT = 30000
    NDS = 14

    def __init__(self, nc, es):
        self.nc = nc
        self.es = es
        self.engs = {'pe': nc.tensor, 'act': nc.scalar, 'dve': nc.vector, 'pool': nc.gpsimd, 'sp': nc.sync}
        self.sems = {}
        self.owner = {}
        self.cur = {}
        self.seen = {e: {} for e in self.engs}
        self.W = {}
        self.R = {}
        self.pend = {e: ([], []) for e in self.engs}
        self.nalloc = 0
        for e in ('pe', 'act', 'dve', 'pool'):
            self._epoch(e)
        self.dsem = []
        for k in range(self.NDS):
            key = self._alloc("dq%d" % k, 'dma')
            self.dsem.append([key, 0])
        self.drr = 0
        self.nwait = 0
        self.nops = 0

    def _alloc(self, name, owner):
        h = self.es.enter_context(self.nc.semaphore(name))
        self.sems[name] = h
        self.owner[name] = owner
        self.nalloc += 1
        return name

    def _epoch(self, e):
        key = self._alloc("%s_e%d" % (e, self.nalloc), e)
        self.cur[e] = [key, 0]

    def reg(self, x):
        if isinstance(x, tuple):
            return [x]
        ap = x
        esz = _dsz(ap.dtype)
        a = ap.ap
        pstep = a[0][0]
        off = ap.offset - ap.start_partition() * pstep if pstep else ap.offset
        ext = 1
        for s, c in a[1:]:
            ext += (c - 1) * abs(s)
        lo = off * esz
        hi = (off + ext) * esz
        name = ap.name
        if name.startswith('ps'):
            return [((name, 0), 0, 2048)]
        out = []
        pg = lo // PAGE
        while pg * PAGE < hi:
            l = max(lo, pg * PAGE)
            h = min(hi, (pg + 1) * PAGE)
            out.append(((name, pg), l, h))
            pg += 1
        return out

    def regs(self, xs):
        out = []
        for x in xs:
            if x is None:
                continue
            out.extend(self.reg(x))
        return out

    def _deps(self, eng, r, w):
        evs = {}
        own = self.owner

        def add(k, v):
            if evs.get(k, 0) < v:
                evs[k] = v
        for (sp, lo, hi) in r:
            for e in self.W.get(sp, ()):
                if e[0] < hi and lo < e[1]:
                    add(e[2][0], e[2][1])
            if sp[0].startswith('ps'):
                for e in self.R.get(sp, ()):
                    for k, v in e[2].items():
                        if own[k] != eng:
                            add(k, v)
        for (sp, lo, hi) in w:
            for e in self.W.get(sp, ()):
                if e[0] < hi and lo < e[1]:
                    add(e[2][0], e[2][1])
            for e in self.R.get(sp, ()):
                if e[0] < hi and lo < e[1]:
                    for k, v in e[2].items():
                        add(k, v)
        if eng == 'pe':
            evs = {k: v for k, v in evs.items() if own[k] != 'pe'}
        return evs

    def _wait(self, eng, evs):
        seen = self.seen[eng]
        for k, v in evs.items():
            if seen.get(k, 0) >= v:
                continue
            self.engs[eng].wait_ge(self.sems[k], v)
            seen[k] = v
            self.nwait += 1

    def _commit(self, r, w, ev):
        k, v = ev
        for (sp, lo, hi) in w:
            wl = self.W.get(sp)
            if wl is None:
                wl = self.W[sp] = []
            else:
                wl[:] = [e for e in wl if not (lo <= e[0] and e[1] <= hi)]
            wl.append((lo, hi, ev))
            rl = self.R.get(sp)
            if rl:
                rl[:] = [e for e in rl if not (lo <= e[0] and e[1] <= hi)]
        for (sp, lo, hi) in r:
            rl = self.R.get(sp)
            if rl is None:
                rl = self.R[sp] = []
            for e in rl:
                if e[0] == lo and e[1] == hi:
                    if e[2].get(k, 0) < v:
                        e[2][k] = v
                    break
            else:
                rl.append((lo, hi, {k: v}))

    def op(self, eng, fn, reads=(), writes=(), inc=True):
        r = self.regs(reads)
        w = self.regs(writes)
        self._wait(eng, self._deps(eng, r, w))
        ins = fn()
        self.nops += 1
        pr, pw = self.pend[eng]
        if inc:
            c = self.cur[eng]
            c[1] += 1
            ins.then_inc(self.sems[c[0]], 1)
            ev = (c[0], c[1])
            if pr or pw:
                r = pr + r
                w = pw + w
                self.pend[eng] = ([], [])
            self._commit(r, w, ev)
            if c[1] >= self.LIMIT:
                self._epoch(eng)
        else:
            pr.extend(r)
            pw.extend(w)
        return ins

    def dma(self, out, in_, reads=(), writes=(), q='sp', **kw):
        r = self.regs(reads)
        w = self.regs(writes)
        evs = self._deps(q, r, w)
        slot = self.dsem[self.drr]
        self.drr = (self.drr + 1) % self.NDS
        if slot[1] > 0 and evs.get(slot[0], 0) < slot[1]:
            evs[slot[0]] = slot[1]
        self._wait(q, evs)
        ins = self.engs[q].dma_start(out=out, in_=in_, **kw)
        slot[1] += 16
        ins.then_inc(self.sems[slot[0]], 16)
        self._commit(r, w, (slot[0], slot[1]))
        self.nops += 1
        return ins

    def finish(self, q='sp'):
        evs = {s[0]: s[1] for s in self.dsem if s[1] > 0}
        self._wait(q, evs)


class Prog:
    def __init__(self, NT, with_sample=True):
        self.NT = NT
        self.NKP = NT * 4 * 128
        self.with_sample = with_sample
        self.es = ExitStack()
        self.nc = nc = bass.Bass("TRN2", target_bir_lowering=False)
        self.S = None

    def dram_in(self, name, shape, dt=F32):
        return self.nc.dram_tensor(name, list(shape), dt, kind="ExternalInput").ap()

    def dram_out(self, name, shape, dt=F32):
        return self.nc.dram_tensor(name, list(shape), dt, kind="ExternalOutput").ap()

    def dram_tmp(self, name, shape, dt):
        return self.nc.dram_tensor(name, list(shape), dt, kind="Internal").ap()

    def alloc(self, nbytes, align=64):
        self.top = (self.top + align - 1) // align * align
        o = self.top
        self.top += nbytes
        return o

    def view(self, off, dt, shape):
        n = 1
        for s in shape[1:]:
            n *= s
        v = self.arena[:, off:off + n * _dsz(dt)].bitcast(dt)
        if len(shape) == 3:
            v = v.rearrange("p (a b) -> p a b", a=shape[1])
        elif len(shape) == 4:
            v = v.rearrange("p (a b c) -> p a b c", a=shape[1], b=shape[2])
        return v

    def build(self):
        nc = self.nc
        es = self.es
        NT = self.NT
        NKT = self.NKP + PAST + 128
        self.NKT = NKT
        d = {}
        d['x_own'] = self.dram_in('x_own', [NT, TT, D])
        d['x_oth'] = self.dram_in('x_oth', [NT, TT, D])
        d['x_smp'] = self.dram_in('x_smp', [NSMP, D])
        d['cache_k'] = self.dram_in('cache_k', [PAST, D])
        d['cache_v'] = self.dram_in('cache_v', [PAST, D])
        for nm in ('norm_mix_pre', 'norm_mix_post', 'norm_ffn_pre', 'norm_ffn_post', 'gm_ln_w', 'gm_ln_b'):
            d[nm] = self.dram_in(nm, [1, D])
        for nm in ('lambda_q1', 'lambda_k1', 'lambda_q2', 'lambda_k2'):
            d[nm] = self.dram_in(nm, [1, HD])
        d['subln_w'] = self.dram_in('subln_w', [1, VD])
        d['gm_ws'] = self.dram_in('gm_ws', [8, 128, 128])
        d['gm_bs'] = self.dram_in('gm_bs', [8, 128])
        d['w_in'] = self.dram_in('w_in', [D, INW])
        d['w_branch_attn'] = self.dram_in('w_branch_attn', [D, D])
        d['w_branch_gmlp'] = self.dram_in('w_branch_gmlp', [D, D])
        d['w_out'] = self.dram_in('w_out', [D, D])
        d['w_ffn_gate'] = self.dram_in('w_ffn_gate', [D, DFF])
        d['w_ffn_up'] = self.dram_in('w_ffn_up', [D, DFF])
        d['w_ffn_down'] = self.dram_in('w_ffn_down', [DFF, D])
        d['c_ident'] = self.dram_in('c_ident', [128, 128])
        d['c_tril'] = self.dram_in('c_tril', [128, 128])
        d['c_btile'] = self.dram_in('c_btile', [NH, 128, 5 * 256])
        d['c_cc'] = self.dram_in('c_cc', [128, NH * NM])
        d['c_sbias'] = self.dram_in('c_sbias', [NH, 128, 9 * NSMP])
        d['y_own'] = self.dram_out('y_own', [NT, TT, D])
        d['k_own'] = self.dram_out('k_own', [NT, TT, D])
        d['v_own'] = self.dram_out('v_own', [NT, TT, D])
        d['y_smp'] = self.dram_out('y_smp', [NSMP, D])
        d['k_smp'] = self.dram_out('k_smp', [NSMP, D])
        d['v_smp'] = self.dram_out('v_smp', [NSMP, D])
        d['g_smp'] = self.dram_out('g_smp', [NSMP, D])
        self.blocks = self.block_list()
        NB = len(self.blocks)
        d['wsc'] = self.dram_tmp('wsc', [NB, 128, 16 * 512], BF16)
        d['KT'] = self.dram_tmp('KTs', [2 * NH, 128, NKT], BF16)
        d['V'] = self.dram_tmp('Vs', [NH, NKT, VD], BF16)
        self.d = d
        ARENA = 210944
        self.arena = es.enter_context(nc.sbuf_tensor("arena", [128, ARENA], U8))
        self.banks = [es.enter_context(nc.psum_tensor("ps%d" % i, [128, 512], F32)) for i in range(8)]
        self.S = Sync(nc, es)
        self.top = 0
        self.rrA = 0
        v = self.view
        self.ring = [v(self.alloc(16384), BF16, [128, 16, 512]) for _ in range(3)]
        self.ident = v(self.alloc(256), BF16, [128, 128])
        self.wsT = v(self.alloc(2048), BF16, [128, 8, 128])
        self.bsT = v(self.alloc(32), F32, [128, 8])
        self.gpre = v(self.alloc(64), F32, [128, 16])
        self.gffn = v(self.alloc(64), F32, [128, 16])
        self.cst = v(self.alloc(32), F32, [128, 8])
        self.stat = v(self.alloc(1024), F32, [128, 256])
        self.nstat = 0
        self.sublnrow = v(self.alloc(1024), F32, [128, 256])
        self.cc = v(self.alloc(NH * NM * 4), F32, [128, NH * NM])
        self.rowA = v(self.alloc(8192), F32, [128, D])
        self.rowB = v(self.alloc(8192), F32, [128, D])
        self.xsb = [v(self.alloc(8192), F32, [128, D]) for _ in range(2)]
        self.xs = self.xsb[0]
        self.xnb = [v(self.alloc(4096), BF16, [128, D]) for _ in range(2)]
        self.xn = self.xnb[0]
        self.junk = v(self.alloc(4096), BF16, [128, D])
        self.junk2 = v(self.alloc(1024), BF16, [128, 512])
        pst = self.alloc(8192)
        self.kout = v(pst, F32, [128, 512])
        self.kbf = v(pst + 2048, BF16, [128, 512])
        self.kTst = v(pst + 3072, BF16, [128, 4, 256])
        self.vout = v(pst + 5120, F32, [128, 512])
        self.vbf = v(pst + 7168, BF16, [128, 512])
        oq = self.alloc(8192)
        self.qT = v(oq, BF16, [128, 16, TT])
        self.mergedT = self.qT
        Y = self.alloc(57344)
        self.hT = v(Y, BF16, [128, 16, TT])
        self.gg = v(Y + 8192, F32, [128, 2, D])
        self.fo = self.gg
        self.u = v(Y + 24576, BF16, [128, 2, D])
        self.vn = v(Y + 32768, BF16, [128, 2, D])
        self.saT = v(Y + 40960, BF16, [128, 16, TT])
        self.sbT = v(Y + 49152, BF16, [128, 16, TT])
        self.f1T = v(Y + 24576, BF16, [128, 44, TT])
        self.sg = v(Y + 24576 + 22528, F32, [128, 4, TT])
        a0 = Y
        self.Kc = [v(a0 + i * 8256, BF16, [128, 2, CK * 128]) for i in range(2)]
        self.Vc = [v(a0 + i * 8256 + 4096, BF16, [128, CK, 260]) for i in range(2)]
        a1 = a0 + 2 * 8256
        self.btl = [v(a1 + i * 5120, F32, [128, 5, 256]) for i in range(2)]
        a2 = a1 + 10240
        self.sadd = [v(a2 + i * 2048, F32, [128, 2, 256]) for i in range(2)]
        a3 = a2 + 4096
        self.PT = [v(a3 + i * 1024, BF16, [128, 2, 256]) for i in range(3)]
        a4 = a3 + 3072
        self.of32 = [v(a4 + i * 1024, F32, [128, 256]) for i in range(2)]
        assert a4 + 2048 <= Y + 40960
        self.cst_f = [v(Y + i * 16384, F32, [128, 8, 512]) for i in range(2)]
        self.cst_b = [v(Y + 32768 + i * 8192, BF16, [128, 8, 512]) for i in range(2)]
        self.gmT = v(self.alloc(8192), BF16, [128, 16, TT])
        R8 = self.alloc(16384)
        self.ao_tm = v(R8, BF16, [128, 2, D])
        self.aoT = v(R8 + 8192, BF16, [128, 16, TT])
        self.mo = v(R8, F32, [128, 2, D])
        self.m1 = v(self.alloc(4096), F32, [128, 4, TT])
        self.tmp2 = v(self.alloc(1024), F32, [128, TT])
        assert self.top <= ARENA, self.top

        import os
        self.stop = os.environ.get("PROG_STOP", "")
        self.prologue()
        if self.stop.startswith("pro"):
            self.S.finish()
            es.close()
            return nc
        self.wseq = self.weight_sequence()
        self.wpos = 0
        self.wload = 0
        for _ in range(3):
            self.issue_wload()
        try:
            seq = []
            for i in range(NT):
                seq += [('oth', i), ('own', i)]
            self.phaseA_pre(*seq[0])
            self.phaseA_pe(*seq[0])
            for idx, (kd, i) in enumerate(seq):
                nxt = seq[idx + 1] if idx + 1 < len(seq) else None
                self.tile(kd, i, nxt)
                self.ck(kd)
            if self.with_sample:
                self.tile('smp', 0)
            assert self.wpos == len(self.wseq), (self.wpos, len(self.wseq))
        except StopIteration:
            pass
        self.S.finish()
        es.close()
        return nc

    def ck(self, name):
        if self.stop == name:
            raise StopIteration()

    def block_list(self):
        bl = []
        for cb in range(INW // 512):
            bl.append(('w_in', cb, 0, 16, 'pre'))
        for cb in range(4):
            bl.append(('w_branch_attn', cb, 0, 16, None))
        for cb in range(4):
            bl.append(('w_branch_gmlp', cb, 0, 16, None))
        for cb in range(4):
            bl.append(('w_out', cb, 0, 16, None))
        for cb in range(11):
            bl.append(('w_ffn_gate', cb, 0, 16, 'ffn'))
        for cb in range(11):
            bl.append(('w_ffn_up', cb, 0, 16, 'ffn'))
        for cb in range(4):
            for (k0, ks) in ((0, 16), (16, 16), (32, 12)):
                bl.append(('w_ffn_down', cb, k0, ks, None))
        self.bidx = {(b[0], b[1], b[2]): i for i, b in enumerate(bl)}
        return bl

    def weight_sequence(self):
        seq = []
        bi = self.bidx

        def full():
            s = [bi[('w_in', cb, 0)] for cb in range(28)]
            for cb in range(4):
                s.append(bi[('w_branch_attn', cb, 0)])
                s.append(bi[('w_branch_gmlp', cb, 0)])
            s += [bi[('w_out', cb, 0)] for cb in range(4)]
            for cb in range(11):
                s.append(bi[('w_ffn_gate', cb, 0)])
                s.append(bi[('w_ffn_up', cb, 0)])
            for cb in range(4):
                for k0 in (0, 16, 32):
                    s.append(bi[('w_ffn_down', cb, k0)])
            return s
        for i in range(self.NT):
            seq += [bi[('w_in', cb, 0)] for cb in range(4, 12)]
            seq += full()
        if self.with_sample:
            seq += full()
        return seq

    def issue_wload(self):
        if self.wload >= len(self.wseq):
            return
        b = self.wseq[self.wload]
        buf = self.ring[self.wload % 3]
        kcs = self.blocks[b][3]
        src = self.d['wsc'][b].rearrange("p (k n) -> p k n", k=16)
        self.S.dma(buf[:, 0:kcs, :], src[:, 0:kcs, :], reads=[('wsc', b, b + 1)], writes=[buf[:, 0:kcs, :]])
        self.wload += 1

    def wnext(self, name, cb, k0=0):
        b = self.wseq[self.wpos]
        assert b == self.bidx[(name, cb, k0)], (self.wpos, b, name, cb, k0)
        buf = self.ring[self.wpos % 3]
        self.wpos += 1
        return buf, self.blocks[b][3]

    def wdone(self):
        self.issue_wload()

    def stc(self, n=1):
        c = self.nstat
        self.nstat = (self.nstat + n) % 240
        if self.nstat + 16 > 240:
            self.nstat = 0
        return self.stat[:, c:c + n]

    def bank(self):
        b = self.banks[self.rrA % 4]
        self.rrA += 1
        return b

    def prologue(self):
        S = self.S
        nc = self.nc
        d = self.d
        V, G, A = nc.vector, nc.gpsimd, nc.scalar
        S.op('dve', lambda: V.memset(self.cst[:, 0:1], 1.0), writes=[self.cst[:, 0:1]])
        S.op('dve', lambda: V.memset(self.cst[:, 1:2], -0.5), writes=[self.cst[:, 1:2]])
        S.op('dve', lambda: V.memset(self.cst[:, 3:4], 0.0), writes=[self.cst[:, 3:4]])
        for i in range(2):
            ones = self.Vc[i][:, :, 256:257]
            S.op('dve', lambda ones=ones: V.memset(ones, 1.0), writes=[ones])
        xsv = self.xs[:, 0:128]
        S.dma(xsv, d['c_ident'], writes=[xsv])
        S.op('dve', lambda: V.tensor_scalar(out=self.ident, in0=xsv, scalar1=1.0, scalar2=None, op0=ALU.mult), reads=[xsv], writes=[self.ident])
        S.dma(self.cc, d['c_cc'], writes=[self.cc])
        for (dstv, srcap, nr, coff) in ((self.bsT, d['gm_bs'], 8, 1280),
                                        (self.gpre, d['norm_mix_pre'].rearrange("o (k p) -> (o k) p", p=128), 16, 1408),
                                        (self.gffn, d['norm_ffn_pre'].rearrange("o (k p) -> (o k) p", p=128), 16, 1536)):
            stg = self.xs[0:nr, coff:coff + 128]
            S.dma(stg, srcap, writes=[stg])
            bk = self.bank()
            o_ = bk[:, 0:nr]
            S.op('pe', lambda o_=o_, stg=stg, nr=nr: nc.tensor.transpose(out=o_, in_=stg, identity=xsv[0:nr, 0:nr]),
                 reads=[stg, xsv], writes=[o_])
            S.op('dve', lambda o_=o_, dstv=dstv: V.tensor_scalar(out=dstv, in0=o_, scalar1=1.0, scalar2=None, op0=ALU.mult), reads=[o_], writes=[dstv])
        S.dma(self.sublnrow, d['subln_w'].partition_broadcast(128), writes=[self.sublnrow])
        S.op('dve', lambda: V.tensor_scalar(out=self.sublnrow, in0=self.sublnrow, scalar1=1.0 - LAMBDA_INIT,
                                            scalar2=None, op0=ALU.mult), reads=[self.sublnrow], writes=[self.sublnrow])
        lw = self.xs[:, 512:1024].rearrange("p (a b) -> p a b", a=4)
        for j, nm in enumerate(('lambda_q1', 'lambda_k1', 'lambda_q2', 'lambda_k2')):
            S.dma(lw[:, j, :], d[nm].partition_broadcast(128), writes=[lw[:, j, :]])
        s1 = self.stc()
        s2 = self.stc()
        jk = self.xs[:, 1024:1152]
        S.op('dve', lambda: V.scalar_tensor_tensor(out=jk, in0=lw[:, 0, :], scalar=1.0, in1=lw[:, 1, :], op0=ALU.mult, op1=ALU.mult, accum_out=s1),
             reads=[lw[:, 0, :], lw[:, 1, :]], writes=[jk, s1])
        S.op('dve', lambda: V.scalar_tensor_tensor(out=jk, in0=lw[:, 2, :], scalar=1.0, in1=lw[:, 3, :], op0=ALU.mult, op1=ALU.mult, accum_out=s2),
             reads=[lw[:, 2, :], lw[:, 3, :]], writes=[jk, s2])
        e1 = self.stc()
        e2 = self.stc()
        S.op('act', lambda: A.activation(out=e1, in_=s1, func=AF.Exp), reads=[s1], writes=[e1])
        S.op('act', lambda: A.activation(out=e2, in_=s2, func=AF.Exp), reads=[s2], writes=[e2])
        t = self.stc()
        S.op('dve', lambda: V.tensor_tensor(out=t, in0=e2, in1=e1, op=ALU.subtract), reads=[e1, e2], writes=[t])
        S.op('dve', lambda: V.tensor_scalar(out=self.cst[:, 2:3], in0=t, scalar1=-LAMBDA_INIT, scalar2=None, op0=ALU.add),
             reads=[t], writes=[self.cst[:, 2:3]])
        wsf = self.gg[:, 0, 0:1024].rearrange("p (g s) -> p g s", g=8)
        S.dma(wsf, d['gm_ws'].rearrange("g t s -> t g s"), writes=[wsf])
        trl = self.xs[:, 128:256]
        S.dma(trl, d['c_tril'], writes=[trl])
        wsb = self.gg[:, 1, 0:512].bitcast(BF16).rearrange("p (g s) -> p g s", g=8)
        S.op('dve', lambda: V.tensor_tensor(out=wsb, in0=wsf, in1=trl.unsqueeze(1).to_broadcast([128, 8, 128]), op=ALU.mult),
             reads=[wsf, trl], writes=[wsb])
        for g2 in range(2):
            bk = self.bank()
            bkb = bk[:, :].bitcast(BF16)
            for j in range(4):
                g = g2 * 4 + j
                S.op('pe', lambda g=g, j=j: nc.tensor.transpose(out=bkb[:, j * 128:(j + 1) * 128], in_=wsb[:, g, :], identity=self.ident),
                     reads=[wsb[:, g, :], self.ident], writes=[bkb[:, j * 128:(j + 1) * 128]], inc=(j == 3))
            S.op('dve', lambda g2=g2: V.tensor_scalar(out=self.wsT[:, g2 * 4:(g2 + 1) * 4, :],
                                                    in0=bkb[:, 0:512].rearrange("p (a b) -> p a b", a=4), scalar1=1.0, scalar2=None, op0=ALU.mult),
                 reads=[bkb[:, 0:512]], writes=[self.wsT[:, g2 * 4:(g2 + 1) * 4, :]])
        if self.stop == "pro1":
            return
        engs = ['act', 'dve']
        n = 0
        for b, (nm, cb, k0, kcs, gain) in enumerate(self.blocks):
            Wm = d[nm].rearrange("(k p) n -> p k n", p=128)
            dst = d['wsc'][b].rearrange("p (k n) -> p k n", k=16)
            for h0 in range(0, kcs, 8):
                hs = min(8, kcs - h0)
                sf = self.cst_f[n % 2]
                sb = self.cst_b[n % 2]
                S.dma(sf[:, 0:hs, :], Wm[:, k0 + h0:k0 + h0 + hs, cb * 512:(cb + 1) * 512], writes=[sf[:, 0:hs, :]])
                for kk in range(hs):
                    kc = k0 + h0 + kk
                    if gain == 'pre':
                        sc = self.gpre[:, kc:kc + 1]
                    elif gain == 'ffn':
                        sc = self.gffn[:, kc:kc + 1]
                    else:
                        sc = self.cst[:, 0:1]
                    e = engs[(n * 8 + kk) % 2]
                    o_, i_ = sb[:, kk, :], sf[:, kk, :]
                    if e == 'act':
                        S.op('act', lambda o_=o_, i_=i_, sc=sc: A.activation(out=o_, in_=i_, func=AF.Copy, scale=sc),
                             reads=[i_, sc], writes=[o_])
                    elif e == 'dve':
                        S.op('dve', lambda o_=o_, i_=i_, sc=sc: V.tensor_scalar(out=o_, in0=i_, scalar1=sc, scalar2=None, op0=ALU.mult),
                             reads=[i_, sc], writes=[o_])
                    else:
                        S.op('pool', lambda o_=o_, i_=i_, sc=sc: G.tensor_scalar(out=o_, in0=i_, scalar1=sc, scalar2=None, op0=ALU.mult),
                             reads=[i_, sc], writes=[o_])
                S.dma(dst[:, h0:h0 + hs, :], sb[:, 0:hs, :], reads=[sb[:, 0:hs, :]], writes=[('wsc', b, b + 1)])
                n += 1

    def rstd_from_ss(self, ss, n, eps, rows):
        S = self.S
        V, G = self.nc.vector, self.nc.gpsimd
        vv = self.stc()
        rs = self.stc()
        S.op('dve', lambda: V.tensor_scalar(out=vv[0:rows], in0=ss[0:rows], scalar1=1.0 / n, scalar2=eps, op0=ALU.mult, op1=ALU.add),
             reads=[ss], writes=[vv])
        S.op('pool', lambda: G.tensor_tensor(out=rs[0:rows], in0=vv[0:rows], in1=self.cst[0:rows, 1:2], op=ALU.pow),
             reads=[vv, self.cst[:, 1:2]], writes=[rs])
        return rs

    def transpose_tm(self, src, dstT, st, rows, evac_eng=('dve', 'act')):
        S = self.S
        nc = self.nc
        for cg in range(4):
            bk = self.bank()
            bkb = bk[:, :].bitcast(BF16)
            for j in range(4):
                c = cg * 4 + j
                i_ = src[0:rows, c * 128:(c + 1) * 128]
                o_ = bkb[:, j * 128:j * 128 + rows]
                S.op('pe', lambda i_=i_, o_=o_: nc.tensor.transpose(out=o_, in_=i_, identity=self.ident[0:rows, 0:rows]),
                     reads=[i_, self.ident], writes=[o_], inc=(j == 3))
            srcv = bkb[:, 0:512].rearrange("p (a b) -> p a b", a=4)[:, :, 0:rows]
            dstv = dstT[:, cg * 4:(cg + 1) * 4, st * 128:st * 128 + rows]
            e = evac_eng[cg % len(evac_eng)]
            if e == 'dve':
                S.op('dve', lambda srcv=srcv, dstv=dstv: nc.vector.tensor_scalar(out=dstv, in0=srcv, scalar1=1.0, scalar2=None, op0=ALU.mult), reads=[srcv], writes=[dstv])
            else:
                S.op('act', lambda srcv=srcv, dstv=dstv: nc.scalar.copy(out=dstv, in_=srcv), reads=[srcv], writes=[dstv])

    def norm_to_T(self, src_f32, dstT, st, rows, pe_part=True):
        S = self.S
        nc = self.nc
        ss = self.stc()
        S.op('act', lambda: nc.scalar.activation(out=self.junk[0:rows, :], in_=src_f32[0:rows, :], func=AF.Square, accum_out=ss[0:rows]),
             reads=[src_f32[0:rows, :]], writes=[ss, self.junk[0:rows, :]])
        rs = self.rstd_from_ss(ss, D, 1e-6, rows)
        xn = self.xnb[st]
        S.op('act', lambda: nc.scalar.activation(out=xn[0:rows, :], in_=src_f32[0:rows, :], func=AF.Copy, scale=rs[0:rows]),
             reads=[src_f32[0:rows, :], rs], writes=[xn[0:rows, :]])
        if pe_part:
            self.transpose_tm(xn, dstT, st, rows)

    def a_type(self, blk, kcs, srcT, kc0, T, evac):
        S = self.S
        nc = self.nc
        for oc in range(4):
            bk = self.bank()
            o_ = bk[:, 0:T]
            for kc in range(kcs):
                l_ = blk[:, kc, oc * 128:(oc + 1) * 128]
                r_ = srcT[:, kc0 + kc, 0:T]
                S.op('pe', lambda l_=l_, r_=r_, kc=kc: nc.tensor.matmul(o_, lhsT=l_, rhs=r_, start=(kc == 0), stop=(kc == kcs - 1)),
                     reads=[l_, r_], writes=[o_], inc=(kc == kcs - 1))
            evac(oc, o_)

    def b_type(self, blk, kcs, srcT, kc0, nst, rows, evac, banks=None, first=True, last=True):
        S = self.S
        nc = self.nc
        for st in range(nst):
            bk = banks[st] if banks else self.bank()
            o_ = bk[0:rows, :]
            for kc in range(kcs):
                l_ = srcT[:, kc0 + kc, st * 128:st * 128 + rows]
                r_ = blk[:, kc, :]
                S.op('pe', lambda l_=l_, r_=r_, kc=kc: nc.tensor.matmul(o_, lhsT=l_, rhs=r_, start=(first and kc == 0),
                                                                        stop=(last and kc == kcs - 1)),
                     reads=[l_, r_], writes=[o_], inc=(kc == kcs - 1))
            if last:
                evac(st, o_)

    def tile_io(self, kind, i):
        d = self.d
        if kind == 'smp':
            return [d['x_smp']], NSMP, 1
        return [d['x_' + kind][i, st * 128:(st + 1) * 128, :] for st in range(2)], 128, 2

    def phaseA_pre(self, kind, i):
        xin, rows, nst = self.tile_io(kind, i)
        for st in range(nst):
            xs = self.xsb[st]
            self.S.dma(xs[0:rows, :], xin[st], writes=[xs[0:rows, :]])
            self.norm_to_T(xs, self.hT, st, rows, pe_part=False)

    def phaseA_pe(self, kind, i):
        xin, rows, nst = self.tile_io(kind, i)
        for st in range(nst):
            self.transpose_tm(self.xnb[st], self.hT, st, rows)

    def tile(self, kind, i, nxt=None):
        S = self.S
        nc = self.nc
        d = self.d
        V, G, A, PE = nc.vector, nc.gpsimd, nc.scalar, nc.tensor
        smp = (kind == 'smp')
        nst = 1 if smp else 2
        rows = NSMP if smp else 128
        T = NSMP if smp else TT
        full = kind in ('own', 'smp')
        if smp:
            xin = [d['x_smp']]
            gblk = [self.NKP // 128 + PAST // 128]
        else:
            xin = [d['x_' + kind][i, st * 128:(st + 1) * 128, :] for st in range(2)]
            gblk = [4 * i + 2 * st + (0 if kind == 'own' else 1) for st in range(2)]
        if smp:
            self.ingest_cache()
            self.phaseA_pre(kind, i)
            self.phaseA_pe(kind, i)
        hT = self.hT
        if kind == 'oth' and nxt is not None:
            self.phaseA_pre(*nxt)
        if full:
            for b in range(4):
                blk, kcs = self.wnext('w_in', b)

                def ev_q(oc, ps, b=b):
                    o_ = self.qT[:, 4 * b + oc, 0:T]
                    S.op('act', lambda: A.activation(out=o_, in_=ps, func=AF.Copy, scale=float(HD) ** -0.5), reads=[ps], writes=[o_])
                self.a_type(blk, kcs, hT, 0, T, ev_q)
                self.wdone()
        self.ck('q')
        kdst = d['k_smp'] if smp else (d['k_own'][i] if kind == 'own' else None)
        vdst = d['v_smp'] if smp else (d['v_own'][i] if kind == 'own' else None)
        import os
        if os.environ.get("NO_KVOUT") == "1":
            kdst = vdst = None
        if os.environ.get("NO_KVOUT") == "k":
            vdst = None
        for b in range(4):
            blk, kcs = self.wnext('w_in', 4 + b)

            def ev_k(st, ps, b=b):
                if kdst is not None:
                    S.op('dve', lambda: V.tensor_scalar(out=self.kout[0:rows, :], in0=ps, scalar1=1.0, scalar2=None, op0=ALU.mult), reads=[ps], writes=[self.kout[0:rows, :]])
                    S.dma(kdst[st * 128:st * 128 + rows, b * 512:(b + 1) * 512], self.kout[0:rows, :], reads=[self.kout[0:rows, :]])
                S.op('act', lambda: A.copy(out=self.kbf[0:rows, :], in_=ps), reads=[ps], writes=[self.kbf[0:rows, :]])
                bk = self.bank()
                bkb = bk[:, :].bitcast(BF16)
                for j in range(4):
                    i_ = self.kbf[0:rows, j * 128:(j + 1) * 128]
                    o_ = bkb[:, j * 128:j * 128 + rows]
                    S.op('pe', lambda i_=i_, o_=o_: PE.transpose(out=o_, in_=i_, identity=self.ident[0:rows, 0:rows]),
                         reads=[i_, self.ident], writes=[o_], inc=(j == 3))
                sv = bkb[:, 0:512].rearrange("p (a b) -> p a b", a=4)[:, :, 0:rows]
                dv = self.kTst[:, :, st * 128:st * 128 + rows]
                S.op('dve', lambda: V.tensor_scalar(out=dv, in0=sv, scalar1=1.0, scalar2=None, op0=ALU.mult), reads=[sv], writes=[dv])
                g = gblk[st]
                dst = d['KT'][4 * b:4 * b + 4, :, g * 128:g * 128 + rows].rearrange("m d t -> d m t")
                S.dma(dst, dv, reads=[dv], writes=[('KT', g, g + 1)])
            self.b_type(blk, kcs, hT, 0, nst, rows, ev_k)
            self.wdone()
        for b in range(4):
            blk, kcs = self.wnext('w_in', 8 + b)

            def ev_v(st, ps, b=b):
                if vdst is not None:
                    S.op('dve', lambda: V.tensor_scalar(out=self.vout[0:rows, :], in0=ps, scalar1=1.0, scalar2=None, op0=ALU.mult), reads=[ps], writes=[self.vout[0:rows, :]])
                    S.dma(vdst[st * 128:st * 128 + rows, b * 512:(b + 1) * 512], self.vout[0:rows, :], reads=[self.vout[0:rows, :]])
                S.op('act', lambda: A.copy(out=self.vbf[0:rows, :], in_=ps), reads=[ps], writes=[self.vbf[0:rows, :]])
                g = gblk[st]
                dst = d['V'][2 * b:2 * b + 2, g * 128:g * 128 + rows, :].rearrange("h t e -> t h e")
                S.dma(dst, self.vbf[0:rows, :].rearrange("p (h e) -> p h e", h=2), reads=[self.vbf[0:rows, :]],
                      writes=[('V', g, g + 1)])
            self.b_type(blk, kcs, hT, 0, nst, rows, ev_v)
            self.wdone()
        if not full:
            if nxt is not None:
                self.phaseA_pe(*nxt)
            return
        self.ck('kv')
        for b in range(4):
            blk, kcs = self.wnext('w_in', 12 + b)

            def ev_u(st, ps, b=b):
                o_ = self.u[0:rows, st, b * 512:(b + 1) * 512]
                S.op('act', lambda: A.activation(out=o_, in_=ps, func=AF.Gelu), reads=[ps], writes=[o_])
            self.b_type(blk, kcs, hT, 0, nst, rows, ev_u)
            self.wdone()
        self.ck('u')
        s1 = [self.stc(4) for _ in range(nst)]
        s2 = [self.stc(4) for _ in range(nst)]
        for b in range(4):
            blk, kcs = self.wnext('w_in', 16 + b)

            def ev_g(st, ps, b=b):
                o_ = self.gg[0:rows, st, b * 512:(b + 1) * 512]
                S.op('act', lambda: A.activation(out=o_, in_=ps, func=AF.Gelu, accum_out=s1[st][0:rows, b:b + 1]),
                     reads=[ps], writes=[o_, s1[st][:, b:b + 1]])
                jk = self.junk2[0:rows, :]
                S.op('dve', lambda: V.scalar_tensor_tensor(out=jk, in0=o_, scalar=1.0, in1=o_, op0=ALU.mult, op1=ALU.mult, accum_out=s2[st][0:rows, b:b + 1]),
                     reads=[o_], writes=[s2[st][:, b:b + 1], jk])
            self.b_type(blk, kcs, hT, 0, nst, rows, ev_g)
            self.wdone()
        self.ck('sp')
        for gi_, dstT in ((0, self.saT), (1, self.sbT)):
            for b in range(4):
                blk, kcs = self.wnext('w_in', 20 + 4 * gi_ + b)

                def ev_s(oc, ps, b=b, dstT=dstT):
                    o_ = dstT[:, 4 * b + oc, 0:T]
                    S.op('act', lambda: A.activation(out=o_, in_=ps, func=AF.Sigmoid), reads=[ps], writes=[o_])
                self.a_type(blk, kcs, hT, 0, T, ev_s)
                self.wdone()
        self.ck('g')
        S.dma(self.rowA, d['gm_ln_w'].partition_broadcast(128), writes=[self.rowA])
        S.dma(self.rowB, d['gm_ln_b'].partition_broadcast(128), writes=[self.rowB])
        for st in range(nst):
            t1, t2, mean, msq, ve = self.stc(), self.stc(), self.stc(), self.stc(), self.stc()
            rs = self.stc()
            S.op('dve', lambda: V.reduce_sum(out=t1[0:rows], in_=s1[st][0:rows, :], axis=mybir.AxisListType.X), reads=[s1[st]], writes=[t1])
            S.op('dve', lambda: V.reduce_sum(out=t2[0:rows], in_=s2[st][0:rows, :], axis=mybir.AxisListType.X), reads=[s2[st]], writes=[t2])
            S.op('dve', lambda: V.tensor_scalar(out=mean[0:rows], in0=t1[0:rows], scalar1=1.0 / D, scalar2=None, op0=ALU.mult),
                 reads=[t1], writes=[mean])
            S.op('dve', lambda: V.tensor_tensor(out=msq[0:rows], in0=mean[0:rows], in1=mean[0:rows], op=ALU.mult), reads=[mean], writes=[msq])
            S.op('dve', lambda: V.tensor_scalar(out=t2[0:rows], in0=t2[0:rows], scalar1=1.0 / D, scalar2=1e-6, op0=ALU.mult, op1=ALU.add),
                 reads=[t2], writes=[t2])
            S.op('dve', lambda: V.tensor_tensor(out=ve[0:rows], in0=t2[0:rows], in1=msq[0:rows], op=ALU.subtract), reads=[t2, msq], writes=[ve])
            S.op('pool', lambda: G.tensor_tensor(out=rs[0:rows], in0=ve[0:rows], in1=self.cst[0:rows, 1:2], op=ALU.pow),
                 reads=[ve, self.cst[:, 1:2]], writes=[rs])
            gs = self.gg[0:rows, st, :]
            S.op('dve', lambda: V.tensor_scalar(out=gs, in0=gs, scalar1=mean[0:rows], scalar2=rs[0:rows], op0=ALU.subtract, op1=ALU.mult),
                 reads=[gs, mean, rs], writes=[gs])
            S.op('dve', lambda: V.tensor_tensor(out=gs, in0=gs, in1=self.rowA[0:rows, :], op=ALU.mult), reads=[gs, self.rowA], writes=[gs])
            vs = self.vn[0:rows, st, :]
            if smp:
                S.op('dve', lambda: V.tensor_tensor(out=gs, in0=gs, in1=self.rowB[0:rows, :], op=ALU.add), reads=[gs, self.rowB], writes=[gs])
                S.dma(d['g_smp'], gs, reads=[gs])
                S.op('dve', lambda: V.tensor_scalar(out=vs, in0=gs, scalar1=1.0, scalar2=None, op0=ALU.mult), reads=[gs], writes=[vs])
            else:
                S.op('dve', lambda: V.tensor_tensor(out=vs, in0=gs, in1=self.rowB[0:rows, :], op=ALU.add), reads=[gs, self.rowB], writes=[vs])
            self.ck('ln')
            for g2 in range(4):
                bk = self.bank()
                for j in range(2):
                    gi = g2 * 2 + j
                    o_ = bk[0:rows, j * 256:(j + 1) * 256]
                    l_ = self.wsT[0:rows, gi, 0:rows]
                    r_ = self.vn[0:rows, st, gi * 256:(gi + 1) * 256]
                    S.op('pe', lambda o_=o_, l_=l_, r_=r_: PE.matmul(o_, lhsT=l_, rhs=r_, start=True, stop=True),
                         reads=[l_, r_], writes=[o_], inc=(j == 1))
                for j in range(2):
                    gi = g2 * 2 + j
                    o_ = bk[0:rows, j * 256:(j + 1) * 256]
                    uu = self.u[0:rows, st, gi * 256:(gi + 1) * 256]
                    S.op('dve', lambda o_=o_, uu=uu, gi=gi: V.scalar_tensor_tensor(out=uu, in0=o_, scalar=self.bsT[0:rows, gi:gi + 1], in1=uu,
                                                                                  op0=ALU.add, op1=ALU.mult),
                         reads=[o_, uu, self.bsT], writes=[uu])
            self.transpose_tm(self.u[:, st, :], self.gmT, st, rows)
        self.ck('B')
        self.attention(kind, i, rows, T)
        self.ck('C')
        for b in range(4):
            blk, kcs = self.wnext('w_branch_attn', b)

            def ev_ba(oc, ps, b=b):
                o_ = self.m1[:, oc, 0:T]
                s_ = self.saT[:, 4 * b + oc, 0:T]
                S.op('dve', lambda: V.tensor_tensor(out=o_, in0=ps, in1=s_, op=ALU.mult), reads=[ps, s_], writes=[o_])
            self.a_type(blk, kcs, self.aoT, 0, T, ev_ba)
            self.wdone()
            blk, kcs = self.wnext('w_branch_gmlp', b)

            def ev_bg(oc, ps, b=b):
                t_ = self.tmp2[:, 0:T]
                s_ = self.sbT[:, 4 * b + oc, 0:T]
                S.op('dve', lambda: V.tensor_tensor(out=t_, in0=ps, in1=s_, op=ALU.mult), reads=[ps, s_], writes=[t_])
                o_ = self.mergedT[:, 4 * b + oc, 0:T]
                m_ = self.m1[:, oc, 0:T]
                S.op('dve', lambda: V.tensor_tensor(out=o_, in0=t_, in1=m_, op=ALU.add), reads=[t_, m_], writes=[o_])
            self.a_type(blk, kcs, self.gmT, 0, T, ev_bg)
            self.wdone()
        sq = [self.stc(4) for _ in range(nst)]
        for b in range(4):
            blk, kcs = self.wnext('w_out', b)

            def ev_o(st, ps, b=b):
                jk = self.junk[0:rows, 0:512]
                S.op('act', lambda: A.activation(out=jk, in_=ps, func=AF.Square, accum_out=sq[st][0:rows, b:b + 1]),
                     reads=[ps], writes=[sq[st][:, b:b + 1], jk])
                o_ = self.mo[0:rows, st, b * 512:(b + 1) * 512]
                S.op('dve', lambda: V.tensor_scalar(out=o_, in0=ps, scalar1=1.0, scalar2=None, op0=ALU.mult), reads=[ps], writes=[o_])
            self.b_type(blk, kcs, self.mergedT, 0, nst, rows, ev_o)
            self.wdone()
        S.dma(self.rowA, d['norm_mix_post'].partition_broadcast(128), writes=[self.rowA])
        for st in range(nst):
            tot = self.stc()
            S.op('dve', lambda: V.reduce_sum(out=tot[0:rows], in_=sq[st][0:rows, :], axis=mybir.AxisListType.X), reads=[sq[st]], writes=[tot])
            rs = self.rstd_from_ss(tot, D, 1e-6, rows)
            xs_ = self.xsb[st]
            S.dma(xs_[0:rows, :], xin[st], writes=[xs_[0:rows, :]])
            ms = self.mo[0:rows, st, :]
            S.op('dve', lambda: V.scalar_tensor_tensor(out=ms, in0=ms, scalar=rs[0:rows], in1=self.rowA[0:rows, :], op0=ALU.mult, op1=ALU.mult),
                 reads=[ms, rs, self.rowA], writes=[ms])
            S.op('dve', lambda: V.tensor_tensor(out=ms, in0=ms, in1=xs_[0:rows, :], op=ALU.add), reads=[ms, xs_[0:rows, :]], writes=[ms])
        self.ck('D')
        for st in range(nst):
            self.norm_to_T(self.mo[:, st, :], self.hT, st, rows)
        for j in range(11):
            blk, kcs = self.wnext('w_ffn_gate', j)

            def ev_gate(oc, ps):
                o_ = self.sg[:, oc, 0:T]
                S.op('act', lambda: A.activation(out=o_, in_=ps, func=AF.Silu), reads=[ps], writes=[o_])
            self.a_type(blk, kcs, self.hT, 0, T, ev_gate)
            self.wdone()
            blk, kcs = self.wnext('w_ffn_up', j)

            def ev_up(oc, ps, j=j):
                o_ = self.f1T[:, 4 * j + oc, 0:T]
                s_ = self.sg[:, oc, 0:T]
                S.op('dve', lambda: V.tensor_tensor(out=o_, in0=ps, in1=s_, op=ALU.mult), reads=[ps, s_], writes=[o_])
            self.a_type(blk, kcs, self.hT, 0, T, ev_up)
            self.wdone()
        sq2 = [self.stc(4) for _ in range(nst)]
        if nxt is not None:
            self.phaseA_pre(*nxt)
        for cb in range(4):
            accb = [self.banks[4 + (cb % 2) * 2 + st] for st in range(2)]
            for ki, k0 in enumerate((0, 16, 32)):
                blk, kcs = self.wnext('w_ffn_down', cb, k0)

                def ev_d(st, ps, cb=cb):
                    jk = self.junk[0:rows, 0:512]
                    S.op('act', lambda: A.activation(out=jk, in_=ps, func=AF.Square, accum_out=sq2[st][0:rows, cb:cb + 1]),
                         reads=[ps], writes=[sq2[st][:, cb:cb + 1], jk])
                    o_ = self.fo[0:rows, st, cb * 512:(cb + 1) * 512]
                    S.op('dve', lambda: V.tensor_scalar(out=o_, in0=ps, scalar1=1.0, scalar2=None, op0=ALU.mult), reads=[ps], writes=[o_])
                self.b_type(blk, kcs, self.f1T, k0, nst, rows, ev_d, banks=accb, first=(ki == 0), last=(ki == 2))
                self.wdone()
        if nxt is not None:
            self.phaseA_pe(*nxt)
        S.dma(self.rowB, d['norm_ffn_post'].partition_broadcast(128), writes=[self.rowB])
        ydst = d['y_smp'] if smp else d['y_own'][i]
        for st in range(nst):
            tot = self.stc()
            S.op('dve', lambda: V.reduce_sum(out=tot[0:rows], in_=sq2[st][0:rows, :], axis=mybir.AxisListType.X), reads=[sq2[st]], writes=[tot])
            rs = self.rstd_from_ss(tot, D, 1e-6, rows)
            fs = self.fo[0:rows, st, :]
            S.op('dve', lambda: V.scalar_tensor_tensor(out=fs, in0=fs, scalar=rs[0:rows], in1=self.rowB[0:rows, :], op0=ALU.mult, op1=ALU.mult),
                 reads=[fs, rs, self.rowB], writes=[fs])
            S.op('dve', lambda: V.tensor_tensor(out=fs, in0=fs, in1=self.mo[0:rows, st, :], op=ALU.add),
                 reads=[fs, self.mo[0:rows, st, :]], writes=[fs])
            S.dma(ydst[st * 128:st * 128 + rows, :], fs, reads=[fs])

    def ingest_cache(self):
        S = self.S
        nc = self.nc
        d = self.d
        V, A, PE = nc.vector, nc.scalar, nc.tensor
        kb0 = self.NKP // 128
        for kb in range(PAST // 128):
            g = kb0 + kb
            S.dma(self.xs, d['cache_k'][kb * 128:(kb + 1) * 128, :], writes=[self.xs])
            S.op('act', lambda: A.copy(out=self.xn, in_=self.xs), reads=[self.xs], writes=[self.xn])
            self.transpose_tm(self.xn, self.hT, 0, 128)
            dst = d['KT'][:, :, g * 128:(g + 1) * 128].rearrange("m d t -> d m t")
            sv = self.hT[:, :, 0:128]
            S.dma(dst, sv, reads=[sv], writes=[('KT', g, g + 1)])
            S.dma(self.xs, d['cache_v'][kb * 128:(kb + 1) * 128, :], writes=[self.xs])
            S.op('dve', lambda: V.tensor_scalar(out=self.xn, in0=self.xs, scalar1=1.0, scalar2=None, op0=ALU.mult), reads=[self.xs], writes=[self.xn])
            dstv = d['V'][:, g * 128:(g + 1) * 128, :].rearrange("h t e -> t h e")
            S.dma(dstv, self.xn.rearrange("p (h e) -> p h e", h=NH), reads=[self.xn], writes=[('V', g, g + 1)])

    def attention(self, kind, i, rows, T):
        S = self.S
        nc = self.nc
        d = self.d
        V, G, A, PE = nc.vector, nc.gpsimd, nc.scalar, nc.tensor
        smp = (kind == 'smp')
        if smp:
            kb_list = [(self.NKP // 128 + kb, 128) for kb in range(PAST // 128)] + [(self.NKP // 128 + PAST // 128, NSMP)]
            nslot = 1
        else:
            kb_list = [(kb, 128) for kb in range(4 * i + 4)]
            nslot = 2
        nkb = len(kb_list)
        nq = T
        qn = min(128, nq)
        LA = 2
        acc = [[self.banks[4 + s * 2 + m] for m in range(2)] for s in range(2)]
        for i2 in range(2):
            ones = self.Vc[i2][:, :, 256:257]
            S.op('dve', lambda ones=ones: V.memset(ones, 1.0), writes=[ones])
        chunks = []
        items = []
        for h in range(NH):
            for c0 in range(0, nkb, CK):
                cn = min(CK, nkb - c0)
                ci = len(chunks)
                chunks.append((h, c0, cn))
                for j in range(cn):
                    items.append((h, ci, j, c0 + j))
        btvs = {}

        def load_bias(h):
            bt = self.btl[h % 2]
            if smp:
                btv = bt[:, :, :].rearrange("p a b -> p (a b)")[:, 0:9 * NSMP]
                S.dma(btv, d['c_sbias'][h], writes=[btv])
                btvs[h] = btv.rearrange("p (a b) -> p a b", a=9)
            else:
                S.dma(bt[:, :, :].rearrange("p a b -> p (a b)"), d['c_btile'][h], writes=[bt[:, :, :]])
                btvs[h] = bt

        def load_chunk(ci):
            h, c0, cn = chunks[ci]
            Kc = self.Kc[ci % 2]
            Vc = self.Vc[ci % 2]
            g0 = kb_list[c0][0]
            nkeys = sum(n for _, n in kb_list[c0:c0 + cn])
            ksrc = d['KT'][2 * h:2 * h + 2, :, g0 * 128:g0 * 128 + nkeys].rearrange("m d t -> d m t")
            S.dma(Kc[:, :, 0:nkeys], ksrc, reads=[('KT', g0, g0 + cn)], writes=[Kc[:, :, 0:nkeys]])
            nfull = nkeys // 128
            if nfull:
                vsrc = d['V'][h, g0 * 128:(g0 + nfull) * 128, :].rearrange("(b k) e -> k b e", k=128)
                S.dma(Vc[:, 0:nfull, 0:256], vsrc, reads=[('V', g0, g0 + nfull)], writes=[Vc[:, 0:nfull, 0:256]])
            if nkeys % 128:
                r_ = nkeys % 128
                vsrc = d['V'][h, (g0 + nfull) * 128:(g0 + nfull) * 128 + r_, :]
                S.dma(Vc[0:r_, nfull, 0:256], vsrc, reads=[('V', g0 + nfull, g0 + nfull + 1)], writes=[Vc[0:r_, nfull, 0:256]])

        sbank = {}

        def emit_qk(t):
            h, ci, j, kbi = items[t]
            Kc = self.Kc[ci % 2]
            nk = kb_list[kbi][1]
            bS = self.bank()
            sbank[t] = bS
            for m in range(2):
                o_ = bS[0:nk, m * 256:m * 256 + nq]
                l_ = Kc[:, m, j * 128:j * 128 + nk]
                r_ = self.qT[:, 2 * h + m, 0:nq]
                S.op('pe', lambda o_=o_, l_=l_, r_=r_: PE.matmul(o_, lhsT=l_, rhs=r_, start=True, stop=True),
                     reads=[l_, r_], writes=[o_], inc=(m == 1))

        def emit_rest(t):
            h, ci, j, kbi = items[t]
            Vc = self.Vc[ci % 2]
            nk = kb_list[kbi][1]
            bS = sbank.pop(t)
            sa = self.sadd[t % 2]
            pt = self.PT[t % 3]
            sin = bS[0:nk, :].rearrange("p (m q) -> p m q", m=2)[:, :, 0:nq]
            if smp:
                bsrc = btvs[h][0:nk, kbi, :]
                ccol = self.cst[0:nk, 3:4]
            else:
                mrel = 4 * i - kb_list[kbi][0]
                ty = 0 if mrel >= 1 else 1 - mrel
                bsrc = btvs[h][0:nk, ty, :]
                ccol = self.cc[0:nk, h * NM + mrel + 3:h * NM + mrel + 4]
            so = sa[0:nk, :, 0:nq]
            S.op('dve', lambda: V.tensor_tensor(out=so, in0=sin, in1=bsrc.unsqueeze(1).to_broadcast([nk, 2, nq]), op=ALU.add),
                 reads=[sin, bsrc], writes=[so])
            po = pt[0:nk, :, 0:nq]
            S.op('act', lambda: A.activation(out=po, in_=so, func=AF.Exp, bias=ccol, scale=1.0),
                 reads=[so, ccol], writes=[po])
            for m in range(2):
                for s in range(nslot):
                    o_ = acc[s][m][0:qn, 0:257]
                    l_ = pt[0:nk, m, s * 128:s * 128 + qn]
                    r_ = Vc[0:nk, j, 0:257]
                    S.op('pe', lambda o_=o_, l_=l_, r_=r_: PE.matmul(o_, lhsT=l_, rhs=r_, start=(kbi == 0), stop=(kbi == nkb - 1)),
                         reads=[l_, r_], writes=[o_], inc=(m == 1 and s == nslot - 1))

        def finalize(h):
            for s in range(nslot):
                r1, r2, nr2 = self.stc(), self.stc(), self.stc()
                a0_, a1_ = acc[s][0], acc[s][1]
                c0 = self.m1[0:qn, 2 * s, :]
                c1 = self.m1[0:qn, 2 * s + 1, :]
                S.op('dve', lambda: V.reciprocal(out=r1[0:qn], in_=a0_[0:qn, 256:257]), reads=[a0_[0:qn, 256:257]], writes=[r1])
                S.op('dve', lambda: V.tensor_scalar(out=c0, in0=a0_[0:qn, 0:256], scalar1=1.0, scalar2=None, op0=ALU.mult),
                     reads=[a0_[0:qn, 0:256]], writes=[c0])
                S.op('dve', lambda: V.reciprocal(out=r2[0:qn], in_=a1_[0:qn, 256:257]), reads=[a1_[0:qn, 256:257]], writes=[r2])
                S.op('dve', lambda: V.tensor_scalar(out=c1, in0=a1_[0:qn, 0:256], scalar1=1.0, scalar2=None, op0=ALU.mult),
                     reads=[a1_[0:qn, 0:256]], writes=[c1])
                S.op('dve', lambda: V.tensor_tensor(out=nr2[0:qn], in0=r2[0:qn], in1=self.cst[0:qn, 2:3], op=ALU.mult),
                     reads=[r2, self.cst[:, 2:3]], writes=[nr2])
                o1 = self.of32[0][0:qn, :]
                o2 = self.of32[1][0:qn, :]
                S.op('dve', lambda: V.tensor_scalar(out=o1, in0=c0, scalar1=r1[0:qn], scalar2=None, op0=ALU.mult),
                     reads=[c0, r1], writes=[o1])
                S.op('dve', lambda: V.scalar_tensor_tensor(out=o2, in0=c1, scalar=nr2[0:qn], in1=o1, op0=ALU.mult, op1=ALU.add),
                     reads=[c1, nr2, o1], writes=[o2])
                ss = self.stc()
                S.op('dve', lambda: V.scalar_tensor_tensor(out=o1, in0=o2, scalar=1.0, in1=o2, op0=ALU.mult, op1=ALU.mult, accum_out=ss[0:qn]),
                     reads=[o2], writes=[o1, ss])
                rs = self.rstd_from_ss(ss, VD, 1e-5, qn)
                ao = self.ao_tm[0:qn, s, h * 256:(h + 1) * 256]
                S.op('dve', lambda: V.scalar_tensor_tensor(out=ao, in0=o2, scalar=rs[0:qn], in1=self.sublnrow[0:qn, :], op0=ALU.mult, op1=ALU.mult),
                     reads=[o2, rs, self.sublnrow], writes=[ao])

        load_bias(0)
        load_bias(1)
        n_it = len(items)
        last_item = {}
        for t_, it_ in enumerate(items):
            last_item[it_[1]] = t_
        loaded = set()
        st_ = {'nr': 0}

        def do_rest(tr):
            h, ci, j, kbi = items[tr]
            emit_rest(tr)
            if kbi == nkb - 1:
                finalize(h)
                if h + 2 < NH:
                    load_bias(h + 2)
            st_['nr'] = tr + 1

        def buffer_free(ci):
            return ci < 2 or st_['nr'] > last_item[ci - 2]

        def ensure_loaded(ci):
            if ci in loaded:
                return
            if ci >= 2:
                while st_['nr'] <= last_item[ci - 2]:
                    do_rest(st_['nr'])
            load_chunk(ci)
            loaded.add(ci)

        for t in range(n_it):
            ci = items[t][1]
            ensure_loaded(ci)
            emit_qk(t)
            while st_['nr'] <= t - LA:
                do_rest(st_['nr'])
            nxt = ci + 1
            if nxt < len(chunks) and nxt not in loaded and buffer_free(nxt):
                load_chunk(nxt)
                loaded.add(nxt)
        while st_['nr'] < n_it:
            do_rest(st_['nr'])
        for s in range(nslot):
            self.transpose_tm(self.ao_tm[:, s, :], self.aoT, s, qn)


_CACHE = {}


def _consts(p):
    slopes = 2.0 ** (-8.0 * np.arange(1, NH + 1, dtype=np.float64) / NH)
    k = np.arange(128)[:, None]
    q = np.arange(256)[None, :]
    qp = q + 128 * (q >= 128)
    bt = np.zeros((NH, 128, 5, 256), np.float64)
    for h in range(NH):
        bt[h, :, 0, :] = -slopes[h] * (qp - k)
        for ty in range(1, 5):
            mrel = 1 - ty
            c = -mrel
            kg = 2 * (c // 2) + (p if c % 2 == 0 else 1 - p)
            s_pos = 128 * kg + k
            t_pos = 128 * p + qp
            allowed = (s_pos // 64) <= (t_pos // 64)
            bias = -slopes[h] * np.abs(t_pos - s_pos)
            bt[h, :, ty, :] = np.where(allowed, bias, NEG)
    cc = np.zeros((NH, NM), np.float64)
    return slopes, bt, cc


def _consts_full(p):
    slopes, bt, cc = _consts(p)
    for h in range(NH):
        for m in range(1, NM - 3):
            dd = m if m % 2 == 0 else m + 2 * p
            cc[h, m + 3] = -slopes[h] * 128.0 * dd
    return slopes, bt.astype(np.float32), cc.astype(np.float32)


def _sbias():
    slopes = 2.0 ** (-8.0 * np.arange(1, NH + 1, dtype=np.float64) / NH)
    sb = np.full((NH, 128, 9, NSMP), NEG, np.float64)
    k = np.arange(128)[:, None]
    q = np.arange(NSMP)[None, :]
    for h in range(NH):
        for kb in range(8):
            sb[h, :, kb, :] = -slopes[h] * np.abs(PAST + q - (128 * kb + k))
        kk = np.arange(NSMP)[:, None]
        sb[h, 0:NSMP, 8, :] = -slopes[h] * np.abs(q - kk)
    return sb.astype(np.float32)


def kernel(**inputs):
    x_prompt = np.asarray(inputs['x_prompt'], np.float32)
    B, SEQ, _ = x_prompt.shape
    NT = SEQ // 512
    x_sample = np.asarray(inputs['x_sample'], np.float32)
    key = NT
    if key not in _CACHE:
        _CACHE[key] = Prog(NT).build()
    nc = _CACHE[key]
    ident = np.eye(128, dtype=np.float32)
    tril = np.tril(np.ones((128, 128), np.float32))
    sbias = _sbias().reshape(NH, 128, 9 * NSMP)
    shared = {}
    for nm in ('norm_mix_pre', 'norm_mix_post', 'norm_ffn_pre', 'norm_ffn_post', 'gm_ln_w', 'gm_ln_b',
               'lambda_q1', 'lambda_k1', 'lambda_q2', 'lambda_k2', 'subln_w'):
        shared[nm] = np.ascontiguousarray(np.asarray(inputs[nm], np.float32).reshape(1, -1))
    shared['gm_ws'] = np.ascontiguousarray(np.asarray(inputs['gm_ws'], np.float32)[0])
    shared['gm_bs'] = np.ascontiguousarray(np.asarray(inputs['gm_bs'], np.float32)[0])
    for nm in ('w_in', 'w_branch_attn', 'w_branch_gmlp', 'w_out', 'w_ffn_gate', 'w_ffn_up', 'w_ffn_down'):
        shared[nm] = np.ascontiguousarray(np.asarray(inputs[nm], np.float32)[0])
    shared['c_ident'] = ident
    shared['c_tril'] = tril
    shared['c_sbias'] = sbias
    cache_k = np.asarray(inputs['cache_k'], np.float32)[0]
    cache_v = np.asarray(inputs['cache_v'], np.float32)[0]
    in_maps = []
    for c in range(8):
        b, p = c // 2, c % 2
        xb = x_prompt[b].reshape(SEQ // 128, 128, D)
        own = np.ascontiguousarray(xb[p::2].reshape(NT, TT, D))
        oth = np.ascontiguousarray(xb[(1 - p)::2].reshape(NT, TT, D))
        _, bt, cc = _consts_full(p)
        m = dict(shared)
        m['x_own'] = own
        m['x_oth'] = oth
        m['x_smp'] = np.ascontiguousarray(x_sample[c])
        m['cache_k'] = np.ascontiguousarray(cache_k[c].reshape(PAST, D))
        m['cache_v'] = np.ascontiguousarray(cache_v[c].reshape(PAST, D))
        m['c_btile'] = np.ascontiguousarray(bt.reshape(NH, 128, 5 * 256))
        m['c_cc'] = np.ascontiguousarray(np.broadcast_to(cc.reshape(1, NH * NM), (128, NH * NM)))
        in_maps.append(m)
    res = run_bass_kernel_spmd(nc, in_maps, core_ids=list(range(8)))
    R = res.results
    y_prompt = np.empty((B, SEQ // 128, 128, D), np.float32)
    k_prompt = np.empty((B, SEQ // 128, 128, D), np.float32)
    v_prompt = np.empty((B, SEQ // 128, 128, D), np.float32)
    for c in range(8):
        b, p = c // 2, c % 2
        y_prompt[b, p::2] = R[c]['y_own'].reshape(-1, 128, D)
        k_prompt[b, p::2] = R[c]['k_own'].reshape(-1, 128, D)
        v_prompt[b, p::2] = R[c]['v_own'].reshape(-1, 128, D)
    y_sample = np.stack([R[c]['y_smp'] for c in range(8)])
    k_sample = np.stack([R[c]['k_smp'] for c in range(8)])
    v_sample = np.stack([R[c]['v_smp'] for c in range(8)])
    g_sample = np.stack([R[c]['g_smp'] for c in range(8)])
    return (y_prompt.reshape(B, SEQ, D), y_sample,
            k_prompt.reshape(1, B, SEQ, NH, 2, HD), v_prompt.reshape(1, B, SEQ, NH, VD),
            k_sample.reshape(1, 8, NSMP, NH, 2, HD), v_sample.reshape(1, 8, NSMP, NH, VD),
            g_sample.reshape(1, 8, NSMP, 8, 256))
```

```python
import numpy as np
from contextlib import ExitStack
import concourse.bass as bass
import concourse.mybir as mybir
from concourse.bass_utils import run_bass_kernel_spmd

F32 = mybir.dt.float32
BF16 = mybir.dt.bfloat16
U8 = mybir.dt.uint8
AF = mybir.ActivationFunctionType
ALU = mybir.AluOpType

D = 2048
NH = 8
HD = 128
VD = 256
DFF = 5632
INW = 14336
TT = 256
PAST = 1024
NSMP = 16
CK = 8
NM = 68
NEG = -30000.0
LAMBDA_INIT = 0.8 - 0.6 * 1.0
PAGE = 4096


def _dsz(dt):
    if dt == F32:
        return 4
    if dt == BF16:
        return 2
    if dt == U8:
        return 1
    raise ValueError(str(dt))


class Sync:
    LIMIT = 30000
    NDS = 14

    def __init__(self, nc, es):
        self.nc = nc
        self.es = es
        self.engs = {'pe': nc.tensor, 'act': nc.scalar, 'dve': nc.vector, 'pool': nc.gpsimd, 'sp': nc.sync}
        self.sems = {}
        self.owner = {}
        self.cur = {}
        self.seen = {e: {} for e in self.engs}
        self.W = {}
        self.R = {}
        self.pend = {e: ([], []) for e in self.engs}
        self.nalloc = 0
        for e in ('pe', 'act', 'dve', 'pool'):
            self._epoch(e)
        self.dsem = []
        for k in range(self.NDS):
            key = self._alloc("dq%d" % k, 'dma')
            self.dsem.append([key, 0])
        self.drr = 0
        self.nwait = 0
        self.nops = 0

    def _alloc(self, name, owner):
        h = self.es.enter_context(self.nc.semaphore(name))
        self.sems[name] = h
        self.owner[name] = owner
        self.nalloc += 1
        return name

    def _epoch(self, e):
        key = self._alloc("%s_e%d" % (e, self.nalloc), e)
        self.cur[e] = [key, 0]

    def reg(self, x):
        if isinstance(x, tuple):
            return [x]
        ap = x
        esz = _dsz(ap.dtype)
        a = ap.ap
        pstep = a[0][0]
        off = ap.offset - ap.start_partition() * pstep if pstep else ap.offset
        ext = 1
        for s, c in a[1:]:
            ext += (c - 1) * abs(s)
        lo = off * esz
        hi = (off + ext) * esz
        name = ap.name
        if name.startswith('ps'):
            return [((name, 0), 0, 2048)]
        out = []
        pg = lo // PAGE
        while pg * PAGE < hi:
            l = max(lo, pg * PAGE)
            h = min(hi, (pg + 1) * PAGE)
            out.append(((name, pg), l, h))
            pg += 1
        return out

    def regs(self, xs):
        out = []
        for x in xs:
            if x is None:
                continue
            out.extend(self.reg(x))
        return out

    def _deps(self, eng, r, w):
        evs = {}
        own = self.owner

        def add(k, v):
            if evs.get(k, 0) < v:
                evs[k] = v
        for (sp, lo, hi) in r:
            for e in self.W.get(sp, ()):
                if e[0] < hi and lo < e[1]:
                    add(e[2][0], e[2][1])
            if sp[0].startswith('ps'):
                for e in self.R.get(sp, ()):
                    for k, v in e[2].items():
                        if own[k] != eng:
                            add(k, v)
        for (sp, lo, hi) in w:
            for e in self.W.get(sp, ()):
                if e[0] < hi and lo < e[1]:
                    add(e[2][0], e[2][1])
            for e in self.R.get(sp, ()):
                if e[0] < hi and lo < e[1]:
                    for k, v in e[2].items():
                        add(k, v)
        if eng == 'pe':
            evs = {k: v for k, v in evs.items() if own[k] != 'pe'}
        return evs

    def _wait(self, eng, evs):
        seen = self.seen[eng]
        for k, v in evs.items():
            if seen.get(k, 0) >= v:
                continue
            self.engs[eng].wait_ge(self.sems[k], v)
            seen[k] = v
            self.nwait += 1

    def _commit(self, r, w, ev):
        k, v = ev
        for (sp, lo, hi) in w:
            wl = self.W.get(sp)
            if wl is None:
                wl = self.W[sp] = []
            else:
                wl[:] = [e for e in wl if not (lo <= e[0] and e[1] <= hi)]
            wl.append((lo, hi, ev))
            rl = self.R.get(sp)
            if rl:
                rl[:] = [e for e in rl if not (lo <= e[0] and e[1] <= hi)]
        for (sp, lo, hi) in r:
            rl = self.R.get(sp)
            if rl is None:
                rl = self.R[sp] = []
            for e in rl:
                if e[0] == lo and e[1] == hi:
                    if e[2].get(k, 0) < v:
                        e[2][k] = v
                    break
            else:
                rl.append((lo, hi, {k: v}))

    def op(self, eng, fn, reads=(), writes=(), inc=True):
        r = self.regs(reads)
        w = self.regs(writes)
        self._wait(eng, self._deps(eng, r, w))
        ins = fn()
        self.nops += 1
        pr, pw = self.pend[eng]
        if inc:
            c = self.cur[eng]
            c[1] += 1
            ins.then_inc(self.sems[c[0]], 1)
            ev = (c[0], c[1])
            if pr or pw:
                r = pr + r
                w = pw + w
                self.pend[eng] = ([], [])
            self._commit(r, w, ev)
            if c[1] >= self.LIMIT:
                self._epoch(eng)
        else:
            pr.extend(r)
            pw.extend(w)
        return ins

    def dma(self, out, in_, reads=(), writes=(), q='sp', **kw):
        r = self.regs(reads)
        w = self.regs(writes)
        evs = self._deps(q, r, w)
        slot = self.dsem[self.drr]
        self.drr = (self.drr + 1) % self.NDS
        if slot[1] > 0 and evs.get(slot[0], 0) < slot[1]:
            evs[slot[0]] = slot[1]
        self._wait(q, evs)
        ins = self.engs[q].dma_start(out=out, in_=in_, **kw)
        slot[1] += 16
        ins.then_inc(self.sems[slot[0]], 16)
        self._commit(r, w, (slot[0], slot[1]))
        self.nops += 1
        return ins

    def finish(self, q='sp'):
        evs = {s[0]: s[1] for s in self.dsem if s[1] > 0}
        self._wait(q, evs)


class Prog:
    def __init__(self, NT, with_sample=True):
        self.NT = NT
        self.NKP = NT * 4 * 128
        self.with_sample = with_sample
        self.es = ExitStack()
        self.nc = nc = bass.Bass("TRN2", target_bir_lowering=False)
        self.S = None

    def dram_in(self, name, shape, dt=F32):
        return self.nc.dram_tensor(name, list(shape), dt, kind="ExternalInput").ap()

    def dram_out(self, name, shape, dt=F32):
        return self.nc.dram_tensor(name, list(shape), dt, kind="ExternalOutput").ap()

    def dram_tmp(self, name, shape, dt):
        return self.nc.dram_tensor(name, list(shape), dt, kind="Internal").ap()

    def alloc(self, nbytes, align=64):
        self.top = (self.top + align - 1) // align * align
        o = self.top
        self.top += nbytes
        return o

    def view(self, off, dt, shape):
        n = 1
        for s in shape[1:]:
            n *= s
        v = self.arena[:, off:off + n * _dsz(dt)].bitcast(dt)
        if len(shape) == 3:
            v = v.rearrange("p (a b) -> p a b", a=shape[1])
        elif len(shape) == 4:
            v = v.rearrange("p (a b c) -> p a b c", a=shape[1], b=shape[2])
        return v

    def build(self):
        nc = self.nc
        es = self.es
        NT = self.NT
        NKT = self.NKP + PAST + 128
        self.NKT = NKT
        d = {}
        d['x_own'] = self.dram_in('x_own', [NT, TT, D])
        d['x_oth'] = self.dram_in('x_oth', [NT, TT, D])
        d['x_smp'] = self.dram_in('x_smp', [NSMP, D])
        d['cache_k'] = self.dram_in('cache_k', [PAST, D])
        d['cache_v'] = self.dram_in('cache_v', [PAST, D])
        for nm in ('norm_mix_pre', 'norm_mix_post', 'norm_ffn_pre', 'norm_ffn_post', 'gm_ln_w', 'gm_ln_b'):
            d[nm] = self.dram_in(nm, [1, D])
        for nm in ('lambda_q1', 'lambda_k1', 'lambda_q2', 'lambda_k2'):
            d[nm] = self.dram_in(nm, [1, HD])
        d['subln_w'] = self.dram_in('subln_w', [1, VD])
        d['gm_ws'] = self.dram_in('gm_ws', [8, 128, 128])
        d['gm_bs'] = self.dram_in('gm_bs', [8, 128])
        d['w_in'] = self.dram_in('w_in', [D, INW])
        d['w_branch_attn'] = self.dram_in('w_branch_attn', [D, D])
        d['w_branch_gmlp'] = self.dram_in('w_branch_gmlp', [D, D])
        d['w_out'] = self.dram_in('w_out', [D, D])
        d['w_ffn_gate'] = self.dram_in('w_ffn_gate', [D, DFF])
        d['w_ffn_up'] = self.dram_in('w_ffn_up', [D, DFF])
        d['w_ffn_down'] = self.dram_in('w_ffn_down', [DFF, D])
        d['c_ident'] = self.dram_in('c_ident', [128, 128])
        d['c_tril'] = self.dram_in('c_tril', [128, 128])
        d['c_btile'] = self.dram_in('c_btile', [NH, 128, 5 * 256])
        d['c_cc'] = self.dram_in('c_cc', [128, NH * NM])
        d['c_sbias'] = self.dram_in('c_sbias', [NH, 128, 9 * NSMP])
        d['y_own'] = self.dram_out('y_own', [NT, TT, D])
        d['k_own'] = self.dram_out('k_own', [NT, TT, D])
        d['v_own'] = self.dram_out('v_own', [NT, TT, D])
        d['y_smp'] = self.dram_out('y_smp', [NSMP, D])
        d['k_smp'] = self.dram_out('k_smp', [NSMP, D])
        d['v_smp'] = self.dram_out('v_smp', [NSMP, D])
        d['g_smp'] = self.dram_out('g_smp', [NSMP, D])
        self.blocks = self.block_list()
        NB = len(self.blocks)
        d['wsc'] = self.dram_tmp('wsc', [NB, 128, 16 * 512], BF16)
        d['KT'] = self.dram_tmp('KTs', [2 * NH, 128, NKT], BF16)
        d['V'] = self.dram_tmp('Vs', [NH, NKT, VD], BF16)
        self.d = d
        ARENA = 210944
        self.arena = es.enter_context(nc.sbuf_tensor("arena", [128, ARENA], U8))
        self.banks = [es.enter_context(nc.psum_tensor("ps%d" % i, [128, 512], F32)) for i in range(8)]
        self.S = Sync(nc, es)
        self.top = 0
        self.rrA = 0
        v = self.view
        self.ring = [v(self.alloc(16384), BF16, [128, 16, 512]) for _ in range(3)]
        self.ident = v(self.alloc(256), BF16, [128, 128])
        self.wsT = v(self.alloc(2048), BF16, [128, 8, 128])
        self.bsT = v(self.alloc(32), F32, [128, 8])
        self.gpre = v(self.alloc(64), F32, [128, 16])
        self.gffn = v(self.alloc(64), F32, [128, 16])
        self.cst = v(self.alloc(32), F32, [128, 8])
        self.stat = v(self.alloc(1024), F32, [128, 256])
        self.nstat = 0
        self.sublnrow = v(self.alloc(1024), F32, [128, 256])
        self.cc = v(self.alloc(NH * NM * 4), F32, [128, NH * NM])
        self.rowA = v(self.alloc(8192), F32, [128, D])
        self.rowB = v(self.alloc(8192), F32, [128, D])
        self.xsb = [v(self.alloc(8192), F32, [128, D]) for _ in range(2)]
        self.xs = self.xsb[0]
        self.xnb = [v(self.alloc(4096), BF16, [128, D]) for _ in range(2)]
        self.xn = self.xnb[0]
        self.junk = v(self.alloc(4096), BF16, [128, D])
        self.junk2 = v(self.alloc(1024), BF16, [128, 512])
        pst = self.alloc(8192)
        self.kout = v(pst, F32, [128, 512])
        self.kbf = v(pst + 2048, BF16, [128, 512])
        self.kTst = v(pst + 3072, BF16, [128, 4, 256])
        self.vout = v(pst + 5120, F32, [128, 512])
        self.vbf = v(pst + 7168, BF16, [128, 512])
        oq = self.alloc(8192)
        self.qT = v(oq, BF16, [128, 16, TT])
        self.mergedT = self.qT
        Y = self.alloc(57344)
        self.hT = v(Y, BF16, [128, 16, TT])
        self.gg = v(Y + 8192, F32, [128, 2, D])
        self.fo = self.gg
        self.u = v(Y + 24576, BF16, [128, 2, D])
        self.vn = v(Y + 32768, BF16, [128, 2, D])
        self.saT = v(Y + 40960, BF16, [128, 16, TT])
        self.sbT = v(Y + 49152, BF16, [128, 16, TT])
        self.f1T = v(Y + 24576, BF16, [128, 44, TT])
        self.sg = v(Y + 24576 + 22528, F32, [128, 4, TT])
        a0 = Y
        self.Kc = [v(a0 + i * 8256, BF16, [128, 2, CK * 128]) for i in range(2)]
        self.Vc = [v(a0 + i * 8256 + 4096, BF16, [128, CK, 260]) for i in range(2)]
        a1 = a0 + 2 * 8256
        self.btl = [v(a1 + i * 5120, F32, [128, 5, 256]) for i in range(2)]
        a2 = a1 + 10240
        self.sadd = [v(a2 + i * 2048, F32, [128, 2, 256]) for i in range(2)]
        a3 = a2 + 4096
        self.PT = [v(a3 + i * 1024, BF16, [128, 2, 256]) for i in range(3)]
        a4 = a3 + 3072
        self.of32 = [v(a4 + i * 1024, F32, [128, 256]) for i in range(2)]
        assert a4 + 2048 <= Y + 40960
        self.cst_f = [v(Y + i * 16384, F32, [128, 8, 512]) for i in range(2)]
        self.cst_b = [v(Y + 32768 + i * 8192, BF16, [128, 8, 512]) for i in range(2)]
        ogm = self.alloc(8192)
        self.gmT = v(ogm, BF16, [128, 16, TT])
        R8 = self.alloc(16384)
        assert R8 == ogm + 8192
        self.cst_f.append(v(ogm, F32, [128, 8, 512]))
        self.cst_b.append(v(ogm + 16384, BF16, [128, 8, 512]))
        self.ao_tm = v(R8, BF16, [128, 2, D])
        self.aoT = v(R8 + 8192, BF16, [128, 16, TT])
        self.mo = v(R8, F32, [128, 2, D])
        self.m1 = v(self.alloc(4096), F32, [128, 4, TT])
        self.tmp2 = v(self.alloc(1024), F32, [128, TT])
        assert self.top <= ARENA, self.top

        import os
        self.stop = os.environ.get("PROG_STOP", "")
        self.prologue()
        if self.stop.startswith("pro"):
            self.S.finish()
            es.close()
            return nc
        self.wseq = self.weight_sequence()
        self.wpos = 0
        self.wload = 0
        for _ in range(3):
            self.issue_wload()
        try:
            seq = []
            for i in range(NT):
                seq += [('oth', i), ('own', i)]
            self.phaseA_pre(*seq[0])
            self.phaseA_pe(*seq[0])
            for idx, (kd, i) in enumerate(seq):
                nxt = seq[idx + 1] if idx + 1 < len(seq) else None
                self.tile(kd, i, nxt)
                self.ck(kd)
            if self.with_sample:
                self.tile('smp', 0)
            assert self.wpos == len(self.wseq), (self.wpos, len(self.wseq))
        except StopIteration:
            pass
        self.S.finish()
        es.close()
        return nc

    def ck(self, name):
        if self.stop == name:
            raise StopIteration()

    def block_list(self):
        bl = []
        for cb in range(INW // 512):
            bl.append(('w_in', cb, 0, 16, 'pre'))
        for cb in range(4):
            bl.append(('w_branch_attn', cb, 0, 16, None))
        for cb in range(4):
            bl.append(('w_branch_gmlp', cb, 0, 16, None))
        for cb in range(4):
            bl.append(('w_out', cb, 0, 16, None))
        for cb in range(11):
            bl.append(('w_ffn_gate', cb, 0, 16, 'ffn'))
        for cb in range(11):
            bl.append(('w_ffn_up', cb, 0, 16, 'ffn'))
        for cb in range(4):
            for (k0, ks) in ((0, 16), (16, 16), (32, 12)):
                bl.append(('w_ffn_down', cb, k0, ks, None))
        self.bidx = {(b[0], b[1], b[2]): i for i, b in enumerate(bl)}
        return bl

    def weight_sequence(self):
        seq = []
        bi = self.bidx

        def full():
            s = [bi[('w_in', cb, 0)] for cb in range(28)]
            for cb in range(4):
                s.append(bi[('w_branch_attn', cb, 0)])
                s.append(bi[('w_branch_gmlp', cb, 0)])
            s += [bi[('w_out', cb, 0)] for cb in range(4)]
            for cb in range(11):
                s.append(bi[('w_ffn_gate', cb, 0)])
                s.append(bi[('w_ffn_up', cb, 0)])
            for cb in range(4):
                for k0 in (0, 16, 32):
                    s.append(bi[('w_ffn_down', cb, k0)])
            return s
        for i in range(self.NT):
            seq += [bi[('w_in', cb, 0)] for cb in range(4, 12)]
            seq += full()
        if self.with_sample:
            seq += full()
        return seq

    def issue_wload(self):
        if self.wload >= len(self.wseq):
            return
        b = self.wseq[self.wload]
        buf = self.ring[self.wload % 3]
        kcs = self.blocks[b][3]
        src = self.d['wsc'][b].rearrange("p (k n) -> p k n", k=16)
        self.S.dma(buf[:, 0:kcs, :], src[:, 0:kcs, :], reads=[('wsc', b, b + 1)], writes=[buf[:, 0:kcs, :]])
        self.wload += 1

    def wnext(self, name, cb, k0=0):
        b = self.wseq[self.wpos]
        assert b == self.bidx[(name, cb, k0)], (self.wpos, b, name, cb, k0)
        buf = self.ring[self.wpos % 3]
        self.wpos += 1
        return buf, self.blocks[b][3]

    def wdone(self):
        self.issue_wload()

    def stc(self, n=1):
        c = self.nstat
        self.nstat = (self.nstat + n) % 240
        if self.nstat + 16 > 240:
            self.nstat = 0
        return self.stat[:, c:c + n]

    def bank(self):
        b = self.banks[self.rrA % 4]
        self.rrA += 1
        return b

    def prologue(self):
        S = self.S
        nc = self.nc
        d = self.d
        V, G, A = nc.vector, nc.gpsimd, nc.scalar
        S.op('dve', lambda: V.memset(self.cst[:, 0:1], 1.0), writes=[self.cst[:, 0:1]])
        S.op('dve', lambda: V.memset(self.cst[:, 1:2], -0.5), writes=[self.cst[:, 1:2]])
        S.op('dve', lambda: V.memset(self.cst[:, 3:4], 0.0), writes=[self.cst[:, 3:4]])
        for i in range(2):
            ones = self.Vc[i][:, :, 256:257]
            S.op('dve', lambda ones=ones: V.memset(ones, 1.0), writes=[ones])
        xsv = self.xs[:, 0:128]
        S.dma(xsv, d['c_ident'], writes=[xsv])
        S.op('dve', lambda: V.tensor_scalar(out=self.ident, in0=xsv, scalar1=1.0, scalar2=None, op0=ALU.mult), reads=[xsv], writes=[self.ident])
        S.dma(self.cc, d['c_cc'], writes=[self.cc])
        for (dstv, srcap, nr, coff) in ((self.bsT, d['gm_bs'], 8, 1280),
                                        (self.gpre, d['norm_mix_pre'].rearrange("o (k p) -> (o k) p", p=128), 16, 1408),
                                        (self.gffn, d['norm_ffn_pre'].rearrange("o (k p) -> (o k) p", p=128), 16, 1536)):
            stg = self.xs[0:nr, coff:coff + 128]
            S.dma(stg, srcap, writes=[stg])
            bk = self.bank()
            o_ = bk[:, 0:nr]
            S.op('pe', lambda o_=o_, stg=stg, nr=nr: nc.tensor.transpose(out=o_, in_=stg, identity=xsv[0:nr, 0:nr]),
                 reads=[stg, xsv], writes=[o_])
            S.op('dve', lambda o_=o_, dstv=dstv: V.tensor_scalar(out=dstv, in0=o_, scalar1=1.0, scalar2=None, op0=ALU.mult), reads=[o_], writes=[dstv])
        S.dma(self.sublnrow, d['subln_w'].partition_broadcast(128), writes=[self.sublnrow])
        S.op('dve', lambda: V.tensor_scalar(out=self.sublnrow, in0=self.sublnrow, scalar1=1.0 - LAMBDA_INIT,
                                            scalar2=None, op0=ALU.mult), reads=[self.sublnrow], writes=[self.sublnrow])
        lw = self.xs[:, 512:1024].rearrange("p (a b) -> p a b", a=4)
        for j, nm in enumerate(('lambda_q1', 'lambda_k1', 'lambda_q2', 'lambda_k2')):
            S.dma(lw[:, j, :], d[nm].partition_broadcast(128), writes=[lw[:, j, :]])
        s1 = self.stc()
        s2 = self.stc()
        jk = self.xs[:, 1024:1152]
        S.op('dve', lambda: V.scalar_tensor_tensor(out=jk, in0=lw[:, 0, :], scalar=1.0, in1=lw[:, 1, :], op0=ALU.mult, op1=ALU.mult, accum_out=s1),
             reads=[lw[:, 0, :], lw[:, 1, :]], writes=[jk, s1])
        S.op('dve', lambda: V.scalar_tensor_tensor(out=jk, in0=lw[:, 2, :], scalar=1.0, in1=lw[:, 3, :], op0=ALU.mult, op1=ALU.mult, accum_out=s2),
             reads=[lw[:, 2, :], lw[:, 3, :]], writes=[jk, s2])
        e1 = self.stc()
        e2 = self.stc()
        S.op('act', lambda: A.activation(out=e1, in_=s1, func=AF.Exp), reads=[s1], writes=[e1])
        S.op('act', lambda: A.activation(out=e2, in_=s2, func=AF.Exp), reads=[s2], writes=[e2])
        t = self.stc()
        S.op('dve', lambda: V.tensor_tensor(out=t, in0=e2, in1=e1, op=ALU.subtract), reads=[e1, e2], writes=[t])
        S.op('dve', lambda: V.tensor_scalar(out=self.cst[:, 2:3], in0=t, scalar1=-LAMBDA_INIT, scalar2=None, op0=ALU.add),
             reads=[t], writes=[self.cst[:, 2:3]])
        wsf = self.gg[:, 0, 0:1024].rearrange("p (g s) -> p g s", g=8)
        S.dma(wsf, d['gm_ws'].rearrange("g t s -> t g s"), writes=[wsf])
        trl = self.xs[:, 128:256]
        S.dma(trl, d['c_tril'], writes=[trl])
        wsb = self.gg[:, 1, 0:512].bitcast(BF16).rearrange("p (g s) -> p g s", g=8)
        S.op('dve', lambda: V.tensor_tensor(out=wsb, in0=wsf, in1=trl.unsqueeze(1).to_broadcast([128, 8, 128]), op=ALU.mult),
             reads=[wsf, trl], writes=[wsb])
        for g2 in range(2):
            bk = self.bank()
            bkb = bk[:, :].bitcast(BF16)
            for j in range(4):
                g = g2 * 4 + j
                S.op('pe', lambda g=g, j=j: nc.tensor.transpose(out=bkb[:, j * 128:(j + 1) * 128], in_=wsb[:, g, :], identity=self.ident),
                     reads=[wsb[:, g, :], self.ident], writes=[bkb[:, j * 128:(j + 1) * 128]], inc=(j == 3))
            S.op('dve', lambda g2=g2: V.tensor_scalar(out=self.wsT[:, g2 * 4:(g2 + 1) * 4, :],
                                                    in0=bkb[:, 0:512].rearrange("p (a b) -> p a b", a=4), scalar1=1.0, scalar2=None, op0=ALU.mult),
                 reads=[bkb[:, 0:512]], writes=[self.wsT[:, g2 * 4:(g2 + 1) * 4, :]])
        if self.stop == "pro1":
            return
        engs = ['act', 'dve']
        n = 0
        for b, (nm, cb, k0, kcs, gain) in enumerate(self.blocks):
            Wm = d[nm].rearrange("(k p) n -> p k n", p=128)
            dst = d['wsc'][b].rearrange("p (k n) -> p k n", k=16)
            for h0 in range(0, kcs, 8):
                hs = min(8, kcs - h0)
                sf = self.cst_f[n % 3]
                sb = self.cst_b[n % 3]
                S.dma(sf[:, 0:hs, :], Wm[:, k0 + h0:k0 + h0 + hs, cb * 512:(cb + 1) * 512], writes=[sf[:, 0:hs, :]])
                for kk in range(hs):
                    kc = k0 + h0 + kk
                    if gain == 'pre':
                        sc = self.gpre[:, kc:kc + 1]
                    elif gain == 'ffn':
                        sc = self.gffn[:, kc:kc + 1]
                    else:
                        sc = self.cst[:, 0:1]
                    e = engs[(n * 8 + kk) % 2]
                    o_, i_ = sb[:, kk, :], sf[:, kk, :]
                    if e == 'act':
                        S.op('act', lambda o_=o_, i_=i_, sc=sc: A.activation(out=o_, in_=i_, func=AF.Copy, scale=sc),
                             reads=[i_, sc], writes=[o_])
                    elif e == 'dve':
                        S.op('dve', lambda o_=o_, i_=i_, sc=sc: V.tensor_scalar(out=o_, in0=i_, scalar1=sc, scalar2=None, op0=ALU.mult),
                             reads=[i_, sc], writes=[o_])
                    else:
                        S.op('pool', lambda o_=o_, i_=i_, sc=sc: G.tensor_scalar(out=o_, in0=i_, scalar1=sc, scalar2=None, op0=ALU.mult),
                             reads=[i_, sc], writes=[o_])
                S.dma(dst[:, h0:h0 + hs, :], sb[:, 0:hs, :], reads=[sb[:, 0:hs, :]], writes=[('wsc', b, b + 1)])
                n += 1

    def rstd_from_ss(self, ss, n, eps, rows):
        S = self.S
        V, G = self.nc.vector, self.nc.gpsimd
        vv = self.stc()
        rs = self.stc()
        S.op('dve', lambda: V.tensor_scalar(out=vv[0:rows], in0=ss[0:rows], scalar1=1.0 / n, scalar2=eps, op0=ALU.mult, op1=ALU.add),
             reads=[ss], writes=[vv])
        S.op('pool', lambda: G.tensor_tensor(out=rs[0:rows], in0=vv[0:rows], in1=self.cst[0:rows, 1:2], op=ALU.pow),
             reads=[vv, self.cst[:, 1:2]], writes=[rs])
        return rs

    def transpose_tm(self, src, dstT, st, rows, evac_eng=('dve', 'act')):
        S = self.S
        nc = self.nc
        for cg in range(4):
            bk = self.bank()
            bkb = bk[:, :].bitcast(BF16)
            for j in range(4):
                c = cg * 4 + j
                i_ = src[0:rows, c * 128:(c + 1) * 128]
                o_ = bkb[:, j * 128:j * 128 + rows]
                S.op('pe', lambda i_=i_, o_=o_: nc.tensor.transpose(out=o_, in_=i_, identity=self.ident[0:rows, 0:rows]),
                     reads=[i_, self.ident], writes=[o_], inc=(j == 3))
            srcv = bkb[:, 0:512].rearrange("p (a b) -> p a b", a=4)[:, :, 0:rows]
            dstv = dstT[:, cg * 4:(cg + 1) * 4, st * 128:st * 128 + rows]
            e = evac_eng[cg % len(evac_eng)]
            if e == 'dve':
                S.op('dve', lambda srcv=srcv, dstv=dstv: nc.vector.tensor_scalar(out=dstv, in0=srcv, scalar1=1.0, scalar2=None, op0=ALU.mult), reads=[srcv], writes=[dstv])
            else:
                S.op('act', lambda srcv=srcv, dstv=dstv: nc.scalar.copy(out=dstv, in_=srcv), reads=[srcv], writes=[dstv])

    def norm_to_T(self, src_f32, dstT, st, rows, pe_part=True):
        S = self.S
        nc = self.nc
        ss = self.stc()
        S.op('act', lambda: nc.scalar.activation(out=self.junk[0:rows, :], in_=src_f32[0:rows, :], func=AF.Square, accum_out=ss[0:rows]),
             reads=[src_f32[0:rows, :]], writes=[ss, self.junk[0:rows, :]])
        rs = self.rstd_from_ss(ss, D, 1e-6, rows)
        xn = self.xnb[st]
        S.op('act', lambda: nc.scalar.activation(out=xn[0:rows, :], in_=src_f32[0:rows, :], func=AF.Copy, scale=rs[0:rows]),
             reads=[src_f32[0:rows, :], rs], writes=[xn[0:rows, :]])
        if pe_part:
            self.transpose_tm(xn, dstT, st, rows)

    def a_type(self, blk, kcs, srcT, kc0, T, evac):
        S = self.S
        nc = self.nc
        for oc in range(4):
            bk = self.bank()
            o_ = bk[:, 0:T]
            for kc in range(kcs):
                l_ = blk[:, kc, oc * 128:(oc + 1) * 128]
                r_ = srcT[:, kc0 + kc, 0:T]
                S.op('pe', lambda l_=l_, r_=r_, kc=kc: nc.tensor.matmul(o_, lhsT=l_, rhs=r_, start=(kc == 0), stop=(kc == kcs - 1)),
                     reads=[l_, r_], writes=[o_], inc=(kc == kcs - 1))
            evac(oc, o_)

    def b_type(self, blk, kcs, srcT, kc0, nst, rows, evac, banks=None, first=True, last=True):
        S = self.S
        nc = self.nc
        for st in range(nst):
            bk = banks[st] if banks else self.bank()
            o_ = bk[0:rows, :]
            for kc in range(kcs):
                l_ = srcT[:, kc0 + kc, st * 128:st * 128 + rows]
                r_ = blk[:, kc, :]
                S.op('pe', lambda l_=l_, r_=r_, kc=kc: nc.tensor.matmul(o_, lhsT=l_, rhs=r_, start=(first and kc == 0),
                                                                        stop=(last and kc == kcs - 1)),
                     reads=[l_, r_], writes=[o_], inc=(kc == kcs - 1))
            if last:
                evac(st, o_)

    def tile_io(self, kind, i):
        d = self.d
        if kind == 'smp':
            return [d['x_smp']], NSMP, 1
        return [d['x_' + kind][i, st * 128:(st + 1) * 128, :] for st in range(2)], 128, 2

    def phaseA_pre(self, kind, i):
        xin, rows, nst = self.tile_io(kind, i)
        for st in range(nst):
            xs = self.xsb[st]
            self.S.dma(xs[0:rows, :], xin[st], writes=[xs[0:rows, :]])
            self.norm_to_T(xs, self.hT, st, rows, pe_part=False)

    def phaseA_pe(self, kind, i):
        xin, rows, nst = self.tile_io(kind, i)
        for st in range(nst):
            self.transpose_tm(self.xnb[st], self.hT, st, rows)

    def tile(self, kind, i, nxt=None):
        S = self.S
        nc = self.nc
        d = self.d
        V, G, A, PE = nc.vector, nc.gpsimd, nc.scalar, nc.tensor
        smp = (kind == 'smp')
        nst = 1 if smp else 2
        rows = NSMP if smp else 128
        T = NSMP if smp else TT
        full = kind in ('own', 'smp')
        if smp:
            xin = [d['x_smp']]
            gblk = [self.NKP // 128 + PAST // 128]
        else:
            xin = [d['x_' + kind][i, st * 128:(st + 1) * 128, :] for st in range(2)]
            gblk = [4 * i + 2 * st + (0 if kind == 'own' else 1) for st in range(2)]
        if smp:
            self.ingest_cache()
            self.phaseA_pre(kind, i)
            self.phaseA_pe(kind, i)
        hT = self.hT
        if kind == 'oth' and nxt is not None:
            self.phaseA_pre(*nxt)
        if full:
            for b in range(4):
                blk, kcs = self.wnext('w_in', b)

                def ev_q(oc, ps, b=b):
                    o_ = self.qT[:, 4 * b + oc, 0:T]
                    S.op('act', lambda: A.activation(out=o_, in_=ps, func=AF.Copy, scale=float(HD) ** -0.5), reads=[ps], writes=[o_])
                self.a_type(blk, kcs, hT, 0, T, ev_q)
                self.wdone()
        self.ck('q')
        kdst = d['k_smp'] if smp else (d['k_own'][i] if kind == 'own' else None)
        vdst = d['v_smp'] if smp else (d['v_own'][i] if kind == 'own' else None)
        import os
        if os.environ.get("NO_KVOUT") == "1":
            kdst = vdst = None
        if os.environ.get("NO_KVOUT") == "k":
            vdst = None
        for b in range(4):
            blk, kcs = self.wnext('w_in', 4 + b)

            def ev_k(st, ps, b=b):
                if kdst is not None:
                    S.op('dve', lambda: V.tensor_scalar(out=self.kout[0:rows, :], in0=ps, scalar1=1.0, scalar2=None, op0=ALU.mult), reads=[ps], writes=[self.kout[0:rows, :]])
                    S.dma(kdst[st * 128:st * 128 + rows, b * 512:(b + 1) * 512], self.kout[0:rows, :], reads=[self.kout[0:rows, :]])
                S.op('act', lambda: A.copy(out=self.kbf[0:rows, :], in_=ps), reads=[ps], writes=[self.kbf[0:rows, :]])
                bk = self.bank()
                bkb = bk[:, :].bitcast(BF16)
                for j in range(4):
                    i_ = self.kbf[0:rows, j * 128:(j + 1) * 128]
                    o_ = bkb[:, j * 128:j * 128 + rows]
                    S.op('pe', lambda i_=i_, o_=o_: PE.transpose(out=o_, in_=i_, identity=self.ident[0:rows, 0:rows]),
                         reads=[i_, self.ident], writes=[o_], inc=(j == 3))
                sv = bkb[:, 0:512].rearrange("p (a b) -> p a b", a=4)[:, :, 0:rows]
                dv = self.kTst[:, :, st * 128:st * 128 + rows]
                S.op('dve', lambda: V.tensor_scalar(out=dv, in0=sv, scalar1=1.0, scalar2=None, op0=ALU.mult), reads=[sv], writes=[dv])
                g = gblk[st]
                dst = d['KT'][4 * b:4 * b + 4, :, g * 128:g * 128 + rows].rearrange("m d t -> d m t")
                S.dma(dst, dv, reads=[dv], writes=[('KT', g, g + 1)])
            self.b_type(blk, kcs, hT, 0, nst, rows, ev_k)
            self.wdone()
        for b in range(4):
            blk, kcs = self.wnext('w_in', 8 + b)

            def ev_v(st, ps, b=b):
                if vdst is not None:
                    S.op('dve', lambda: V.tensor_scalar(out=self.vout[0:rows, :], in0=ps, scalar1=1.0, scalar2=None, op0=ALU.mult), reads=[ps], writes=[self.vout[0:rows, :]])
                    S.dma(vdst[st * 128:st * 128 + rows, b * 512:(b + 1) * 512], self.vout[0:rows, :], reads=[self.vout[0:rows, :]])
                S.op('act', lambda: A.copy(out=self.vbf[0:rows, :], in_=ps), reads=[ps], writes=[self.vbf[0:rows, :]])
                g = gblk[st]
                dst = d['V'][2 * b:2 * b + 2, g * 128:g * 128 + rows, :].rearrange("h t e -> t h e")
                S.dma(dst, self.vbf[0:rows, :].rearrange("p (h e) -> p h e", h=2), reads=[self.vbf[0:rows, :]],
                      writes=[('V', g, g + 1)])
            self.b_type(blk, kcs, hT, 0, nst, rows, ev_v)
            self.wdone()
        if not full:
            if nxt is not None:
                self.phaseA_pe(*nxt)
            return
        self.ck('kv')
        for b in range(4):
            blk, kcs = self.wnext('w_in', 12 + b)

            def ev_u(st, ps, b=b):
                o_ = self.u[0:rows, st, b * 512:(b + 1) * 512]
                S.op('act', lambda: A.activation(out=o_, in_=ps, func=AF.Gelu), reads=[ps], writes=[o_])
            self.b_type(blk, kcs, hT, 0, nst, rows, ev_u)
            self.wdone()
        self.ck('u')
        s1 = [self.stc(4) for _ in range(nst)]
        s2 = [self.stc(4) for _ in range(nst)]
        for b in range(4):
            blk, kcs = self.wnext('w_in', 16 + b)

            def ev_g(st, ps, b=b):
                o_ = self.gg[0:rows, st, b * 512:(b + 1) * 512]
                S.op('act', lambda: A.activation(out=o_, in_=ps, func=AF.Gelu, accum_out=s1[st][0:rows, b:b + 1]),
                     reads=[ps], writes=[o_, s1[st][:, b:b + 1]])
                jk = self.junk2[0:rows, :]
                S.op('dve', lambda: V.scalar_tensor_tensor(out=jk, in0=o_, scalar=1.0, in1=o_, op0=ALU.mult, op1=ALU.mult, accum_out=s2[st][0:rows, b:b + 1]),
                     reads=[o_], writes=[s2[st][:, b:b + 1], jk])
            self.b_type(blk, kcs, hT, 0, nst, rows, ev_g)
            self.wdone()
        self.ck('sp')
        for gi_, dstT in ((0, self.saT), (1, self.sbT)):
            for b in range(4):
                blk, kcs = self.wnext('w_in', 20 + 4 * gi_ + b)

                def ev_s(oc, ps, b=b, dstT=dstT):
                    o_ = dstT[:, 4 * b + oc, 0:T]
                    S.op('act', lambda: A.activation(out=o_, in_=ps, func=AF.Sigmoid), reads=[ps], writes=[o_])
                self.a_type(blk, kcs, hT, 0, T, ev_s)
                self.wdone()
        self.ck('g')
        S.dma(self.rowA, d['gm_ln_w'].partition_broadcast(128), writes=[self.rowA])
        S.dma(self.rowB, d['gm_ln_b'].partition_broadcast(128), writes=[self.rowB])
        for st in range(nst):
            t1, t2, mean, msq, ve = self.stc(), self.stc(), self.stc(), self.stc(), self.stc()
            rs = self.stc()
            S.op('dve', lambda: V.reduce_sum(out=t1[0:rows], in_=s1[st][0:rows, :], axis=mybir.AxisListType.X), reads=[s1[st]], writes=[t1])
            S.op('dve', lambda: V.reduce_sum(out=t2[0:rows], in_=s2[st][0:rows, :], axis=mybir.AxisListType.X), reads=[s2[st]], writes=[t2])
            S.op('dve', lambda: V.tensor_scalar(out=mean[0:rows], in0=t1[0:rows], scalar1=1.0 / D, scalar2=None, op0=ALU.mult),
                 reads=[t1], writes=[mean])
            S.op('dve', lambda: V.tensor_tensor(out=msq[0:rows], in0=mean[0:rows], in1=mean[0:rows], op=ALU.mult), reads=[mean], writes=[msq])
            S.op('dve', lambda: V.tensor_scalar(out=t2[0:rows], in0=t2[0:rows], scalar1=1.0 / D, scalar2=1e-6, op0=ALU.mult, op1=ALU.add),
                 reads=[t2], writes=[t2])
            S.op('dve', lambda: V.tensor_tensor(out=ve[0:rows], in0=t2[0:rows], in1=msq[0:rows], op=ALU.subtract), reads=[t2, msq], writes=[ve])
            S.op('pool', lambda: G.tensor_tensor(out=rs[0:rows], in0=ve[0:rows], in1=self.cst[0:rows, 1:2], op=ALU.pow),
                 reads=[ve, self.cst[:, 1:2]], writes=[rs])
            gs = self.gg[0:rows, st, :]
            S.op('dve', lambda: V.tensor_scalar(out=gs, in0=gs, scalar1=mean[0:rows], scalar2=rs[0:rows], op0=ALU.subtract, op1=ALU.mult),
                 reads=[gs, mean, rs], writes=[gs])
            S.op('dve', lambda: V.tensor_tensor(out=gs, in0=gs, in1=self.rowA[0:rows, :], op=ALU.mult), reads=[gs, self.rowA], writes=[gs])
            vs = self.vn[0:rows, st, :]
            if smp:
                S.op('dve', lambda: V.tensor_tensor(out=gs, in0=gs, in1=self.rowB[0:rows, :], op=ALU.add), reads=[gs, self.rowB], writes=[gs])
                S.dma(d['g_smp'], gs, reads=[gs])
                S.op('dve', lambda: V.tensor_scalar(out=vs, in0=gs, scalar1=1.0, scalar2=None, op0=ALU.mult), reads=[gs], writes=[vs])
            else:
                S.op('dve', lambda: V.tensor_tensor(out=vs, in0=gs, in1=self.rowB[0:rows, :], op=ALU.add), reads=[gs, self.rowB], writes=[vs])
            self.ck('ln')
            for g2 in range(4):
                bk = self.bank()
                for j in range(2):
                    gi = g2 * 2 + j
                    o_ = bk[0:rows, j * 256:(j + 1) * 256]
                    l_ = self.wsT[0:rows, gi, 0:rows]
                    r_ = self.vn[0:rows, st, gi * 256:(gi + 1) * 256]
                    S.op('pe', lambda o_=o_, l_=l_, r_=r_: PE.matmul(o_, lhsT=l_, rhs=r_, start=True, stop=True),
                         reads=[l_, r_], writes=[o_], inc=(j == 1))
                for j in range(2):
                    gi = g2 * 2 + j
                    o_ = bk[0:rows, j * 256:(j + 1) * 256]
                    uu = self.u[0:rows, st, gi * 256:(gi + 1) * 256]
                    S.op('dve', lambda o_=o_, uu=uu, gi=gi: V.scalar_tensor_tensor(out=uu, in0=o_, scalar=self.bsT[0:rows, gi:gi + 1], in1=uu,
                                                                                  op0=ALU.add, op1=ALU.mult),
                         reads=[o_, uu, self.bsT], writes=[uu])
            self.transpose_tm(self.u[:, st, :], self.gmT, st, rows)
        self.ck('B')
        self.attention(kind, i, rows, T)
        self.ck('C')
        for b in range(4):
            blk, kcs = self.wnext('w_branch_attn', b)

            def ev_ba(oc, ps, b=b):
                o_ = self.m1[:, oc, 0:T]
                s_ = self.saT[:, 4 * b + oc, 0:T]
                S.op('dve', lambda: V.tensor_tensor(out=o_, in0=ps, in1=s_, op=ALU.mult), reads=[ps, s_], writes=[o_])
            self.a_type(blk, kcs, self.aoT, 0, T, ev_ba)
            self.wdone()
            blk, kcs = self.wnext('w_branch_gmlp', b)

            def ev_bg(oc, ps, b=b):
                t_ = self.tmp2[:, 0:T]
                s_ = self.sbT[:, 4 * b + oc, 0:T]
                S.op('dve', lambda: V.tensor_tensor(out=t_, in0=ps, in1=s_, op=ALU.mult), reads=[ps, s_], writes=[t_])
                o_ = self.mergedT[:, 4 * b + oc, 0:T]
                m_ = self.m1[:, oc, 0:T]
                S.op('dve', lambda: V.tensor_tensor(out=o_, in0=t_, in1=m_, op=ALU.add), reads=[t_, m_], writes=[o_])
            self.a_type(blk, kcs, self.gmT, 0, T, ev_bg)
            self.wdone()
        sq = [self.stc(4) for _ in range(nst)]
        for b in range(4):
            blk, kcs = self.wnext('w_out', b)

            def ev_o(st, ps, b=b):
                jk = self.junk[0:rows, 0:512]
                S.op('act', lambda: A.activation(out=jk, in_=ps, func=AF.Square, accum_out=sq[st][0:rows, b:b + 1]),
                     reads=[ps], writes=[sq[st][:, b:b + 1], jk])
                o_ = self.mo[0:rows, st, b * 512:(b + 1) * 512]
                S.op('dve', lambda: V.tensor_scalar(out=o_, in0=ps, scalar1=1.0, scalar2=None, op0=ALU.mult), reads=[ps], writes=[o_])
            self.b_type(blk, kcs, self.mergedT, 0, nst, rows, ev_o)
            self.wdone()
        S.dma(self.rowA, d['norm_mix_post'].partition_broadcast(128), writes=[self.rowA])
        for st in range(nst):
            tot = self.stc()
            S.op('dve', lambda: V.reduce_sum(out=tot[0:rows], in_=sq[st][0:rows, :], axis=mybir.AxisListType.X), reads=[sq[st]], writes=[tot])
            rs = self.rstd_from_ss(tot, D, 1e-6, rows)
            xs_ = self.xsb[st]
            S.dma(xs_[0:rows, :], xin[st], writes=[xs_[0:rows, :]])
            ms = self.mo[0:rows, st, :]
            S.op('dve', lambda: V.scalar_tensor_tensor(out=ms, in0=ms, scalar=rs[0:rows], in1=self.rowA[0:rows, :], op0=ALU.mult, op1=ALU.mult),
                 reads=[ms, rs, self.rowA], writes=[ms])
            S.op('dve', lambda: V.tensor_tensor(out=ms, in0=ms, in1=xs_[0:rows, :], op=ALU.add), reads=[ms, xs_[0:rows, :]], writes=[ms])
        self.ck('D')
        for st in range(nst):
            self.norm_to_T(self.mo[:, st, :], self.hT, st, rows)
        for j in range(11):
            blk, kcs = self.wnext('w_ffn_gate', j)

            def ev_gate(oc, ps):
                o_ = self.sg[:, oc, 0:T]
                S.op('act', lambda: A.activation(out=o_, in_=ps, func=AF.Silu), reads=[ps], writes=[o_])
            self.a_type(blk, kcs, self.hT, 0, T, ev_gate)
            self.wdone()
            blk, kcs = self.wnext('w_ffn_up', j)

            def ev_up(oc, ps, j=j):
                o_ = self.f1T[:, 4 * j + oc, 0:T]
                s_ = self.sg[:, oc, 0:T]
                S.op('dve', lambda: V.tensor_tensor(out=o_, in0=ps, in1=s_, op=ALU.mult), reads=[ps, s_], writes=[o_])
            self.a_type(blk, kcs, self.hT, 0, T, ev_up)
            self.wdone()
        sq2 = [self.stc(4) for _ in range(nst)]
        if nxt is not None:
            self.phaseA_pre(*nxt)
        for cb in range(4):
            accb = [self.banks[4 + (cb % 2) * 2 + st] for st in range(2)]
            for ki, k0 in enumerate((0, 16, 32)):
                blk, kcs = self.wnext('w_ffn_down', cb, k0)

                def ev_d(st, ps, cb=cb):
                    jk = self.junk[0:rows, 0:512]
                    S.op('act', lambda: A.activation(out=jk, in_=ps, func=AF.Square, accum_out=sq2[st][0:rows, cb:cb + 1]),
                         reads=[ps], writes=[sq2[st][:, cb:cb + 1], jk])
                    o_ = self.fo[0:rows, st, cb * 512:(cb + 1) * 512]
                    S.op('dve', lambda: V.tensor_scalar(out=o_, in0=ps, scalar1=1.0, scalar2=None, op0=ALU.mult), reads=[ps], writes=[o_])
                self.b_type(blk, kcs, self.f1T, k0, nst, rows, ev_d, banks=accb, first=(ki == 0), last=(ki == 2))
                self.wdone()
        if nxt is not None:
            self.phaseA_pe(*nxt)
        S.dma(self.rowB, d['norm_ffn_post'].partition_broadcast(128), writes=[self.rowB])
        ydst = d['y_smp'] if smp else d['y_own'][i]
        for st in range(nst):
            tot = self.stc()
            S.op('dve', lambda: V.reduce_sum(out=tot[0:rows], in_=sq2[st][0:rows, :], axis=mybir.AxisListType.X), reads=[sq2[st]], writes=[tot])
            rs = self.rstd_from_ss(tot, D, 1e-6, rows)
            fs = self.fo[0:rows, st, :]
            S.op('dve', lambda: V.scalar_tensor_tensor(out=fs, in0=fs, scalar=rs[0:rows], in1=self.rowB[0:rows, :], op0=ALU.mult, op1=ALU.mult),
                 reads=[fs, rs, self.rowB], writes=[fs])
            S.op('dve', lambda: V.tensor_tensor(out=fs, in0=fs, in1=self.mo[0:rows, st, :], op=ALU.add),
                 reads=[fs, self.mo[0:rows, st, :]], writes=[fs])
            S.dma(ydst[st * 128:st * 128 + rows, :], fs, reads=[fs])

    def ingest_cache(self):
        S = self.S
        nc = self.nc
        d = self.d
        V, A, PE = nc.vector, nc.scalar, nc.tensor
        kb0 = self.NKP // 128
        for kb in range(PAST // 128):
            g = kb0 + kb
            S.dma(self.xs, d['cache_k'][kb * 128:(kb + 1) * 128, :], writes=[self.xs])
            S.op('act', lambda: A.copy(out=self.xn, in_=self.xs), reads=[self.xs], writes=[self.xn])
            self.transpose_tm(self.xn, self.hT, 0, 128)
            dst = d['KT'][:, :, g * 128:(g + 1) * 128].rearrange("m d t -> d m t")
            sv = self.hT[:, :, 0:128]
            S.dma(dst, sv, reads=[sv], writes=[('KT', g, g + 1)])
            S.dma(self.xs, d['cache_v'][kb * 128:(kb + 1) * 128, :], writes=[self.xs])
            S.op('dve', lambda: V.tensor_scalar(out=self.xn, in0=self.xs, scalar1=1.0, scalar2=None, op0=ALU.mult), reads=[self.xs], writes=[self.xn])
            dstv = d['V'][:, g * 128:(g + 1) * 128, :].rearrange("h t e -> t h e")
            S.dma(dstv, self.xn.rearrange("p (h e) -> p h e", h=NH), reads=[self.xn], writes=[('V', g, g + 1)])

    def attention(self, kind, i, rows, T):
        S = self.S
        nc = self.nc
        d = self.d
        V, G, A, PE = nc.vector, nc.gpsimd, nc.scalar, nc.tensor
        smp = (kind == 'smp')
        if smp:
            kb_list = [(self.NKP // 128 + kb, 128) for kb in range(PAST // 128)] + [(self.NKP // 128 + PAST // 128, NSMP)]
            nslot = 1
        else:
            kb_list = [(kb, 128) for kb in range(4 * i + 4)]
            nslot = 2
        nkb = len(kb_list)
        nq = T
        qn = min(128, nq)
        LA = 2
        acc = [[self.banks[4 + s * 2 + m] for m in range(2)] for s in range(2)]
        for i2 in range(2):
            ones = self.Vc[i2][:, :, 256:257]
            S.op('dve', lambda ones=ones: V.memset(ones, 1.0), writes=[ones])
        chunks = []
        items = []
        for h in range(NH):
            for c0 in range(0, nkb, CK):
                cn = min(CK, nkb - c0)
                ci = len(chunks)
                chunks.append((h, c0, cn))
                for j in range(cn):
                    items.append((h, ci, j, c0 + j))
        btvs = {}

        def load_bias(h):
            bt = self.btl[h % 2]
            if smp:
                btv = bt[:, :, :].rearrange("p a b -> p (a b)")[:, 0:9 * NSMP]
                S.dma(btv, d['c_sbias'][h], writes=[btv])
                btvs[h] = btv.rearrange("p (a b) -> p a b", a=9)
            else:
                S.dma(bt[:, :, :].rearrange("p a b -> p (a b)"), d['c_btile'][h], writes=[bt[:, :, :]])
                btvs[h] = bt

        def load_chunk(ci):
            h, c0, cn = chunks[ci]
            Kc = self.Kc[ci % 2]
            Vc = self.Vc[ci % 2]
            g0 = kb_list[c0][0]
            nkeys = sum(n for _, n in kb_list[c0:c0 + cn])
            ksrc = d['KT'][2 * h:2 * h + 2, :, g0 * 128:g0 * 128 + nkeys].rearrange("m d t -> d m t")
            S.dma(Kc[:, :, 0:nkeys], ksrc, reads=[('KT', g0, g0 + cn)], writes=[Kc[:, :, 0:nkeys]])
            nfull = nkeys // 128
            if nfull:
                vsrc = d['V'][h, g0 * 128:(g0 + nfull) * 128, :].rearrange("(b k) e -> k b e", k=128)
                S.dma(Vc[:, 0:nfull, 0:256], vsrc, reads=[('V', g0, g0 + nfull)], writes=[Vc[:, 0:nfull, 0:256]])
            if nkeys % 128:
                r_ = nkeys % 128
                vsrc = d['V'][h, (g0 + nfull) * 128:(g0 + nfull) * 128 + r_, :]
                S.dma(Vc[0:r_, nfull, 0:256], vsrc, reads=[('V', g0 + nfull, g0 + nfull + 1)], writes=[Vc[0:r_, nfull, 0:256]])

        sbank = {}

        def emit_qk(t):
            h, ci, j, kbi = items[t]
            Kc = self.Kc[ci % 2]
            nk = kb_list[kbi][1]
            bS = self.bank()
            sbank[t] = bS
            for m in range(2):
                o_ = bS[0:nk, m * 256:m * 256 + nq]
                l_ = Kc[:, m, j * 128:j * 128 + nk]
                r_ = self.qT[:, 2 * h + m, 0:nq]
                S.op('pe', lambda o_=o_, l_=l_, r_=r_: PE.matmul(o_, lhsT=l_, rhs=r_, start=True, stop=True),
                     reads=[l_, r_], writes=[o_], inc=(m == 1))

        def emit_rest(t):
            h, ci, j, kbi = items[t]
            Vc = self.Vc[ci % 2]
            nk = kb_list[kbi][1]
            bS = sbank.pop(t)
            sa = self.sadd[t % 2]
            pt = self.PT[t % 3]
            sin = bS[0:nk, :].rearrange("p (m q) -> p m q", m=2)[:, :, 0:nq]
            if smp:
                bsrc = btvs[h][0:nk, kbi, :]
                ccol = self.cst[0:nk, 3:4]
            else:
                mrel = 4 * i - kb_list[kbi][0]
                ty = 0 if mrel >= 1 else 1 - mrel
                bsrc = btvs[h][0:nk, ty, :]
                ccol = self.cc[0:nk, h * NM + mrel + 3:h * NM + mrel + 4]
            so = sa[0:nk, :, 0:nq]
            S.op('dve', lambda: V.tensor_tensor(out=so, in0=sin, in1=bsrc.unsqueeze(1).to_broadcast([nk, 2, nq]), op=ALU.add),
                 reads=[sin, bsrc], writes=[so])
            po = pt[0:nk, :, 0:nq]
            S.op('act', lambda: A.activation(out=po, in_=so, func=AF.Exp, bias=ccol, scale=1.0),
                 reads=[so, ccol], writes=[po])
            for m in range(2):
                for s in range(nslot):
                    o_ = acc[s][m][0:qn, 0:257]
                    l_ = pt[0:nk, m, s * 128:s * 128 + qn]
                    r_ = Vc[0:nk, j, 0:257]
                    S.op('pe', lambda o_=o_, l_=l_, r_=r_: PE.matmul(o_, lhsT=l_, rhs=r_, start=(kbi == 0), stop=(kbi == nkb - 1)),
                         reads=[l_, r_], writes=[o_], inc=(m == 1 and s == nslot - 1))

        def finalize(h):
            for s in range(nslot):
                r1, r2, nr2 = self.stc(), self.stc(), self.stc()
                a0_, a1_ = acc[s][0], acc[s][1]
                c0 = self.m1[0:qn, 2 * s, :]
                c1 = self.m1[0:qn, 2 * s + 1, :]
                S.op('dve', lambda: V.reciprocal(out=r1[0:qn], in_=a0_[0:qn, 256:257]), reads=[a0_[0:qn, 256:257]], writes=[r1])
                S.op('dve', lambda: V.tensor_scalar(out=c0, in0=a0_[0:qn, 0:256], scalar1=1.0, scalar2=None, op0=ALU.mult),
                     reads=[a0_[0:qn, 0:256]], writes=[c0])
                S.op('dve', lambda: V.reciprocal(out=r2[0:qn], in_=a1_[0:qn, 256:257]), reads=[a1_[0:qn, 256:257]], writes=[r2])
                S.op('dve', lambda: V.tensor_scalar(out=c1, in0=a1_[0:qn, 0:256], scalar1=1.0, scalar2=None, op0=ALU.mult),
                     reads=[a1_[0:qn, 0:256]], writes=[c1])
                S.op('dve', lambda: V.tensor_tensor(out=nr2[0:qn], in0=r2[0:qn], in1=self.cst[0:qn, 2:3], op=ALU.mult),
                     reads=[r2, self.cst[:, 2:3]], writes=[nr2])
                o1 = self.of32[0][0:qn, :]
                o2 = self.of32[1][0:qn, :]
                S.op('dve', lambda: V.tensor_scalar(out=o1, in0=c0, scalar1=r1[0:qn], scalar2=None, op0=ALU.mult),
                     reads=[c0, r1], writes=[o1])
                S.op('dve', lambda: V.scalar_tensor_tensor(out=o2, in0=c1, scalar=nr2[0:qn], in1=o1, op0=ALU.mult, op1=ALU.add),
                     reads=[c1, nr2, o1], writes=[o2])
                ss = self.stc()
                S.op('dve', lambda: V.scalar_tensor_tensor(out=o1, in0=o2, scalar=1.0, in1=o2, op0=ALU.mult, op1=ALU.mult, accum_out=ss[0:qn]),
                     reads=[o2], writes=[o1, ss])
                rs = self.rstd_from_ss(ss, VD, 1e-5, qn)
                ao = self.ao_tm[0:qn, s, h * 256:(h + 1) * 256]
                S.op('dve', lambda: V.scalar_tensor_tensor(out=ao, in0=o2, scalar=rs[0:qn], in1=self.sublnrow[0:qn, :], op0=ALU.mult, op1=ALU.mult),
                     reads=[o2, rs, self.sublnrow], writes=[ao])

        load_bias(0)
        load_bias(1)
        n_it = len(items)
        last_item = {}
        for t_, it_ in enumerate(items):
            last_item[it_[1]] = t_
        loaded = set()
        st_ = {'nr': 0}

        def do_rest(tr):
            h, ci, j, kbi = items[tr]
            emit_rest(tr)
            if kbi == nkb - 1:
                finalize(h)
                if h + 2 < NH:
                    load_bias(h + 2)
            st_['nr'] = tr + 1

        def buffer_free(ci):
            return ci < 2 or st_['nr'] > last_item[ci - 2]

        def ensure_loaded(ci):
            if ci in loaded:
                return
            if ci >= 2:
                while st_['nr'] <= last_item[ci - 2]:
                    do_rest(st_['nr'])
            load_chunk(ci)
            loaded.add(ci)

        for t in range(n_it):
            ci = items[t][1]
            ensure_loaded(ci)
            emit_qk(t)
            while st_['nr'] <= t - LA:
                do_rest(st_['nr'])
            nxt = ci + 1
            if nxt < len(chunks) and nxt not in loaded and buffer_free(nxt):
                load_chunk(nxt)
                loaded.add(nxt)
        while st_['nr'] < n_it:
            do_rest(st_['nr'])
        for s in range(nslot):
            self.transpose_tm(self.ao_tm[:, s, :], self.aoT, s, qn)


_CACHE = {}


def _consts(p):
    slopes = 2.0 ** (-8.0 * np.arange(1, NH + 1, dtype=np.float64) / NH)
    k = np.arange(128)[:, None]
    q = np.arange(256)[None, :]
    qp = q + 128 * (q >= 128)
    bt = np.zeros((NH, 128, 5, 256), np.float64)
    for h in range(NH):
        bt[h, :, 0, :] = -slopes[h] * (qp - k)
        for ty in range(1, 5):
            mrel = 1 - ty
            c = -mrel
            kg = 2 * (c // 2) + (p if c % 2 == 0 else 1 - p)
            s_pos = 128 * kg + k
            t_pos = 128 * p + qp
            allowed = (s_pos // 64) <= (t_pos // 64)
            bias = -slopes[h] * np.abs(t_pos - s_pos)
            bt[h, :, ty, :] = np.where(allowed, bias, NEG)
    cc = np.zeros((NH, NM), np.float64)
    return slopes, bt, cc


def _consts_full(p):
    slopes, bt, cc = _consts(p)
    for h in range(NH):
        for m in range(1, NM - 3):
            dd = m if m % 2 == 0 else m + 2 * p
            cc[h, m + 3] = -slopes[h] * 128.0 * dd
    return slopes, bt.astype(np.float32), cc.astype(np.float32)


def _sbias():
    slopes = 2.0 ** (-8.0 * np.arange(1, NH + 1, dtype=np.float64) / NH)
    sb = np.full((NH, 128, 9, NSMP), NEG, np.float64)
    k = np.arange(128)[:, None]
    q = np.arange(NSMP)[None, :]
    for h in range(NH):
        for kb in range(8):
            sb[h, :, kb, :] = -slopes[h] * np.abs(PAST + q - (128 * kb + k))
        kk = np.arange(NSMP)[:, None]
        sb[h, 0:NSMP, 8, :] = -slopes[h] * np.abs(q - kk)
    return sb.astype(np.float32)


def kernel(**inputs):
    x_prompt = np.asarray(inputs['x_prompt'], np.float32)
    B, SEQ, _ = x_prompt.shape
    NT = SEQ // 512
    x_sample = np.asarray(inputs['x_sample'], np.float32)
    key = NT
    if key not in _CACHE:
        _CACHE[key] = Prog(NT).build()
    nc = _CACHE[key]
    ident = np.eye(128, dtype=np.float32)
    tril = np.tril(np.ones((128, 128), np.float32))
    sbias = _sbias().reshape(NH, 128, 9 * NSMP)
    shared = {}
    for nm in ('norm_mix_pre', 'norm_mix_post', 'norm_ffn_pre', 'norm_ffn_post', 'gm_ln_w', 'gm_ln_b',
               'lambda_q1', 'lambda_k1', 'lambda_q2', 'lambda_k2', 'subln_w'):
        shared[nm] = np.ascontiguousarray(np.asarray(inputs[nm], np.float32).reshape(1, -1))
    shared['gm_ws'] = np.ascontiguousarray(np.asarray(inputs['gm_ws'], np.float32)[0])
    shared['gm_bs'] = np.ascontiguousarray(np.asarray(inputs['gm_bs'], np.float32)[0])
    for nm in ('w_in', 'w_branch_attn', 'w_branch_gmlp', 'w_out', 'w_ffn_gate', 'w_ffn_up', 'w_ffn_down'):
        shared[nm] = np.ascontiguousarray(np.asarray(inputs[nm], np.float32)[0])
    shared['c_ident'] = ident
    shared['c_tril'] = tril
    shared['c_sbias'] = sbias
    cache_k = np.asarray(inputs['cache_k'], np.float32)[0]
    cache_v = np.asarray(inputs['cache_v'], np.float32)[0]
    in_maps = []
    for c in range(8):
        b, p = c // 2, c % 2
        xb = x_prompt[b].reshape(SEQ // 128, 128, D)
        own = np.ascontiguousarray(xb[p::2].reshape(NT, TT, D))
        oth = np.ascontiguousarray(xb[(1 - p)::2].reshape(NT, TT, D))
        _, bt, cc = _consts_full(p)
        m = dict(shared)
        m['x_own'] = own
        m['x_oth'] = oth
        m['x_smp'] = np.ascontiguousarray(x_sample[c])
        m['cache_k'] = np.ascontiguousarray(cache_k[c].reshape(PAST, D))
        m['cache_v'] = np.ascontiguousarray(cache_v[c].reshape(PAST, D))
        m['c_btile'] = np.ascontiguousarray(bt.reshape(NH, 128, 5 * 256))
        m['c_cc'] = np.ascontiguousarray(np.broadcast_to(cc.reshape(1, NH * NM), (128, NH * NM)))
        in_maps.append(m)
    res = run_bass_kernel_spmd(nc, in_maps, core_ids=list(range(8)))
    R = res.results
    y_prompt = np.empty((B, SEQ // 128, 128, D), np.float32)
    k_prompt = np.empty((B, SEQ // 128, 128, D), np.float32)
    v_prompt = np.empty((B, SEQ // 128, 128, D), np.float32)
    for c in range(8):
        b, p = c // 2, c % 2
        y_prompt[b, p::2] = R[c]['y_own'].reshape(-1, 128, D)
        k_prompt[b, p::2] = R[c]['k_own'].reshape(-1, 128, D)
        v_prompt[b, p::2] = R[c]['v_own'].reshape(-1, 128, D)
    y_sample = np.stack([R[c]['y_smp'] for c in range(8)])
    k_sample = np.stack([R[c]['k_smp'] for c in range(8)])
    v_sample = np.stack([R[c]['v_smp'] for c in range(8)])
    g_sample = np.stack([R[c]['g_smp'] for c in range(8)])
    return (y_prompt.reshape(B, SEQ, D), y_sample,
            k_prompt.reshape(1, B, SEQ, NH, 2, HD), v_prompt.reshape(1, B, SEQ, NH, VD),
            k_sample.reshape(1, 8, NSMP, NH, 2, HD), v_sample.reshape(1, 8, NSMP, NH, VD),
            g_sample.reshape(1, 8, NSMP, 8, 256))
```
